# Optimizing a Trainium2 kernel written in Bass

```python
import math
import jax, jax.numpy as jnp
from jax import lax
import numpy as np

D_MODEL = 1024
BATCH = 8
SEQ = 2048
DEPTH = 1
DEC_BATCH = 128
DEC_SEQ = 8
PAST_LEN = 16384
PAGE_SIZE = 128

N_META = 16
D_SSM = D_MODEL
SSM_GROUP_CH = 16
SSM_GROUPS = D_SSM // SSM_GROUP_CH
SSM_STATE = 64
D_POOL = D_MODEL
POOL_WINDOWS = (2, 4, 8, 16)
POOL_GROUPS = len(POOL_WINDOWS)
POOL_GROUP_CH = D_POOL // POOL_GROUPS
POOL_BUF = max(POOL_WINDOWS) - 1
D_IN = 2 * D_SSM + 2 * D_POOL + 2 * D_MODEL
EPS = 1e-6
DT_MIN = 1e-3
DT_MAX = 1e-1

kernel_name = "gated_s5_pool_hybrid_step"


def rmsnorm(x, g):
    xf = x.astype(jnp.float32)
    y = xf * lax.rsqrt(jnp.mean(xf * xf, axis=-1, keepdims=True) + EPS)
    return (y * g.astype(jnp.float32)).astype(x.dtype)


def _complex_scan_combine(e1, e2):
    a1r, a1i, b1r, b1i = e1
    a2r, a2i, b2r, b2i = e2
    ar = a2r * a1r - a2i * a1i
    ai = a2r * a1i + a2i * a1r
    br = a2r * b1r - a2i * b1i + b2r
    bi = a2r * b1i + a2i * b1r + b2i
    return (ar, ai, br, bi)


def ssm_branch(u, s0_re, s0_im, a_re, a_im, log_dt, b_re, b_im, c_re, c_im, d, w_glu, b_glu):
    bsz, seq_len, _ = u.shape
    f32 = jnp.float32
    uf = u.astype(f32).reshape(bsz, seq_len, SSM_GROUPS, SSM_GROUP_CH)
    dt = jnp.exp(log_dt.astype(f32))[:, None]
    ar = a_re.astype(f32)
    ai = a_im.astype(f32)
    mag = jnp.exp(dt * ar)
    ang = dt * ai
    abar_re = mag * jnp.cos(ang)
    abar_im = mag * jnp.sin(ang)
    den = ar * ar + ai * ai
    nr = abar_re - 1.0
    ni = abar_im
    q_re = (nr * ar + ni * ai) / den
    q_im = (ni * ar - nr * ai) / den
    br = b_re.astype(f32)
    bi = b_im.astype(f32)
    bbar_re = q_re[..., None] * br - q_im[..., None] * bi
    bbar_im = q_re[..., None] * bi + q_im[..., None] * br
    bu_re = jnp.einsum('blgc,gpc->lbgp', uf, bbar_re)
    bu_im = jnp.einsum('blgc,gpc->lbgp', uf, bbar_im)
    a_seq_re = jnp.broadcast_to(abar_re[None, None], (seq_len, 1, SSM_GROUPS, SSM_STATE))
    a_seq_im = jnp.broadcast_to(abar_im[None, None], (seq_len, 1, SSM_GROUPS, SSM_STATE))
    _, _, s_re, s_im = lax.associative_scan(
        _complex_scan_combine, (a_seq_re, a_seq_im, bu_re, bu_im), axis=0)
    t = jnp.arange(1, seq_len + 1, dtype=f32)[:, None, None]
    pmag = jnp.exp(t * dt[None] * ar[None])
    pang = t * dt[None] * ai[None]
    pw_re = (pmag * jnp.cos(pang))[:, None]
    pw_im = (pmag * jnp.sin(pang))[:, None]
    s0r = s0_re.astype(f32)[None]
    s0i = s0_im.astype(f32)[None]
    s_re = s_re + pw_re * s0r - pw_im * s0i
    s_im = s_im + pw_re * s0i + pw_im * s0r
    y = (jnp.einsum('lbgp,gcp->blgc', s_re, c_re.astype(f32))
         - jnp.einsum('lbgp,gcp->blgc', s_im, c_im.astype(f32)))
    y = y.reshape(bsz, seq_len, D_SSM) + d.astype(f32) * uf.reshape(bsz, seq_len, D_SSM)
    y = jax.nn.gelu(y)
    y = y * jax.nn.sigmoid(y @ w_glu.astype(f32) + b_glu.astype(f32))
    return y.astype(u.dtype), s_re[-1].astype(s0_re.dtype), s_im[-1].astype(s0_im.dtype)


def pool_branch(u, buf, pos0, pool_mix, pool_scale):
    bsz, seq_len, _ = u.shape
    f32 = jnp.float32
    ext = jnp.concatenate([buf.astype(u.dtype), u], axis=1)
    uf = ext.astype(f32)
    cs = jnp.concatenate([jnp.zeros((bsz, 1, D_POOL), f32), jnp.cumsum(uf, axis=1)], axis=1)
    pos = pos0 + jnp.arange(seq_len)
    means = []
    for gi, w in enumerate(POOL_WINDOWS):
        lo, hi = gi * POOL_GROUP_CH, (gi + 1) * POOL_GROUP_CH
        win = (cs[:, POOL_BUF + 1:POOL_BUF + 1 + seq_len, lo:hi]
               - cs[:, POOL_BUF + 1 - w:POOL_BUF + 1 - w + seq_len, lo:hi])
        cnt = jnp.minimum(pos + 1, w).astype(f32)[None, :, None]
        means.append(win / cnt)
    pooled = jnp.concatenate(means, axis=-1) - uf[:, POOL_BUF:]
    mixed = jnp.einsum('blgc,gcd->blgd',
                       pooled.reshape(bsz, seq_len, POOL_GROUPS, POOL_GROUP_CH),
                       pool_mix.astype(f32)).reshape(bsz, seq_len, D_POOL)
    mixed = mixed * pool_scale.astype(f32)
    return mixed.astype(u.dtype), ext[:, -POOL_BUF:]


def hybrid_layer(h, s0_re, s0_im, buf, pos0, norm_gain, w_in, b_gate, ssm_a_re, ssm_a_im,
                 ssm_log_dt, ssm_b_re, ssm_b_im, ssm_c_re, ssm_c_im, ssm_d, w_glu, b_glu,
                 pool_mix, pool_scale, w_branch_ssm, w_branch_pool, w_out):
    xn = rmsnorm(h, norm_gain)
    proj = xn @ w_in
    o1 = D_SSM
    o2 = o1 + D_SSM
    o3 = o2 + D_POOL
    o4 = o3 + D_POOL
    u_s = proj[..., :o1]
    z_s = proj[..., o1:o2]
    u_p = proj[..., o2:o3]
    z_p = proj[..., o3:o4]
    gates = jax.nn.sigmoid(proj[..., o4:] + b_gate)
    g_s = gates[..., :D_MODEL]
    g_p = gates[..., D_MODEL:]
    y_s, new_re, new_im = ssm_branch(u_s, s0_re, s0_im, ssm_a_re, ssm_a_im, ssm_log_dt,
                                     ssm_b_re, ssm_b_im, ssm_c_re, ssm_c_im, ssm_d, w_glu, b_glu)
    y_p, new_buf = pool_branch(u_p, buf, pos0, pool_mix, pool_scale)
    a = y_s * jax.nn.silu(z_s)
    b = y_p * jax.nn.silu(z_p)
    merged = g_s * (a @ w_branch_ssm) + g_p * (b @ w_branch_pool)
    return h + merged @ w_out, new_re, new_im, new_buf


def setup_inputs(seed: int = 0) -> dict:
    key = jax.random.key(seed)
    ks = jax.random.split(key, 32)
    f32 = jnp.float32
    nrm = lambda k, shape, s: (jax.random.normal(k, shape, f32) * s)
    n_idx = jnp.arange(SSM_STATE, dtype=f32)
    a_re = -0.5 + 0.01 * jax.random.normal(ks[8], (DEPTH, SSM_GROUPS, SSM_STATE), f32)
    a_im = math.pi * n_idx[None, None, :] + 0.01 * jax.random.normal(ks[9], (DEPTH, SSM_GROUPS, SSM_STATE), f32)
    log_dt = jax.random.uniform(ks[10], (DEPTH, SSM_GROUPS), f32,
                                minval=math.log(DT_MIN), maxval=math.log(DT_MAX))
    return {
        "x_prompt": nrm(ks[0], (BATCH, SEQ, D_MODEL), 1.0),
        "x_sample": nrm(ks[1], (DEC_BATCH, DEC_SEQ, D_MODEL), 1.0),
        "state_ssm_re": nrm(ks[2], (DEPTH, DEC_BATCH, SSM_GROUPS, SSM_STATE), 0.1),
        "state_ssm_im": nrm(ks[3], (DEPTH, DEC_BATCH, SSM_GROUPS, SSM_STATE), 0.1),
        "state_pool": nrm(ks[4], (DEPTH, DEC_BATCH, POOL_BUF, D_POOL), 1.0),
        "meta_tokens": nrm(ks[5], (N_META, D_MODEL), 1.0),
        "norm_gain": 1.0 + nrm(ks[6], (DEPTH, D_MODEL), 0.01),
        "w_in": nrm(ks[7], (DEPTH, D_MODEL, D_IN), D_MODEL ** -0.5),
        "b_gate": nrm(ks[11], (DEPTH, 2 * D_MODEL), 0.01),
        "ssm_a_re": a_re,
        "ssm_a_im": a_im,
        "ssm_log_dt": log_dt,
        "ssm_b_re": nrm(ks[12], (DEPTH, SSM_GROUPS, SSM_STATE, SSM_GROUP_CH), (2.0 * SSM_GROUP_CH) ** -0.5),
        "ssm_b_im": nrm(ks[13], (DEPTH, SSM_GROUPS, SSM_STATE, SSM_GROUP_CH), (2.0 * SSM_GROUP_CH) ** -0.5),
        "ssm_c_re": nrm(ks[14], (DEPTH, SSM_GROUPS, SSM_GROUP_CH, SSM_STATE), (2.0 * SSM_STATE) ** -0.5),
        "ssm_c_im": nrm(ks[15], (DEPTH, SSM_GROUPS, SSM_GROUP_CH, SSM_STATE), (2.0 * SSM_STATE) ** -0.5),
        "ssm_d": 1.0 + nrm(ks[16], (DEPTH, D_SSM), 0.1),
        "w_glu": nrm(ks[17], (DEPTH, D_SSM, D_SSM), D_SSM ** -0.5),
        "b_glu": nrm(ks[18], (DEPTH, D_SSM), 0.01),
        "pool_mix": nrm(ks[19], (DEPTH, POOL_GROUPS, POOL_GROUP_CH, POOL_GROUP_CH), POOL_GROUP_CH ** -0.5),
        "pool_scale": 1.0 + nrm(ks[20], (DEPTH, D_POOL), 0.02),
        "w_branch_ssm": nrm(ks[21], (DEPTH, D_SSM, D_MODEL), D_SSM ** -0.5),
        "w_branch_pool": nrm(ks[22], (DEPTH, D_POOL, D_MODEL), D_POOL ** -0.5),
        "w_out": nrm(ks[23], (DEPTH, D_MODEL, D_MODEL), D_MODEL ** -0.5),
        "final_norm_gain": 1.0 + nrm(ks[24], (D_MODEL,), 0.01),
    }


def reference(x_prompt, x_sample, state_ssm_re, state_ssm_im, state_pool, meta_tokens,
              norm_gain, w_in, b_gate, ssm_a_re, ssm_a_im, ssm_log_dt, ssm_b_re, ssm_b_im,
              ssm_c_re, ssm_c_im, ssm_d, w_glu, b_glu, pool_mix, pool_scale,
              w_branch_ssm, w_branch_pool, w_out, final_norm_gain):
    meta = jnp.broadcast_to(meta_tokens[None].astype(x_prompt.dtype), (BATCH, N_META, D_MODEL))
    h_p = jnp.concatenate([meta, x_prompt], axis=1)
    h_s = x_sample
    p_re, p_im, p_buf = [], [], []
    s_re, s_im, s_buf = [], [], []
    for l in range(DEPTH):
        lw = (norm_gain[l], w_in[l], b_gate[l], ssm_a_re[l], ssm_a_im[l], ssm_log_dt[l],
              ssm_b_re[l], ssm_b_im[l], ssm_c_re[l], ssm_c_im[l], ssm_d[l], w_glu[l], b_glu[l],
              pool_mix[l], pool_scale[l], w_branch_ssm[l], w_branch_pool[l], w_out[l])
        zero_state = jnp.zeros((BATCH, SSM_GROUPS, SSM_STATE), state_ssm_re.dtype)
        zero_buf = jnp.zeros((BATCH, POOL_BUF, D_POOL), x_prompt.dtype)
        h_p, nr, ni, nb = hybrid_layer(h_p, zero_state, zero_state, zero_buf, 0, *lw)
        p_re.append(nr)
        p_im.append(ni)
        p_buf.append(nb)
        h_s, nr, ni, nb = hybrid_layer(h_s, state_ssm_re[l], state_ssm_im[l], state_pool[l],
                                       PAST_LEN, *lw)
        s_re.append(nr)
        s_im.append(ni)
        s_buf.append(nb)
    y_prompt = rmsnorm(h_p, final_norm_gain)[:, N_META:]
    y_sample = rmsnorm(h_s, final_norm_gain)
    new_ssm_re_prompt = jnp.stack(p_re, axis=0)
    new_ssm_im_prompt = jnp.stack(p_im, axis=0)
    new_pool_prompt = jnp.stack(p_buf, axis=0)
    new_ssm_re_sample = jnp.stack(s_re, axis=0)
    new_ssm_im_sample = jnp.stack(s_im, axis=0)
    new_pool_sample = jnp.stack(s_buf, axis=0)
    return (y_prompt, y_sample, new_ssm_re_prompt, new_ssm_im_prompt, new_pool_prompt,
            new_ssm_re_sample, new_ssm_im_sample, new_pool_sample)
```

```python
import contextlib
import math
import numpy as np
import concourse.bass as bass
import concourse.mybir as mybir
from concourse.bass_utils import run_bass_kernel_spmd

F32 = mybir.dt.float32
BF16 = mybir.dt.bfloat16
I32 = mybir.dt.int32
ALU = mybir.AluOpType
AF = mybir.ActivationFunctionType

NCORES = 8
D = 1024
NPROMPT = 2064
NTOT = 2192
SLOT = 9408
NTM = 1168
KU = 146
KX = 147
TILES = [(0, 1024), (1024, 1168)]
LAST = len(TILES) - 1
EPS = 1e-6


class Res:
    __slots__ = ("w", "r", "name", "excl")

    def __init__(self, name="", excl=False):
        self.w = None
        self.r = []
        self.name = name
        self.excl = excl


class Op:
    __slots__ = ("eng", "fn", "deps", "pos", "sigidx", "is_dma", "sem", "semval", "needed")

    def __init__(self, eng, fn, is_dma):
        self.eng = eng
        self.fn = fn
        self.deps = []
        self.pos = -1
        self.sigidx = None
        self.is_dma = is_dma
        self.sem = None
        self.semval = None
        self.needed = False


HAZ = 2


class Prog:
    ENGS = ["pe", "act", "dve", "pool", "sp"]

    def __init__(self, nc, n_dma_sems=14):
        self.nc = nc
        self.q = {e: [] for e in self.ENGS}
        self.n_dma_sems = n_dma_sems

    def _mk(self, eng, fn, reads, writes, deps, is_dma):
        op = Op(eng, fn, is_dma)
        ds = list(deps)
        if any(r.excl for r in reads):
            writes = list(writes) + [r for r in reads if r.excl]
            reads = [r for r in reads if not r.excl]
        for r in reads:
            if r.w is not None:
                ds.append(r.w)
        for w in writes:
            if w.w is not None:
                ds.append(w.w)
            ds.extend(w.r)
        best = {}
        dmas = []
        seen = set()
        for d in ds:
            if d is None or id(d) in seen:
                continue
            seen.add(id(d))
            if d.is_dma:
                dmas.append(d)
            else:
                b = best.get(d.eng)
                if b is None or d.pos > b.pos:
                    best[d.eng] = d
        op.deps = dmas + list(best.values())
        for r in reads:
            r.r.append(op)
            if len(r.r) > 64:
                keep = {}
                kd = []
                for x in r.r:
                    if x.is_dma:
                        kd.append(x)
                    elif x.eng not in keep or x.pos > keep[x.eng].pos:
                        keep[x.eng] = x
                r.r = kd + list(keep.values())
        for w in writes:
            w.w = op
            w.r = []
        op.pos = len(self.q[eng])
        self.q[eng].append(op)
        return op

    def op(self, eng, fn, reads=(), writes=(), deps=()):
        return self._mk(eng, fn, reads, writes, deps, False)

    def dma(self, eng, fn, reads=(), writes=(), deps=()):
        return self._mk(eng, fn, reads, writes, deps, True)

    def emit(self):
        nc = self.nc
        for e in self.ENGS:
            for op in self.q[e]:
                for d in op.deps:
                    if d.is_dma or d.eng != op.eng:
                        d.needed = True
                    elif d.eng != "pe" and (op.pos - d.pos) <= HAZ:
                        d.needed = True
        for e in self.ENGS:
            c = 0
            for op in self.q[e]:
                if (not op.is_dma) and op.needed:
                    c += 1
                    op.sigidx = c
        with contextlib.ExitStack() as st:
            esem = {e: st.enter_context(nc.semaphore("s_" + e)) for e in self.ENGS}
            dsems = {}
            for e in self.ENGS:
                if any(o.is_dma for o in self.q[e]):
                    dsems[e] = [st.enter_context(nc.semaphore("d_%s_%d" % (e, i)))
                                for i in range(self.n_dma_sems)]
            for e, sems in dsems.items():
                cnt = [0] * len(sems)
                prev = [None] * len(sems)
                i = 0
                for op in self.q[e]:
                    if op.is_dma:
                        k = i % len(sems)
                        cnt[k] += 1
                        op.sem = sems[k]
                        op.semval = 16 * cnt[k]
                        if prev[k] is not None:
                            op.deps.append(prev[k])
                        prev[k] = op
                        i += 1
            block = st.enter_context(nc.Block())
            handles = {"pe": block.tensor, "act": block.scalar, "dve": block.vector,
                       "pool": block.gpsimd, "sp": block.sync}

            def mk(e):
                ops = self.q[e]

                def body(eng):
                    waited = {}
                    for op in ops:
                        for d in op.deps:
                            if d.is_dma:
                                key = ("d", d.sem.name)
                                val = d.semval
                                sem = d.sem
                            else:
                                if d.eng == op.eng:
                                    if d.eng == "pe" or (op.pos - d.pos) > HAZ:
                                        continue
                                key = ("e", d.eng)
                                val = d.sigidx
                                sem = esem[d.eng]
                            if waited.get(key, 0) >= val:
                                continue
                            waited[key] = val
                            eng.wait_ge(sem, val)
                        ins = op.fn(eng)
                        if op.is_dma:
                            ins.then_inc(op.sem, 16)
                        elif op.needed:
                            ins.then_inc(esem[op.eng], 1)
                    if e in dsems:
                        last = {}
                        for op in ops:
                            if op.is_dma:
                                last[op.sem.name] = op
                        for op in last.values():
                            if waited.get(("d", op.sem.name), 0) < op.semval:
                                eng.wait_ge(op.sem, op.semval)
                return body

            for e in self.ENGS:
                if self.q[e]:
                    handles[e](mk(e))


def build_program():
    nc = bass.Bass("TRN2", target_bir_lowering=False)

    def din(name, shape):
        return nc.dram_tensor(name, list(shape), F32, kind="ExternalInput").ap()

    def dout(name, shape):
        return nc.dram_tensor(name, list(shape), F32, kind="ExternalOutput").ap()

    xall = din("xall", [NTOT, D])
    s0re = din("s0re", [16, 64, 64])
    s0im = din("s0im", [16, 64, 64])
    spool = din("spool", [16, 15, D])
    w_in = din("w_in", [D, 6 * D])
    w_glu = din("w_glu", [D, D])
    pool_mix = din("pool_mix", [4, 256, 256])
    w_bs = din("w_bs", [D, D])
    w_bp = din("w_bp", [D, D])
    w_out = din("w_out", [D, D])
    norm_gain = din("norm_gain", [D])
    b_gate = din("b_gate", [2 * D])
    ssm_d = din("ssm_d", [D])
    b_glu = din("b_glu", [D])
    pool_scale = din("pool_scale", [D])
    fgain = din("fgain", [D])
    a_re = din("a_re", [64, 64])
    a_im = din("a_im", [64, 64])
    log_dt = din("log_dt", [64])
    b_re = din("b_re", [64, 64, 16])
    b_im = din("b_im", [64, 64, 16])
    c_re = din("c_re", [64, 16, 64])
    c_im = din("c_im", [64, 16, 64])
    c_ident = din("c_ident", [128, 128])
    c_mask = din("c_mask", [128, 128])
    c_nvals = din("c_nvals", [128, 32, 8])
    c_invc = din("c_invc", [128, 4, 16])

    y_p = dout("y_p", [2048, D])
    y_s = dout("y_s", [128, D])
    nre_p = dout("nre_p", [64, 64])
    nim_p = dout("nim_p", [64, 64])
    npool_p = dout("npool_p", [15, D])
    nre_s = dout("nre_s", [16, 64, 64])
    nim_s = dout("nim_s", [16, 64, 64])
    npool_s = dout("npool_s", [16, 15, D])

    with contextlib.ExitStack() as st:
        def sb(name, shape, dt):
            return st.enter_context(nc.sbuf_tensor(name, list(shape), dt))

        P = Prog(nc)
        NC = True

        arena = sb("arena", [128, 5 * SLOT], BF16)
        SL = {k: i * SLOT for i, k in enumerate("ABCED")}
        RS = {k: [Res(k + "0"), Res(k + "1"), Res(k + "2")] for k in "ABCDE"}

        def slot_fm(k):
            o = SL[k]
            return arena[:, o:o + 8 * NTM].rearrange("p (a n) -> p a n", a=8)

        def slot_raw(k, n=SLOT):
            o = SL[k]
            return arena[:, o:o + n]

        wbuf = sb("wbuf", [128, 2, 8, 512], BF16)
        RW = [Res("w0"), Res("w1")]
        W1 = sb("W1", [128, 64, 128], BF16)
        TM = sb("TM", [128, 64, 128], BF16)
        W3 = sb("W3", [128, 2, 32, 128], BF16)
        R_W1, R_TM, R_W3 = Res("W1"), Res("TM"), Res("W3")
        pmw = sb("pmw", [128, 4, 2, 256], BF16)
        R_pmw = Res("pmw")
        gB = sb("gB", [128, D], F32)
        fB = sb("fB", [128, D], F32)
        R_gB, R_fB = Res("gB"), Res("fB")
        NXT = 3
        xt = [sb("xt%d" % i, [128, D], F32) for i in range(NXT)]
        RXT = [Res("xt%d" % i) for i in range(NXT)]
        xn = [sb("xn%d" % i, [128, D], BF16) for i in range(2)]
        RXN = [Res("xn0"), Res("xn1")]
        NTMP = 2
        tmpb = [sb("tmpb%d" % i, [128, 512], BF16) for i in range(NTMP)]
        RTMP = [Res("tmp%d" % i) for i in range(NTMP)]
        NYML = 4
        yml = [sb("yml%d" % i, [128, 4, 128], BF16) for i in range(NYML)]
        RYML = [Res("yml%d" % i) for i in range(NYML)]
        ycm = sb("ycm", [128, 2, 1024], BF16)
        RYCM = [Res("ycm0"), Res("ycm1")]
        identf = sb("identf", [128, 128], F32)
        identb = sb("identb", [128, 128], BF16)
        maskf = sb("maskf", [128, 128], F32)
        invc = sb("invc", [128, 4, 16], F32)
        R_const = Res("const")
        vecs = sb("vecs", [128, 32], F32)
        bgate = vecs[:, 0:16]
        bglu = vecs[:, 16:24]
        pscale = vecs[:, 24:32]
        NSTG = 2
        stg = sb("stg", [128, NSTG, 128], F32)
        RSTG = [Res("stg%d" % i) for i in range(NSTG)]
        R_vec = Res("vec")
        Dt = sb("Dt", [128, 64], F32)
        ArAr = sb("ArAr", [128, 2, 32], F32)
        AiPM = sb("AiPM", [128, 2, 32], F32)
        ArAr2 = sb("ArAr2", [128, 2, 32], F32)
        AiPM2 = sb("AiPM2", [128, 2, 32], F32)
        R_A8 = Res("A8")
        Sf = [sb("Sf%d" % i, [128, 2, 32], F32) for i in range(2)]
        RSF = [Res("Sf0"), Res("Sf1")]
        st1 = sb("st1", [128, 2, 32], F32)
        st2 = sb("st2", [128, 2, 32], F32)
        R_st1, R_st2 = Res("st1"), Res("st2")
        R_XSx = Res("XSx")
        S0f = sb("S0f", [128, 2, 32, 16], F32)
        R_S0f = Res("S0f")
        UPF = sb("UPF", [128, 8, 144], F32)
        R_UPF = Res("UPF")
        bufT = arena[:, SL["B"] + 7168:SL["B"] + 7168 + 1920].rearrange("p (a b r) -> p a b r", a=8, b=16)
        carry = sb("carry", [128, 8, 15], BF16)
        R_carry = Res("carry")
        stat = sb("stat", [128, 8], F32)
        RSTAT = [Res("stat%d" % i) for i in range(4)]

        NB = 8
        psf = [st.enter_context(nc.psum_tensor("ps%d" % i, [128, 512], F32)) for i in range(NB)]
        RB = [Res("bank%d" % i, excl=True) for i in range(NB)]
        bank_ctr = [0]

        def getbank():
            i = bank_ctr[0] % NB
            bank_ctr[0] += 1
            return psf[i], RB[i]

        rr = {"xt": 0, "xn": 0, "tmp": 0, "yml": 0, "w": 0, "ev": 0, "stat": 0, "stg": 0}

        def nxt(key, n):
            i = rr[key] % n
            rr[key] += 1
            return i

        def evac_eng():
            rr["ev"] += 1
            return "act" if rr["ev"] % 2 == 0 else "dve"

        def copy_op(eng, out, in_):
            if eng == "act":
                return lambda e: e.activation(out=out, in_=in_, func=AF.Copy)
            return lambda e: e.tensor_copy(out=out, in_=in_)

        P.dma("sp", lambda e: e.dma_start(out=identf[:], in_=c_ident), writes=[R_const])
        P.dma("sp", lambda e: e.dma_start(out=maskf[:], in_=c_mask), writes=[R_const])
        P.dma("sp", lambda e: e.dma_start(out=invc[:], in_=c_invc), writes=[R_const])
        P.dma("sp", lambda e: e.dma_start(out=gB[:], in_=norm_gain.partition_broadcast(128)), writes=[R_gB])
        P.dma("sp", lambda e: e.dma_start(out=fB[:], in_=fgain.partition_broadcast(128)), writes=[R_fB])
        P.op("dve", lambda e: e.memset(stg[:], 0.0), writes=RSTG)

        def stage_T(loads, K, ncols, evacs, tag=""):
            k = nxt("stg", NSTG)
            for (sl, src) in loads:
                P.dma("sp", lambda e, sl=sl, src=src: e.dma_start(out=sl(k), in_=src), writes=[RSTG[k]])
            bank, rb = getbank()
            P.op("pe", lambda e: e.transpose(bank[:, 0:K], in_=stg[0:K, k, :], identity=identf[0:K, 0:K]),
                 reads=[RSTG[k], R_const], writes=[rb])
            for (dst, srcf, wr) in evacs:
                eng = evac_eng()
                P.op(eng, copy_op(eng, dst, srcf(bank)), reads=[rb], writes=wr)

        stage_T([(lambda k: stg[0:16, k, :], b_gate.rearrange("(a p) -> a p", p=128)),
                 (lambda k: stg[16:24, k, :], b_glu.rearrange("(a p) -> a p", p=128)),
                 (lambda k: stg[24:32, k, :], pool_scale.rearrange("(a p) -> a p", p=128))],
                32, 128, [(vecs[:, :], lambda bank: bank[:, 0:32], [R_vec])], tag="v")
        P.op("dve", lambda e: e.tensor_copy(out=identb[:], in_=identf[:]), reads=[R_const], writes=[R_const])

        def f32view(off_bf16, shape):
            n = int(np.prod(shape))
            v = arena[:, off_bf16:off_bf16 + 2 * n].bitcast(F32)
            return v

        W3f_off = 0
        H_off = 16384
        sm_off = 32768
        W3f = W3[:].rearrange("p r g (j c) -> p r g j c", j=8)
        Hf = arena[:, H_off:H_off + 8192].rearrange("p (r g j c) -> p r g j c", r=2, g=32, j=8)
        R_W3f, R_Hf = R_W3, Res("Hf")
        smp = [sm_off]

        def small(shape):
            n = int(np.prod(shape))
            v = f32view(smp[0], [n])
            smp[0] += 2 * n
            return v

        def small3(a, b):
            return small([a * b]).rearrange("p (a b) -> p a b", a=a)

        are = small([32]); aim = small([32]); ldt = small([32]); dtt = small([32])
        dtar = small([32]); th = small([32])
        upf_flat = UPF[:].rearrange("p a b -> p (a b)")

        def as_s3(ap2d):
            v = ap2d if ap2d.dtype == F32 else ap2d.bitcast(F32)
            return v.rearrange("p (a b) -> p a b", a=32)
        ang = as_s3(yml[0][:].rearrange("p a b -> p (a b)"))
        ang2 = as_s3(yml[1][:].rearrange("p a b -> p (a b)"))
        marg = as_s3(yml[2][:].rearrange("p a b -> p (a b)"))
        magp = as_s3(yml[3][:].rearrange("p a b -> p (a b)"))
        magn = as_s3(tmpb[0][:])
        yk = as_s3(tmpb[1][:])
        kf = as_s3(upf_flat[:, 0:256])
        rs = as_s3(upf_flat[:, 256:512])
        rc = small3(32, 8)
        sn = small3(32, 8); cs = small3(32, 8)
        Pre = small3(32, 8); Pim = small3(32, 8); Nre = small3(32, 8); Nim = small3(32, 8)
        nr = small([32]); den = small([32]); rden = small([32])
        qre = small([32]); qim = small([32]); tq1 = small([32]); tq2 = small([32])
        nvals = upf_flat[:, 512:768].rearrange("p (a b) -> p a b", a=32)
        ki = upf_flat[:, 768:1024].bitcast(I32).rearrange("p (a b) -> p a b", a=32)

        def xth(i, part):
            return xt[i][:, part * 512:(part + 1) * 512].rearrange("p (a b) -> p a b", a=32)
        Bre = xth(0, 0); Bim = xth(0, 1); Cre = xth(1, 0); Cim = xth(1, 1); Bbre = xth(2, 0); Bbim = xth(2, 1)
        wflat = wbuf[:].rearrange("p a k n -> p (a k n)").bitcast(F32)
        t4a = wflat[:, 0:2048].rearrange("p (g j c) -> p g j c", g=32, j=4)
        t4b = wflat[:, 2048:4096].rearrange("p (g j c) -> p g j c", g=32, j=4)
        assert smp[0] <= 4 * SLOT, smp[0]
        R_sx = [RXT[0], RXT[1], RXT[2], R_UPF, RW[0], RW[1]] + RYML + RTMP
        R_s = Res("setup_small")

        for (src, dstv) in ((a_re, are), (a_im, aim)):
            stage_T([(lambda k: stg[0:64, k, 0:64], src), (lambda k: stg[0:64, k, 64:128], src)], 64, 128,
                    [(dstv[0:64, :], lambda bank: bank[0:64, 0:32], [R_s]),
                     (dstv[64:128, :], lambda bank: bank[64:128, 32:64], [R_s])], tag="a")
        for gh in range(2):
            ps_ = slice(gh * 64, (gh + 1) * 64)
            gs_ = slice(gh * 32, (gh + 1) * 32)
            P.dma("sp", lambda e, ps_=ps_, gs_=gs_: e.dma_start(
                out=ldt[ps_, :], in_=log_dt[gs_].partition_broadcast(64)), writes=[R_s])
            P.dma("sp", lambda e, ps_=ps_, gs_=gs_: e.dma_start(
                out=Bre[ps_, :, :], in_=b_re[gs_].rearrange("g p c -> p g c")), writes=[R_s, RXT[0]])
            P.dma("sp", lambda e, ps_=ps_, gs_=gs_: e.dma_start(
                out=Bim[ps_, :, :], in_=b_im[gs_].rearrange("g p c -> p g c")), writes=[R_s, RXT[0]])
        for r in range(8):
            gh = r // 4
            ps_ = slice(gh * 64, (gh + 1) * 64)
            for (src, dstC) in ((c_re, Cre), (c_im, Cim)):
                stage_T([(lambda k, ps_=ps_: stg[:, k, ps_],
                          src.rearrange("g c p -> (g c) p")[128 * r:128 * (r + 1), :])], 128, 128,
                        [(dstC[ps_, (r % 4) * 8:(r % 4) * 8 + 8, :].rearrange("p g c -> p (g c)"),
                          lambda bank, ps_=ps_: bank[ps_, 0:128], [R_s, RXT[1]])], tag="c")
        P.dma("sp", lambda e: e.dma_start(out=nvals, in_=c_nvals), writes=[R_s, R_UPF])
        stage_T([(lambda k: stg[0:64, k, :].rearrange("g (s c) -> g s c", s=8),
                  ssm_d.rearrange("(g c) -> g c", c=16).unsqueeze(1).broadcast_to([64, 8, 16]))], 64, 128,
                [(Dt[:, :], lambda bank: bank[:, 0:64], [R_vec])], tag="d")
        def load_S0(S0f, R_S0f):
            for ri, src in enumerate((s0re, s0im)):
                for r in range(8):
                    rows = src.rearrange("b g p -> (b g) p")[128 * r:128 * (r + 1), :]
                    stage_T([(lambda k: stg[:, k, 0:64], rows), (lambda k: stg[:, k, 64:128], rows)], 128, 128,
                            [(S0f[0:64, ri, :, 2 * r:2 * r + 2].rearrange("p g b -> p b g"),
                              lambda bank: bank[0:64, 0:128].rearrange("p (b g) -> p b g", b=2)[:, :, 0:32], [R_S0f]),
                             (S0f[64:128, ri, :, 2 * r:2 * r + 2].rearrange("p g b -> p b g"),
                              lambda bank: bank[64:128, 0:128].rearrange("p (b g) -> p b g", b=2)[:, :, 32:64],
                              [R_S0f])])
        P.dma("pool", lambda e: e.dma_start(out=pmw[:], in_=pool_mix.rearrange("g (k p) n -> p g k n", p=128)),
              writes=[R_pmw])

        def phaseN_gen(ti, T0, NT, first_deps, xt_ids=None):
            xnT = slot_fm("D")
            for r0 in range(0, NT, 128):
                rows = min(128, NT - r0)
                ix = nxt("xt", NXT) if xt_ids is None else xt_ids[(r0 // 128) % len(xt_ids)]
                P.dma("sp", lambda e, ix=ix, r0=r0, rows=rows: e.dma_start(
                    out=xt[ix][0:rows, :], in_=xall[T0 + r0:T0 + r0 + rows, :]), writes=[RXT[ix]])
                si = nxt("stat", 4)
                jn = nxt("xn", 2)
                P.op("act", lambda e, ix=ix, rows=rows, si=si, jn=jn: e.activation(
                    out=xn[jn][0:rows, :], in_=xt[ix][0:rows, :], func=AF.Square,
                    accum_out=stat[0:rows, 2 * si:2 * si + 1]), reads=[RXT[ix]], writes=[RXN[jn], RSTAT[si]])
                P.op("act", lambda e, rows=rows, si=si: e.activation(
                    out=stat[0:rows, 2 * si + 1:2 * si + 2], in_=stat[0:rows, 2 * si:2 * si + 1],
                    func=AF.Sqrt, scale=1.0 / D, bias=EPS), reads=[RSTAT[si]], writes=[RSTAT[si]])
                P.op("dve", lambda e, rows=rows, si=si: e.reciprocal(
                    out=stat[0:rows, 2 * si:2 * si + 1], in_=stat[0:rows, 2 * si + 1:2 * si + 2]),
                    reads=[RSTAT[si]], writes=[RSTAT[si]])
                P.op("dve", lambda e, ix=ix, jn=jn, rows=rows, si=si: e.scalar_tensor_tensor(
                    out=xn[jn][0:rows, :], in0=xt[ix][0:rows, :], scalar=stat[0:rows, 2 * si:2 * si + 1],
                    in1=gB[0:rows, :], op0=ALU.mult, op1=ALU.mult),
                    reads=[RXT[ix], RSTAT[si], R_gB], writes=[RXN[jn]])
                bank, rb = getbank()
                bb = bank[:].bitcast(BF16)

                def trn(e, jn=jn, rows=rows, bb=bb):
                    ins = None
                    for kt in range(8):
                        ins = e.transpose(bb[:, kt * 128:kt * 128 + rows], in_=xn[jn][0:rows, kt * 128:(kt + 1) * 128],
                                          identity=identb[0:rows, 0:rows])
                    return ins
                P.op("pe", trn, reads=[RXN[jn], R_const], writes=[rb])
                eng = evac_eng()
                P.op(eng, copy_op(eng, xnT[:, :, r0:r0 + rows],
                                  bb.rearrange("p (k t) -> p k t", k=8)[:, :, 0:rows]),
                     reads=[rb], writes=[RS["D"][r0 // 512]], deps=first_deps)
                yield


        def S(eng, fn):
            return P.op(eng, fn, reads=[R_s, R_vec], writes=[R_s] + R_sx)

        def bc8(v):
            return v.unsqueeze(2).broadcast_to([128, 32, 8])

        TWO_PI = 2.0 * math.pi
        S("act", lambda e: e.activation(out=dtt, in_=ldt, func=AF.Exp))
        S("dve", lambda e: e.tensor_tensor(out=dtar, in0=dtt, in1=are, op=ALU.mult))
        S("dve", lambda e: e.tensor_tensor(out=th, in0=dtt, in1=aim, op=ALU.mult))
        S("dve", lambda e: e.tensor_tensor(out=ang, in0=nvals, in1=bc8(th), op=ALU.mult))
        S("dve", lambda e: e.tensor_tensor(out=marg, in0=nvals, in1=bc8(dtar), op=ALU.mult))
        S("act", lambda e: e.activation(out=magp, in_=marg, func=AF.Exp))
        S("act", lambda e: e.activation(out=magn, in_=marg, func=AF.Exp, scale=-1.0))
        S("dve", lambda e: e.tensor_scalar(out=ang2, in0=ang, scalar1=math.pi / 2, scalar2=None, op0=ALU.add))

        def reduce_angle(src, dst):
            S("dve", lambda e: e.tensor_scalar(out=yk, in0=src, scalar1=1.0 / TWO_PI, scalar2=None, op0=ALU.mult))
            S("dve", lambda e: e.tensor_copy(out=ki, in_=yk))
            S("dve", lambda e: e.tensor_copy(out=kf, in_=ki))
            S("dve", lambda e: e.scalar_tensor_tensor(out=dst, in0=kf, scalar=-TWO_PI, in1=src,
                                                      op0=ALU.mult, op1=ALU.add))
            S("dve", lambda e: e.tensor_scalar(out=dst, in0=dst, scalar1=3.1415925, scalar2=-3.1415925,
                                               op0=ALU.min, op1=ALU.max))

        reduce_angle(ang, rs)
        S("act", lambda e: e.activation(out=sn, in_=rs, func=AF.Sin))
        reduce_angle(ang2, rc)
        S("act", lambda e: e.activation(out=cs, in_=rc, func=AF.Sin))
        S("dve", lambda e: e.tensor_tensor(out=Pre, in0=magp, in1=cs, op=ALU.mult))
        S("dve", lambda e: e.tensor_tensor(out=Pim, in0=magp, in1=sn, op=ALU.mult))
        S("dve", lambda e: e.tensor_tensor(out=Nre, in0=magn, in1=cs, op=ALU.mult))
        S("dve", lambda e: e.scalar_tensor_tensor(out=Nim, in0=magn, scalar=-1.0, in1=sn, op0=ALU.mult, op1=ALU.mult))
        S("dve", lambda e: e.tensor_scalar(out=nr, in0=Pre[:, :, 0], scalar1=-1.0, scalar2=None, op0=ALU.add))
        S("dve", lambda e: e.tensor_tensor(out=den, in0=are, in1=are, op=ALU.mult))
        S("dve", lambda e: e.tensor_tensor(out=tq1, in0=aim, in1=aim, op=ALU.mult))
        S("dve", lambda e: e.tensor_tensor(out=den, in0=den, in1=tq1, op=ALU.add))
        S("dve", lambda e: e.reciprocal(out=rden, in_=den))
        S("dve", lambda e: e.tensor_tensor(out=tq1, in0=nr, in1=are, op=ALU.mult))
        S("dve", lambda e: e.tensor_tensor(out=tq2, in0=Pim[:, :, 0], in1=aim, op=ALU.mult))
        S("dve", lambda e: e.tensor_tensor(out=tq1, in0=tq1, in1=tq2, op=ALU.add))
        S("dve", lambda e: e.tensor_tensor(out=qre, in0=tq1, in1=rden, op=ALU.mult))
        S("dve", lambda e: e.tensor_tensor(out=tq1, in0=Pim[:, :, 0], in1=are, op=ALU.mult))
        S("dve", lambda e: e.tensor_tensor(out=tq2, in0=nr, in1=aim, op=ALU.mult))
        S("dve", lambda e: e.tensor_tensor(out=tq1, in0=tq1, in1=tq2, op=ALU.subtract))
        S("dve", lambda e: e.tensor_tensor(out=qim, in0=tq1, in1=rden, op=ALU.mult))

        def bc16(v):
            return v.unsqueeze(2).broadcast_to([128, 32, 16])

        t3a = t4a[:, :, 0, :]
        t3b = t4b[:, :, 0, :]
        S("dve", lambda e: e.tensor_tensor(out=t3a, in0=Bre, in1=bc16(qre), op=ALU.mult))
        S("dve", lambda e: e.tensor_tensor(out=t3b, in0=Bim, in1=bc16(qim), op=ALU.mult))
        S("dve", lambda e: e.tensor_tensor(out=Bbre, in0=t3a, in1=t3b, op=ALU.subtract))
        S("dve", lambda e: e.tensor_tensor(out=t3a, in0=Bim, in1=bc16(qre), op=ALU.mult))
        S("dve", lambda e: e.tensor_tensor(out=t3b, in0=Bre, in1=bc16(qim), op=ALU.mult))
        S("dve", lambda e: e.tensor_tensor(out=Bbim, in0=t3a, in1=t3b, op=ALU.add))
        P.op("dve", lambda e: e.tensor_copy(out=ArAr[:, 0, :], in_=Pre[:, :, 7]), reads=[R_s], writes=[R_A8])
        P.op("dve", lambda e: e.tensor_copy(out=ArAr[:, 1, :], in_=Pre[:, :, 7]), reads=[R_s], writes=[R_A8])
        P.op("dve", lambda e: e.tensor_scalar(out=AiPM[:, 0, :], in0=Pim[:, :, 7], scalar1=-1.0, scalar2=None,
                                              op0=ALU.mult), reads=[R_s], writes=[R_A8])
        P.op("dve", lambda e: e.tensor_copy(out=AiPM[:, 1, :], in_=Pim[:, :, 7]), reads=[R_s], writes=[R_A8])
        S("dve", lambda e: e.tensor_tensor(out=tq1, in0=Pre[:, :, 7], in1=Pre[:, :, 7], op=ALU.mult))
        S("dve", lambda e: e.tensor_tensor(out=tq2, in0=Pim[:, :, 7], in1=Pim[:, :, 7], op=ALU.mult))
        P.op("dve", lambda e: e.tensor_tensor(out=ArAr2[:, 0, :], in0=tq1, in1=tq2, op=ALU.subtract),
             reads=[R_s], writes=[R_A8])
        P.op("dve", lambda e: e.tensor_tensor(out=ArAr2[:, 1, :], in0=tq1, in1=tq2, op=ALU.subtract),
             reads=[R_s], writes=[R_A8])
        S("dve", lambda e: e.tensor_tensor(out=tq1, in0=Pre[:, :, 7], in1=Pim[:, :, 7], op=ALU.mult))
        P.op("dve", lambda e: e.tensor_scalar(out=AiPM2[:, 0, :], in0=tq1, scalar1=-2.0, scalar2=None, op0=ALU.mult),
             reads=[R_s], writes=[R_A8])
        P.op("dve", lambda e: e.tensor_scalar(out=AiPM2[:, 1, :], in0=tq1, scalar1=2.0, scalar2=None, op0=ALU.mult),
             reads=[R_s], writes=[R_A8])
        P.op("dve", lambda e: e.memset(Sf[0][:], 0.0), writes=[RSF[0]])
        P.op("dve", lambda e: e.memset(carry[:], 0.0), writes=[R_carry])

        def bcj(v, n=4):
            return v.unsqueeze(2).broadcast_to([128, 32, n, 16])

        def bcc(v, n=4):
            return v.unsqueeze(3).broadcast_to([128, 32, n, 16])

        def cplx(dst_re, dst_im, Xre, Xim, Yre, Yim, neg_im, writes, n=4):
            ta, tb = t4a[:, :, 0:n, :], t4b[:, :, 0:n, :]

            def op(fn):
                P.op("dve", fn, reads=[R_s], writes=writes + [R_s] + R_sx)
            op(lambda e: e.tensor_tensor(out=ta, in0=Xre, in1=Yre, op=ALU.mult))
            op(lambda e: e.tensor_tensor(out=tb, in0=Xim, in1=Yim, op=ALU.mult))
            op(lambda e: e.tensor_tensor(out=dst_re, in0=ta, in1=tb, op=ALU.subtract))
            op(lambda e: e.tensor_tensor(out=ta, in0=Xre, in1=Yim, op=ALU.mult))
            op(lambda e: e.tensor_tensor(out=tb, in0=Xim, in1=Yre, op=ALU.mult))
            if neg_im:
                op(lambda e: e.scalar_tensor_tensor(out=dst_im, in0=ta, scalar=-1.0, in1=tb,
                                                    op0=ALU.mult, op1=ALU.subtract))
            else:
                op(lambda e: e.tensor_tensor(out=dst_im, in0=ta, in1=tb, op=ALU.add))

        for jh in range(2):
            js = slice(jh * 4, jh * 4 + 4)
            cplx(W3f[:, 0, :, js, :], W3f[:, 1, :, js, :], bcj(Cre), bcj(Cim), bcc(Pre[:, :, js]), bcc(Pim[:, :, js]),
                 True, [R_W3f])
        for jh in range(2):
            js = slice(jh * 4, jh * 4 + 4)
            cplx(Hf[:, 0, :, js, :], Hf[:, 1, :, js, :], bcj(Bbre), bcj(Bbim), bcc(Nre[:, :, js]), bcc(Nim[:, :, js]),
                 False, [R_Hf])
        tmfs = [ycm[:, 0, :].bitcast(F32).rearrange("p (a b) -> p a b", a=4),
                ycm[:, 1, :].bitcast(F32).rearrange("p (a b) -> p a b", a=4)]
        R_tmf = RYCM
        n0_gen = phaseN_gen(0, TILES[0][0], TILES[0][1], [], xt_ids=[0, 1])
        for g4 in range(16):
            if g4 % 2 == 1:
                next(n0_gen, None)
            bank, rb = getbank()

            def mm(e, g4=g4, bank=bank):
                ins = None
                for gi in range(4):
                    g = g4 * 4 + gi
                    gh, g32 = divmod(g, 32)
                    ps_ = slice(gh * 64, (gh + 1) * 64)
                    o = bank[:, gi * 128:(gi + 1) * 128]
                    e.matmul(o, lhsT=Hf[ps_, 0, g32].rearrange("p j c -> p (j c)"),
                             rhs=W3[ps_, 0, g32, :], start=True, stop=False)
                    ins = e.matmul(o, lhsT=Hf[ps_, 1, g32].rearrange("p j c -> p (j c)"),
                                   rhs=W3[ps_, 1, g32, :], start=False, stop=True)
                return ins
            P.op("pe", mm, reads=[R_Hf, R_W3f], writes=[rb])
            tmf_ = tmfs[g4 % 2]
            rt_ = R_tmf[g4 % 2]
            P.op("dve", lambda e, bank=bank, tmf_=tmf_: e.tensor_tensor(
                out=tmf_, in0=bank[:, :].rearrange("p (a b) -> p a b", a=4),
                in1=maskf[:].unsqueeze(1).broadcast_to([128, 4, 128]), op=ALU.mult),
                reads=[rb, R_const], writes=[rt_])
            for gi in range(4):
                g = g4 * 4 + gi
                P.op("dve", lambda e, g=g, gi=gi, tmf_=tmf_: e.scalar_tensor_tensor(
                    out=TM[:, g, :], in0=identf[:], scalar=Dt[:, g:g + 1], in1=tmf_[:, gi, :],
                    op0=ALU.mult, op1=ALU.add), reads=[rt_, R_const, R_vec], writes=[R_TM])
        W1T = Hf

        def W(fn):
            P.op("dve", fn, reads=[R_s], writes=[R_Hf, R_s] + R_sx)
        PreR = Pre[:, :, 6::-1]
        PimR = Pim[:, :, 6::-1]
        for (j0, n) in ((0, 4), (4, 3)):
            js = slice(j0, j0 + n)
            cplx(W1T[:, 0, :, js, :], W1T[:, 1, :, js, :], bcj(Bbre, n), bcj(Bbim, n),
                 bcc(PreR[:, :, js], n), bcc(PimR[:, :, js], n), False, [R_Hf], n=n)
        W(lambda e: e.tensor_copy(out=W1T[:, 0, :, 7, :], in_=Bbre))
        W(lambda e: e.tensor_copy(out=W1T[:, 1, :, 7, :], in_=Bbim))
        for g4 in range(16):
            bank, rb = getbank()
            bbw = bank[:].bitcast(BF16)

            def tr(e, g4=g4, bbw=bbw):
                ins = None
                for gi in range(4):
                    g = g4 * 4 + gi
                    gh, g32 = divmod(g, 32)
                    ps_ = slice(gh * 64, (gh + 1) * 64)
                    for ri in range(2):
                        c0 = (gi * 2 + ri) * 64
                        ins = e.transpose(bbw[:, c0:c0 + 64], in_=W1T[ps_, ri, g32, :, :].rearrange("p j c -> p (j c)"),
                                          identity=identb[ps_, ps_])
                return ins
            P.op("pe", tr, reads=[R_Hf, R_const], writes=[rb])
            eng = evac_eng()
            P.op(eng, copy_op(eng, W1[:, g4 * 4:(g4 + 1) * 4, :].rearrange("p a b -> p (a b)"), bbw[:, 0:512]),
                 reads=[rb], writes=[R_W1])

        setup_done = [P.q[e_][-1] for e_ in ("pe", "act", "dve") if P.q[e_]]


        NBLK = 20
        wcache = nc.dram_tensor("wcache", [NBLK, 128, 4096], BF16).ap()
        wc_idx = {}
        RC = [Res("wc%d" % i) for i in range(NBLK)]

        def load_w(src_ap, col0, name):
            i = nxt("w", 2)
            key = (name, col0)
            flat = wbuf[:, i].rearrange("p k n -> p (k n)")
            if key not in wc_idx:
                idx = len(wc_idx)
                wc_idx[key] = idx
                P.dma("pool", lambda e: e.dma_start(
                    out=wbuf[:, i, :, :],
                    in_=src_ap[:, col0:col0 + 512].rearrange("(k p) n -> p k n", p=128)), writes=[RW[i]])
                P.dma("pool", lambda e: e.dma_start(out=wcache[idx], in_=flat), reads=[RW[i]], writes=[RC[idx]])
            else:
                idx = wc_idx[key]
                P.dma("pool", lambda e: e.dma_start(out=flat, in_=wcache[idx]), reads=[RC[idx]], writes=[RW[i]])
            return wbuf[:, i], RW[i]

        state = {"sf": 0}

        def do_tile(ti, T0, NT):
            NCH = NT // 8
            ntiles = [(n0, min(512, NT - n0)) for n0 in range(0, NT, 512)]
            csubs = [(c0, min(128, NCH - c0)) for c0 in range(0, NCH, 128)]
            nP = 128 if ti == 0 else 130
            NTp = 8 * nP

            def rh(k, n0):
                return RS[k][n0 // 512]


            first_deps = setup_done if ti == 0 else []
            xnT = slot_fm("D")

            if ti == 0:
                for _ in n0_gen:
                    pass

            xnT = slot_fm("D")
            UCM = slot_raw("A", 8192).rearrange("p (g s c) -> p g s c", g=64, s=8)
            Uml = slot_raw("B", 64 * KU).rearrange("p (g k) -> p g k", g=64)
            XS = slot_raw("C", 64 * KX).rearrange("p (r g k) -> p r g k", r=2, g=32)
            for h in range(2):
                wv, rw = load_w(w_in, h * 512, "w_in")
                for (c0, nc) in csubs:
                    for s in range(8):
                        bank, rb = getbank()

                        def mm(e, s=s, bank=bank, wv=wv, c0=c0, nc=nc):
                            ins = None
                            for kt in range(8):
                                ins = e.matmul(bank[0:nc, 0:512], lhsT=xnT[:, kt, 8 * c0 + s:8 * (c0 + nc):8],
                                               rhs=wv[:, kt, :], start=(kt == 0), stop=(kt == 7))
                            return ins
                        P.op("pe", mm, reads=RS["D"] + [rw], writes=[rb])
                        eng = evac_eng()
                        P.op(eng, copy_op(eng, UCM[0:nc, h * 32:(h + 1) * 32, s, :],
                                          bank[0:nc, 0:512].rearrange("p (g c) -> p g c", g=32)),
                             reads=[rb], writes=RS["A"], deps=first_deps)
                    for gb in range(4 * h, 4 * h + 4):
                        bank, rb = getbank()
                        bb = bank[:].bitcast(BF16)

                        def tr(e, gb=gb, bb=bb, nc=nc):
                            ins = None
                            for gi in range(8):
                                g = gb * 8 + gi
                                ins = e.transpose(bb[:, gi * 128:gi * 128 + nc],
                                                  in_=UCM[0:nc, g, :, :].rearrange("p s c -> p (s c)"),
                                                  identity=identb[0:nc, 0:nc])
                            return ins
                        P.op("pe", tr, reads=RS["A"] + [R_const], writes=[rb])
                        eng = evac_eng()
                        P.op(eng, copy_op(eng, Uml[:, gb * 8:(gb + 1) * 8, c0:c0 + nc],
                                          bb.rearrange("p (g k) -> p g k", g=8)[:, :, 0:nc]),
                             reads=[rb], writes=RS["B"], deps=first_deps)
            for q in range(32):
                bank, rb = getbank()

                def mm(e, q=q, bank=bank):
                    ins = None
                    for gh in range(2):
                        g = gh * 32 + q
                        for ri in range(2):
                            ins = e.matmul(bank[gh * 64:(gh + 1) * 64, ri * 256:ri * 256 + NCH],
                                           lhsT=W1[:, g, ri * 64:(ri + 1) * 64], rhs=Uml[:, g, 0:NCH],
                                           start=True, stop=True)
                    return ins
                P.op("pe", mm, reads=RS["B"] + [R_W1], writes=[rb])
                eng = evac_eng()
                P.op(eng, copy_op(eng, XS[:, :, q, 1:1 + NCH],
                                  bank[:, :].rearrange("p (r k) -> p r k", r=2)[:, :, 0:NCH]),
                     reads=[rb], writes=RS["C"] + [R_XSx], deps=first_deps)

            def proj_gen(src_w, col0, rhs_slot, evac, name):
                rhs = slot_fm(rhs_slot)
                for h in range(2):
                    wv, rw = load_w(src_w, col0 + h * 512, name)
                    for (n0, nn) in ntiles:
                        for f4 in range(4):
                            fo = h * 4 + f4
                            bank, rb = getbank()

                            def mm(e, bank=bank, wv=wv, f4=f4, n0=n0, nn=nn):
                                ins = None
                                for kt in range(8):
                                    ins = e.matmul(bank[:, 0:nn], lhsT=wv[:, kt, f4 * 128:(f4 + 1) * 128],
                                                   rhs=rhs[:, kt, n0:n0 + nn], start=(kt == 0), stop=(kt == 7))
                                return ins
                            P.op("pe", mm, reads=[rh(rhs_slot, n0), rw], writes=[rb])
                            evac(bank, rb, fo, n0, nn)
                            yield

            def proj(src_w, col0, rhs_slot, evac, name):
                for _ in proj_gen(src_w, col0, rhs_slot, evac, name):
                    pass

            gsB = slot_fm("E")

            def ev_gs(bank, rb, fo, n0, nn):
                P.op("act", lambda e: e.activation(out=gsB[:, fo, n0:n0 + nn], in_=bank[:, 0:nn], func=AF.Sigmoid,
                                                   bias=bgate[:, fo:fo + 1]),
                     reads=[rb, R_vec], writes=[rh("E", n0)])
            gs_gen = proj_gen(w_in, 4 * D, "D", ev_gs, "w_in")
            ysA = slot_fm("A")

            def ev_zs(bank, rb, fo, n0, nn):
                P.op("act", lambda e: e.activation(out=ysA[:, fo, n0:n0 + nn], in_=bank[:, 0:nn], func=AF.Silu),
                     reads=[rb], writes=[rh("A", n0)])
            zs_gen = proj_gen(w_in, 1 * D, "D", ev_zs, "w_in")

            if ti == 0:
                load_S0(S0f, R_S0f)
            hstep = nP // 2
            cur = state["sf"]
            P.op("act", lambda e, cur=cur: e.activation(out=XS[:, :, :, 0], in_=Sf[cur][:], func=AF.Copy),
                 reads=[RSF[cur]], writes=RS["C"])
            Xe = XS[:, :, :, 1:nP + 1:2]
            Xe_sw = XS[:, ::-1, :, 1:nP + 1:2]
            Xo = XS[:, :, :, 2:nP + 2:2]
            tA = slot_raw("A", 2 * 64 * hstep).bitcast(F32).rearrange("p (r g j) -> p r g j", r=2, g=32)
            tE = slot_raw("E", 2 * 64 * hstep).bitcast(F32).rearrange("p (r g j) -> p r g j", r=2, g=32)
            bch = lambda v, n_: v.unsqueeze(3).broadcast_to([128, 2, 32, n_])
            P.op("dve", lambda e: e.tensor_tensor(out=tA, in0=Xe, in1=bch(ArAr[:], hstep), op=ALU.mult),
                 reads=[R_XSx, R_A8], writes=RS["A"])
            P.op("dve", lambda e: e.tensor_tensor(out=tE, in0=Xe_sw, in1=bch(AiPM[:], hstep), op=ALU.mult),
                 reads=[R_XSx, R_A8], writes=RS["E"])
            P.op("dve", lambda e: e.tensor_tensor(out=tA, in0=tA, in1=tE, op=ALU.add),
                 reads=RS["E"], writes=RS["A"])
            P.op("dve", lambda e: e.tensor_tensor(out=Xo, in0=tA, in1=Xo, op=ALU.add),
                 reads=RS["A"], writes=RS["C"] + [R_XSx])
            for j in range(hstep):
                if next(gs_gen, "done") == "done":
                    next(zs_gen, None)
                nx = 1 - cur
                col = 2 * j + 2
                P.op("dve", lambda e, cur=cur: e.tensor_tensor(out=st1[:], in0=Sf[cur][:], in1=ArAr2[:], op=ALU.mult),
                     reads=[RSF[cur], R_A8], writes=[R_st1])
                P.op("dve", lambda e, cur=cur: e.tensor_tensor(out=st2[:], in0=Sf[cur][:, ::-1, :], in1=AiPM2[:],
                                                               op=ALU.mult),
                     reads=[RSF[cur], R_A8], writes=[R_st2])
                P.op("dve", lambda e, col=col: e.tensor_tensor(out=st1[:], in0=st1[:], in1=XS[:, :, :, col], op=ALU.add),
                     reads=[R_st1, R_XSx], writes=[R_st1])
                P.op("dve", lambda e, nx=nx: e.tensor_tensor(out=Sf[nx][:], in0=st1[:], in1=st2[:], op=ALU.add),
                     reads=[R_st1, R_st2], writes=[RSF[nx]])
                P.op("act", lambda e, nx=nx, col=col: e.activation(out=XS[:, :, :, col], in_=Sf[nx][:], func=AF.Copy),
                     reads=[RSF[nx]], writes=RS["C"])
                cur = nx
            p1, p2 = nxt("xt", NXT), nxt("xt", NXT)
            for jb0 in range(0, hstep, 16):
                n_ = min(16, hstep - jb0)
                w1 = xt[p1][:, 0:64 * n_].rearrange("p (r g j) -> p r g j", r=2, g=32)
                w2 = xt[p2][:, 0:64 * n_].rearrange("p (r g j) -> p r g j", r=2, g=32)
                so = XS[:, :, :, 2 * jb0:2 * (jb0 + n_):2]
                so_sw = XS[:, ::-1, :, 2 * jb0:2 * (jb0 + n_):2]
                xe = XS[:, :, :, 2 * jb0 + 1:2 * (jb0 + n_) + 1:2]
                P.op("dve", lambda e, w1=w1, so=so, n_=n_: e.tensor_tensor(out=w1, in0=so, in1=bch(ArAr[:], n_),
                                                                          op=ALU.mult),
                     reads=RS["C"] + [R_A8], writes=[RXT[p1]])
                P.op("dve", lambda e, w2=w2, so_sw=so_sw, n_=n_: e.tensor_tensor(out=w2, in0=so_sw,
                                                                                in1=bch(AiPM[:], n_), op=ALU.mult),
                     reads=RS["C"] + [R_A8], writes=[RXT[p2]])
                P.op("dve", lambda e, w1=w1, w2=w2: e.tensor_tensor(out=w1, in0=w1, in1=w2, op=ALU.add),
                     reads=[RXT[p2]], writes=[RXT[p1]])
                P.op("dve", lambda e, w1=w1, xe=xe: e.tensor_tensor(out=xe, in0=w1, in1=xe, op=ALU.add),
                     reads=[RXT[p1]], writes=RS["C"])
            state["sf"] = cur
            for _ in gs_gen:
                pass
            for _ in zs_gen:
                pass
            if ti == LAST:
                io0 = nxt("xt", NXT)
                for gh in range(2):
                    bank, rb = getbank()
                    ps_ = slice(gh * 64, (gh + 1) * 64)

                    def trp(e, bank=bank, cur=cur, ps_=ps_):
                        ins = None
                        for ri in range(2):
                            ins = e.transpose(bank[0:32, ri * 64:(ri + 1) * 64], in_=Sf[cur][ps_, ri, :],
                                              identity=identf[ps_, ps_])
                        return ins
                    P.op("pe", trp, reads=[RSF[cur], R_const], writes=[rb])
                    P.op("dve", lambda e, bank=bank, io0=io0, gh=gh: e.tensor_copy(
                        out=xt[io0][0:32, gh * 128:(gh + 1) * 128], in_=bank[0:32, 0:128]),
                        reads=[rb], writes=[RXT[io0]])
                for ri, dst in enumerate((nre_p, nim_p)):
                    for gh in range(2):
                        idx = gh * 2 + ri
                        P.dma("sp", lambda e, dst=dst, gh=gh, idx=idx, io0=io0: e.dma_start(
                            out=dst[gh * 32:(gh + 1) * 32, :], in_=xt[io0][0:32, idx * 64:(idx + 1) * 64]),
                            reads=[RXT[io0]])
                i1, i2, i3 = nxt("xt", NXT), nxt("xt", NXT), nxt("xt", NXT)
                v1 = xt[i1][:].rearrange("p (r g b) -> p r g b", r=2, g=32)
                v2 = xt[i2][:].rearrange("p (r g b) -> p r g b", r=2, g=32)
                v3p = xt[i3][:].rearrange("p (r b g) -> p r b g", r=2, b=16)
                v3 = v3p.rearrange("p r b g -> p r g b")
                bcb = lambda v: v.unsqueeze(3).broadcast_to([128, 2, 32, 16])
                P.op("dve", lambda e: e.tensor_tensor(out=v1, in0=S0f[:], in1=bcb(ArAr[:]), op=ALU.mult),
                     reads=[R_S0f, R_A8], writes=[RXT[i1]])
                P.op("dve", lambda e: e.tensor_tensor(out=v2, in0=S0f[:, ::-1, :, :], in1=bcb(AiPM[:]), op=ALU.mult),
                     reads=[R_S0f, R_A8], writes=[RXT[i2]])
                P.op("dve", lambda e: e.tensor_tensor(out=v1, in0=v1, in1=v2, op=ALU.add),
                     reads=[RXT[i1], RXT[i2]], writes=[RXT[i1]])
                P.op("dve", lambda e: e.tensor_tensor(out=v3, in0=v1, in1=XS[:, :, :, nP + 1:nP + 17], op=ALU.add),
                     reads=[RXT[i1]] + RS["C"], writes=[RXT[i3]])
                io1 = nxt("xt", NXT)
                for gh in range(2):
                    bank, rb = getbank()
                    ps_ = slice(gh * 64, (gh + 1) * 64)

                    def trs(e, bank=bank, ps_=ps_):
                        ins = None
                        for j in range(8):
                            ri, b4 = divmod(j, 4)
                            ins = e.transpose(bank[:, j * 64:(j + 1) * 64],
                                              in_=v3p[ps_, ri, b4 * 4:(b4 + 1) * 4, :].rearrange("p b g -> p (b g)"),
                                              identity=identf[ps_, ps_])
                        return ins
                    P.op("pe", trs, reads=[RXT[i3], R_const], writes=[rb])
                    eng = evac_eng()
                    P.op(eng, copy_op(eng, xt[io1][:, gh * 512:(gh + 1) * 512], bank[:, :]),
                         reads=[rb], writes=[RXT[io1]])
                for gh in range(2):
                    for j in range(8):
                        ri, b4 = divmod(j, 4)
                        idx = gh * 8 + j
                        dst = (nre_s, nim_s)[ri]
                        for bb in range(4):
                            P.dma("sp", lambda e, dst=dst, gh=gh, b4=b4, bb=bb, idx=idx, io1=io1: e.dma_start(
                                out=dst[b4 * 4 + bb, gh * 32:(gh + 1) * 32, :],
                                in_=xt[io1][bb * 32:(bb + 1) * 32, idx * 64:(idx + 1) * 64]), reads=[RXT[io1]])
                P.op("act", lambda e: e.activation(out=XS[:, :, :, nP:nP + 16], in_=S0f[:], func=AF.Copy),
                     reads=[R_S0f], writes=RS["C"])

            ygT = slot_fm("B")

            def stA(gb, c0, nc):
                ymls = []
                for half in range(2):
                    bank, rb = getbank()

                    def mm(e, half=half, bank=bank):
                        ins = None
                        for gi in range(4):
                            g = gb * 8 + half * 4 + gi
                            gh, g32 = divmod(g, 32)
                            ps_ = slice(gh * 64, (gh + 1) * 64)
                            o = bank[:, gi * 128:gi * 128 + nc]
                            e.matmul(o, lhsT=TM[:, g, :], rhs=Uml[:, g, c0:c0 + nc], start=True, stop=False)
                            e.matmul(o, lhsT=W3[ps_, 0, g32, :], rhs=XS[ps_, 0, g32, c0:c0 + nc],
                                     start=False, stop=False)
                            ins = e.matmul(o, lhsT=W3[ps_, 1, g32, :], rhs=XS[ps_, 1, g32, c0:c0 + nc],
                                           start=False, stop=True)
                        return ins
                    P.op("pe", mm, reads=RS["B"] + RS["C"] + [R_TM, R_W3], writes=[rb])
                    iy = nxt("yml", NYML)
                    eng = evac_eng()
                    P.op(eng, copy_op(eng, yml[iy][:, :, 0:nc],
                                      bank[:, :].rearrange("p (g k) -> p g k", g=4)[:, :, 0:nc]),
                         reads=[rb], writes=[RYML[iy]])
                    ymls.append(iy)
                return ymls

            def stB(ic, ymls, nc):
                bank, rb = getbank()
                bb = bank[:].bitcast(BF16)

                def tr(e):
                    ins = None
                    for half in range(2):
                        for gi in range(4):
                            q0 = (half * 4 + gi) * 128
                            ins = e.transpose(bb[0:nc, q0:q0 + 128], in_=yml[ymls[half]][:, gi, 0:nc],
                                              identity=identb[:, :])
                    return ins
                P.op("pe", tr, reads=[RYML[ymls[0]], RYML[ymls[1]], R_const], writes=[rb])
                P.op("act", lambda e: e.activation(
                    out=ycm[0:nc, ic, :].rearrange("p (j g c) -> p g j c", j=8, g=8),
                    in_=bb[0:nc, :].rearrange("p (g j c) -> p g j c", g=8, j=8), func=AF.Gelu_apprx_tanh),
                    reads=[rb], writes=[RYCM[ic]])

            def stC(ic, gb, c0, nc):
                bank2, rb2 = getbank()
                bb2 = bank2[:].bitcast(BF16)

                def tr2(e):
                    ins = None
                    for j in range(8):
                        ins = e.transpose(bb2[:, j * 128:j * 128 + nc], in_=ycm[0:nc, ic, j * 128:(j + 1) * 128],
                                          identity=identb[0:nc, 0:nc])
                    return ins
                P.op("pe", tr2, reads=[RYCM[ic], R_const], writes=[rb2])
                eng = evac_eng()
                P.op(eng, copy_op(eng, ygT[:, gb, 8 * c0:8 * (c0 + nc)].rearrange("p (k j) -> p j k", j=8),
                                  bb2.rearrange("p (j k) -> p j k", j=8)[:, :, 0:nc]),
                     reads=[rb2], writes=RS["B"])

            items = [(gb, c0, nc) for gb in range(8) for (c0, nc) in csubs]
            ymls_of = {}
            for it_ in range(len(items) + 2):
                if it_ < len(items):
                    ymls_of[it_] = stA(*items[it_])
                if 0 <= it_ - 1 < len(items):
                    stB((it_ - 1) % 2, ymls_of[it_ - 1], items[it_ - 1][2])
                if 0 <= it_ - 2 < len(items):
                    stC((it_ - 2) % 2, *items[it_ - 2])

            def ev_glu(bank, rb, fo, n0, nn):
                it = nxt("tmp", NTMP)
                P.op("act", lambda e: e.activation(out=tmpb[it][:, 0:nn], in_=bank[:, 0:nn], func=AF.Sigmoid,
                                                   bias=bglu[:, fo:fo + 1]),
                     reads=[rb, R_vec], writes=[RTMP[it]])
                P.op("dve", lambda e: e.tensor_tensor(out=ysA[:, fo, n0:n0 + nn], in0=ysA[:, fo, n0:n0 + nn],
                                                      in1=tmpb[it][:, 0:nn], op=ALU.mult),
                     reads=[RTMP[it]], writes=[rh("A", n0)])
                P.op("dve", lambda e: e.tensor_tensor(out=ysA[:, fo, n0:n0 + nn], in0=ysA[:, fo, n0:n0 + nn],
                                                      in1=ygT[:, fo, n0:n0 + nn], op=ALU.mult),
                     reads=[rh("B", n0)], writes=[rh("A", n0)])
            proj(w_glu, 0, "B", ev_glu, "w_glu")

            def ev_bs(bank, rb, fo, n0, nn):
                P.op("dve", lambda e: e.tensor_tensor(out=gsB[:, fo, n0:n0 + nn], in0=bank[:, 0:nn],
                                                      in1=gsB[:, fo, n0:n0 + nn], op=ALU.mult),
                     reads=[rb], writes=[rh("E", n0)])
            proj(w_bs, 0, "A", ev_bs, "w_bs")

            RCX = [Res("cx%d" % i) for i in range(4)]
            Lp = 15 + NTp
            extp = slot_raw("C", 8 * Lp).rearrange("p (a l) -> p a l", a=8)
            if ti == LAST:
                exts = arena[:, SL["B"] + 4224:SL["B"] + 4224 + 8 * 16 * 23].rearrange(
                    "p (a b l) -> p a b l", a=8, b=16)
                for half in range(2):
                    ix = nxt("xt", NXT)
                    P.dma("sp", lambda e, ix=ix, half=half: e.dma_start(
                        out=xt[ix][0:120, :],
                        in_=spool[half * 8:(half + 1) * 8].rearrange("b r f -> (b r) f")), writes=[RXT[ix]])
                    for q in range(2):
                        bank, rb = getbank()

                        def tr(e, ix=ix, q=q, bank=bank):
                            ins = None
                            for f4 in range(4):
                                ft = q * 4 + f4
                                ins = e.transpose(bank[:, f4 * 128:f4 * 128 + 120],
                                                  in_=xt[ix][0:120, ft * 128:(ft + 1) * 128],
                                                  identity=identf[0:120, 0:120])
                            return ins
                        P.op("pe", tr, reads=[RXT[ix], R_const], writes=[rb])
                        eng = evac_eng()
                        P.op(eng, copy_op(
                            eng, bufT[:, q * 4:(q + 1) * 4, half * 8:(half + 1) * 8, :].rearrange("p a b r -> p a (b r)"),
                            bank[:, :].rearrange("p (a t) -> p a t", a=4)[:, :, 0:120]),
                            reads=[rb], writes=RS["B"])
                P.op("dve", lambda e: e.tensor_copy(out=exts[:, :, :, 0:15], in_=bufT[:]),
                     reads=[], writes=RS["B"] + RCX)
                P.dma("sp", lambda e: e.dma_start(out=npool_s[:, 0:7, :], in_=spool[:, 8:15, :]))
            P.op("dve", lambda e: e.tensor_copy(out=extp[:, :, 0:15], in_=carry[:]),
                 reads=[R_carry], writes=RS["C"] + RCX)

            def ev_up(bank, rb, fo, n0, nn):
                wr = [RCX[fo // 2]]
                if n0 + nn <= NTp:
                    eng = evac_eng()
                    P.op(eng, copy_op(eng, extp[:, fo, 15 + n0:15 + n0 + nn], bank[:, 0:nn]),
                         reads=[rb], writes=wr)
                else:
                    assert ti == LAST and n0 == 1024 and nn == 144 and NTp == 1040
                    P.op("act", lambda e: e.activation(out=extp[:, fo, 15 + n0:15 + n0 + 16], in_=bank[:, 0:16],
                                                       func=AF.Copy), reads=[rb], writes=wr)
                    P.op("dve", lambda e: e.tensor_copy(
                        out=exts[:, fo, :, 15:23], in_=bank[:, 16:144].rearrange("p (b t) -> p b t", b=16)),
                        reads=[rb], writes=wr + RS["B"])
                    P.op("act", lambda e: e.activation(out=UPF[:, fo, :], in_=bank[:, 0:144], func=AF.Copy),
                         reads=[rb], writes=[R_UPF])
            up_gen = proj_gen(w_in, 2 * D, "D", ev_up, "w_in")

            pooledA = slot_fm("A")
            T1o, T2o = (SL["B"] + 5184, SL["B"] + 7296) if ti != LAST else (SL["B"], SL["B"] + 2112)

            def pool_group(ext3, L, rows, gi, out3, cnt_fix, fin_views=None):
                w = 2 ** (gi + 1)
                t1 = arena[:, T1o:T1o + rows * L].rearrange("p (a l) -> p a l", a=rows)
                t2 = arena[:, T2o:T2o + rows * L].rearrange("p (a l) -> p a l", a=rows)

                def lvl(dst, src, lo, d):
                    P.op("dve", lambda e: e.tensor_tensor(out=dst[:, :, lo:L], in0=src[:, :, lo:L],
                                                          in1=src[:, :, lo - d:L - d], op=ALU.add),
                         reads=[RCX[gi]] + RS["B"], writes=RS["B"])
                lvl(t1, ext3, 1, 1)
                fin = t1
                if w >= 4:
                    lvl(t2, t1, 3, 2)
                    fin = t2
                if w >= 8:
                    lvl(t1, t2, 7, 4)
                    fin = t1
                if w >= 16:
                    lvl(t2, t1, 15, 8)
                    fin = t2
                if cnt_fix:
                    P.op("dve", lambda e: e.tensor_tensor(
                        out=fin[:, :, 15:31], in0=fin[:, :, 15:31],
                        in1=invc[:, gi, :].unsqueeze(1).broadcast_to([128, rows, 16]), op=ALU.mult),
                        reads=RS["B"] + [R_const], writes=RS["B"])
                if fin_views is None:
                    o_, a_, b_ = out3, fin[:, :, 15:L], ext3[:, :, 15:L]
                else:
                    o_, a_, b_ = fin_views(fin)
                P.op("dve", lambda e: e.scalar_tensor_tensor(
                    out=o_, in0=a_, scalar=1.0 / w, in1=b_,
                    op0=ALU.mult, op1=ALU.subtract), reads=[RCX[gi]] + RS["C"] + RS["B"], writes=RS["A"])

            def pool_gi(gi):
                fs = slice(2 * gi, 2 * gi + 2)
                pool_group(extp[:, fs, :], Lp, 2, gi, pooledA[:, fs, 0:NTp], ti == 0)
                if ti == LAST:
                    pool_group(exts[:, fs, :, :].rearrange("p a b l -> p (a b) l"), 23, 32, gi, None, False,
                               fin_views=lambda fin, fs=fs: (
                                   pooledA[:, fs, NTp:NTp + 128].rearrange("p a (b t) -> p a b t", b=16),
                                   fin.rearrange("p (a b) l -> p a b l", a=2)[:, :, :, 15:23],
                                   exts[:, fs, :, 15:23]))

            ypE = slot_fm("B")

            def pm(gis):
                for (n0, nn) in ntiles:
                    for gi in gis:
                        for fo2 in range(2):
                            fo = 2 * gi + fo2
                            bank, rb = getbank()

                            def mm(e, bank=bank, gi=gi, fo2=fo2, n0=n0, nn=nn):
                                ins = None
                                for k2 in range(2):
                                    ins = e.matmul(bank[:, 0:nn], lhsT=pmw[:, gi, k2, fo2 * 128:(fo2 + 1) * 128],
                                                   rhs=pooledA[:, 2 * gi + k2, n0:n0 + nn],
                                                   start=(k2 == 0), stop=(k2 == 1))
                                return ins
                            P.op("pe", mm, reads=[rh("A", n0), R_pmw], writes=[rb])
                            P.op("act", lambda e, bank=bank, fo=fo, n0=n0, nn=nn: e.activation(
                                out=ypE[:, fo, n0:n0 + nn], in_=bank[:, 0:nn], func=AF.Copy,
                                scale=pscale[:, fo:fo + 1]), reads=[rb, R_vec], writes=[rh("B", n0)])

            nb_blk = 4 * len(ntiles)
            for _ in range(nb_blk):
                next(up_gen)
            pool_gi(0)
            pool_gi(1)
            for _ in up_gen:
                pass
            if ti != LAST:
                P.op("dve", lambda e: e.tensor_copy(out=carry[:], in_=extp[:, :, NTp:NTp + 15]),
                     reads=RS["C"] + RCX, writes=[R_carry])
                pm([0, 1])
                pool_gi(2)
                pool_gi(3)
                pm([2, 3])
            else:
                pool_gi(2)
                pool_gi(3)
                pm([0, 1, 2, 3])

            def ev_zp(bank, rb, fo, n0, nn):
                it = nxt("tmp", NTMP)
                P.op("act", lambda e: e.activation(out=tmpb[it][:, 0:nn], in_=bank[:, 0:nn], func=AF.Silu),
                     reads=[rb], writes=[RTMP[it]])
                P.op("dve", lambda e: e.tensor_tensor(out=ypE[:, fo, n0:n0 + nn], in0=ypE[:, fo, n0:n0 + nn],
                                                      in1=tmpb[it][:, 0:nn], op=ALU.mult),
                     reads=[RTMP[it]], writes=[rh("B", n0)])
            proj(w_in, 3 * D, "D", ev_zp, "w_in")

            gpA = slot_fm("A")

            def ev_gp(bank, rb, fo, n0, nn):
                P.op("act", lambda e: e.activation(out=gpA[:, fo, n0:n0 + nn], in_=bank[:, 0:nn], func=AF.Sigmoid,
                                                   bias=bgate[:, 8 + fo:9 + fo]),
                     reads=[rb, R_vec], writes=[rh("A", n0)])
            proj(w_in, 5 * D, "D", ev_gp, "w_in")

            def ev_bp(bank, rb, fo, n0, nn):
                it = nxt("tmp", NTMP)
                P.op("dve", lambda e: e.tensor_tensor(out=tmpb[it][:, 0:nn], in0=bank[:, 0:nn],
                                                      in1=gpA[:, fo, n0:n0 + nn], op=ALU.mult),
                     reads=[rb, rh("A", n0)], writes=[RTMP[it]])
                P.op("dve", lambda e: e.tensor_tensor(out=gsB[:, fo, n0:n0 + nn], in0=gsB[:, fo, n0:n0 + nn],
                                                      in1=tmpb[it][:, 0:nn], op=ALU.add),
                     reads=[RTMP[it]], writes=[rh("E", n0)])
            bp_gen = proj_gen(w_bp, 0, "B", ev_bp, "w_bp")
            if ti + 1 < len(TILES):
                n_gen = phaseN_gen(ti + 1, TILES[ti + 1][0], TILES[ti + 1][1], [])
            else:
                n_gen = iter(())
            for ib, _ in enumerate(bp_gen):
                if ib % 2 == 1:
                    next(n_gen, None)
            for _ in n_gen:
                pass

            if ti == LAST:
                for q in range(2):
                    bank, rb = getbank()

                    def tr(e, q=q, bank=bank):
                        ins = None
                        for f4 in range(4):
                            ft = q * 4 + f4
                            ins = e.transpose(bank[:, f4 * 128:(f4 + 1) * 128], in_=UPF[:, ft, 16:144],
                                              identity=identf[:, :])
                        return ins
                    P.op("pe", tr, reads=[R_UPF, R_const], writes=[rb])
                    ix = nxt("xt", NXT) if q == 0 else ix
                    eng = evac_eng()
                    P.op(eng, copy_op(eng, xt[ix][:, q * 512:(q + 1) * 512], bank[:, :]),
                         reads=[rb], writes=[RXT[ix]])
                for b in range(16):
                    P.dma("sp", lambda e, b=b, ix=ix: e.dma_start(out=npool_s[b, 7:15, :], in_=xt[ix][b * 8:(b + 1) * 8, :]),
                          reads=[RXT[ix]])
                for q in range(2):
                    bank, rb = getbank()

                    def tr(e, q=q, bank=bank):
                        ins = None
                        for f4 in range(4):
                            ft = q * 4 + f4
                            ins = e.transpose(bank[0:16, f4 * 128:(f4 + 1) * 128], in_=UPF[:, ft, 0:16],
                                              identity=identf[:, :])
                        return ins
                    P.op("pe", tr, reads=[R_UPF, R_const], writes=[rb])
                    ix2 = nxt("xt", NXT) if q == 0 else ix2
                    eng = evac_eng()
                    P.op(eng, copy_op(eng, xt[ix2][0:16, q * 512:(q + 1) * 512], bank[0:16, :]),
                         reads=[rb], writes=[RXT[ix2]])
                P.dma("sp", lambda e, ix2=ix2: e.dma_start(out=npool_p[:, :], in_=xt[ix2][1:16, :]), reads=[RXT[ix2]])

            wv0, rw0 = load_w(w_out, 0, "w_out")
            wv1, rw1 = load_w(w_out, 512, "w_out")
            wvs = [(wv0, rw0), (wv1, rw1)]
            mB = slot_fm("E")
            if ti == 0:
                rowt = [(16 + 128 * i, min(128, 1024 - 16 - 128 * i)) for i in range(8)]
            else:
                rowt = [(1024 + 128 * i, 128) for i in range(8)] + [(2048, 16), (2064, 128)]
            for (tok0, rows) in rowt:
                c0 = tok0 - T0
                ix = nxt("xt", NXT)
                P.dma("sp", lambda e, ix=ix, tok0=tok0, rows=rows: e.dma_start(
                    out=xt[ix][0:rows, :], in_=xall[tok0:tok0 + rows, :]), writes=[RXT[ix]])
                for h in range(2):
                    bank, rb = getbank()
                    wv, rw = wvs[h]

                    def mm(e, bank=bank, wv=wv, c0=c0, rows=rows):
                        ins = None
                        for kt in range(8):
                            ins = e.matmul(bank[0:rows, 0:512], lhsT=mB[:, kt, c0:c0 + rows], rhs=wv[:, kt, :],
                                           start=(kt == 0), stop=(kt == 7))
                        return ins
                    P.op("pe", mm, reads=[rh("E", c0), rh("E", c0 + rows - 1), rw], writes=[rb])
                    P.op("dve", lambda e, bank=bank, ix=ix, h=h, rows=rows: e.tensor_tensor(
                        out=xt[ix][0:rows, h * 512:(h + 1) * 512], in0=bank[0:rows, 0:512],
                        in1=xt[ix][0:rows, h * 512:(h + 1) * 512], op=ALU.add), reads=[rb], writes=[RXT[ix]])
                si = nxt("stat", 4)
                jq = nxt("xn", 2)
                P.op("act", lambda e, ix=ix, rows=rows, si=si, jq=jq: e.activation(
                    out=xn[jq][0:rows, :], in_=xt[ix][0:rows, :], func=AF.Square,
                    accum_out=stat[0:rows, 2 * si:2 * si + 1]), reads=[RXT[ix]], writes=[RXN[jq], RSTAT[si]])
                P.op("act", lambda e, rows=rows, si=si: e.activation(
                    out=stat[0:rows, 2 * si + 1:2 * si + 2], in_=stat[0:rows, 2 * si:2 * si + 1],
                    func=AF.Sqrt, scale=1.0 / D, bias=EPS), reads=[RSTAT[si]], writes=[RSTAT[si]])
                P.op("dve", lambda e, rows=rows, si=si: e.reciprocal(
                    out=stat[0:rows, 2 * si:2 * si + 1], in_=stat[0:rows, 2 * si + 1:2 * si + 2]),
                    reads=[RSTAT[si]], writes=[RSTAT[si]])
                P.op("dve", lambda e, ix=ix, rows=rows, si=si: e.scalar_tensor_tensor(
                    out=xt[ix][0:rows, :], in0=xt[ix][0:rows, :], scalar=stat[0:rows, 2 * si:2 * si + 1],
                    in1=fB[0:rows, :], op0=ALU.mult, op1=ALU.mult),
                    reads=[RSTAT[si], R_fB], writes=[RXT[ix]])
                if tok0 < NPROMPT:
                    dst = y_p[tok0 - 16:tok0 - 16 + rows, :]
                else:
                    dst = y_s[tok0 - NPROMPT:tok0 - NPROMPT + rows, :]
                P.dma("sp", lambda e, ix=ix, rows=rows, dst=dst: e.dma_start(out=dst, in_=xt[ix][0:rows, :]),
                      reads=[RXT[ix]])

        for ti_, (T0_, NT_) in enumerate(TILES):
            do_tile(ti_, T0_, NT_)
        P.emit()
    return nc


_CACHE = {}


def _consts():
    ident = np.eye(128, dtype=np.float32)
    s_idx = np.arange(128) // 16
    mask = (s_idx[None, :] >= s_idx[:, None]).astype(np.float32)
    nvals = np.broadcast_to(np.arange(1, 9, dtype=np.float32)[None, None, :], (128, 32, 8)).copy()
    invc = np.zeros((128, 4, 16), np.float32)
    for gi in range(4):
        w = 2 ** (gi + 1)
        for pos in range(16):
            invc[:, gi, pos] = w / min(pos + 1, w)
    return ident, mask, nvals, invc


def kernel(x_prompt, x_sample, state_ssm_re, state_ssm_im, state_pool, meta_tokens,
           norm_gain, w_in, b_gate, ssm_a_re, ssm_a_im, ssm_log_dt, ssm_b_re, ssm_b_im,
           ssm_c_re, ssm_c_im, ssm_d, w_glu, b_glu, pool_mix, pool_scale,
           w_branch_ssm, w_branch_pool, w_out, final_norm_gain):
    f = lambda a: np.ascontiguousarray(np.asarray(a, dtype=np.float32))
    x_prompt, x_sample = f(x_prompt), f(x_sample)
    meta = f(meta_tokens)
    if "nc" not in _CACHE:
        _CACHE["nc"] = build_program()
    nc = _CACHE["nc"]
    ident, mask, nvals, invc = _consts()
    shared = {
        "w_in": f(w_in[0]), "w_glu": f(w_glu[0]), "pool_mix": f(pool_mix[0]), "w_bs": f(w_branch_ssm[0]),
        "w_bp": f(w_branch_pool[0]), "w_out": f(w_out[0]), "norm_gain": f(norm_gain[0]), "b_gate": f(b_gate[0]),
        "ssm_d": f(ssm_d[0]), "b_glu": f(b_glu[0]), "pool_scale": f(pool_scale[0]), "fgain": f(final_norm_gain),
        "a_re": f(ssm_a_re[0]), "a_im": f(ssm_a_im[0]), "log_dt": f(ssm_log_dt[0]),
        "b_re": f(ssm_b_re[0]), "b_im": f(ssm_b_im[0]), "c_re": f(ssm_c_re[0]), "c_im": f(ssm_c_im[0]),
        "c_ident": ident, "c_mask": mask, "c_nvals": nvals, "c_invc": invc,
    }
    in_maps = []
    for c in range(NCORES):
        m = dict(shared)
        m["xall"] = np.ascontiguousarray(np.concatenate(
            [meta, x_prompt[c], x_sample[16 * c:16 * (c + 1)].reshape(128, D)], axis=0))
        m["s0re"] = f(state_ssm_re[0, 16 * c:16 * (c + 1)])
        m["s0im"] = f(state_ssm_im[0, 16 * c:16 * (c + 1)])
        m["spool"] = f(state_pool[0, 16 * c:16 * (c + 1)])
        in_maps.append(m)
    res = run_bass_kernel_spmd(nc, in_maps, core_ids=list(range(NCORES)))
    R = res.results
    y_prompt = np.stack([R[c]["y_p"] for c in range(NCORES)], axis=0)
    y_sample = np.concatenate([R[c]["y_s"].reshape(16, 8, D) for c in range(NCORES)], axis=0)
    nre_p = np.stack([R[c]["nre_p"] for c in range(NCORES)], axis=0)[None]
    nim_p = np.stack([R[c]["nim_p"] for c in range(NCORES)], axis=0)[None]
    npool_p = np.stack([R[c]["npool_p"] for c in range(NCORES)], axis=0)[None]
    nre_s = np.concatenate([R[c]["nre_s"] for c in range(NCORES)], axis=0)[None]
    nim_s = np.concatenate([R[c]["nim_s"] for c in range(NCORES)], axis=0)[None]
    npool_s = np.concatenate([R[c]["npool_s"] for c in range(NCORES)], axis=0)[None]
    return (y_prompt.astype(np.float32), y_sample.astype(np.float32), nre_p.astype(np.float32),
            nim_p.astype(np.float32), npool_p.astype(np.float32), nre_s.astype(np.float32),
            nim_s.astype(np.float32), npool_s.astype(np.float32))
```

```python
import contextlib
import math
import numpy as np
import concourse.bass as bass
import concourse.mybir as mybir
from concourse.bass_utils import run_bass_kernel_spmd

F32 = mybir.dt.float32
BF16 = mybir.dt.bfloat16
I32 = mybir.dt.int32
ALU = mybir.AluOpType
AF = mybir.ActivationFunctionType

NCORES = 8
D = 1024
NPROMPT = 2064
NTOT = 2192
SLOT = 9408
NTM = 1168
KU = 146
KX = 147
TILES = [(0, 1024), (1024, 1168)]
LAST = len(TILES) - 1
EPS = 1e-6


class Res:
    __slots__ = ("w", "r", "name", "excl")

    def __init__(self, name="", excl=False):
        self.w = None
        self.r = []
        self.name = name
        self.excl = excl


class Op:
    __slots__ = ("eng", "fn", "deps", "pos", "sigidx", "is_dma", "sem", "semval", "needed")

    def __init__(self, eng, fn, is_dma):
        self.eng = eng
        self.fn = fn
        self.deps = []
        self.pos = -1
        self.sigidx = None
        self.is_dma = is_dma
        self.sem = None
        self.semval = None
        self.needed = False


HAZ = 2


class Prog:
    ENGS = ["pe", "act", "dve", "pool", "sp"]

    def __init__(self, nc, n_dma_sems=14):
        self.nc = nc
        self.q = {e: [] for e in self.ENGS}
        self.n_dma_sems = n_dma_sems

    def _mk(self, eng, fn, reads, writes, deps, is_dma):
        op = Op(eng, fn, is_dma)
        ds = list(deps)
        if any(r.excl for r in reads):
            writes = list(writes) + [r for r in reads if r.excl]
            reads = [r for r in reads if not r.excl]
        for r in reads:
            if r.w is not None:
                ds.append(r.w)
        for w in writes:
            if w.w is not None:
                ds.append(w.w)
            ds.extend(w.r)
        best = {}
        dmas = []
        seen = set()
        for d in ds:
            if d is None or id(d) in seen:
                continue
            seen.add(id(d))
            if d.is_dma:
                dmas.append(d)
            else:
                b = best.get(d.eng)
                if b is None or d.pos > b.pos:
                    best[d.eng] = d
        op.deps = dmas + list(best.values())
        for r in reads:
            r.r.append(op)
            if len(r.r) > 64:
                keep = {}
                kd = []
                for x in r.r:
                    if x.is_dma:
                        kd.append(x)
                    elif x.eng not in keep or x.pos > keep[x.eng].pos:
                        keep[x.eng] = x
                r.r = kd + list(keep.values())
        for w in writes:
            w.w = op
            w.r = []
        op.pos = len(self.q[eng])
        self.q[eng].append(op)
        return op

    def op(self, eng, fn, reads=(), writes=(), deps=()):
        return self._mk(eng, fn, reads, writes, deps, False)

    def dma(self, eng, fn, reads=(), writes=(), deps=()):
        return self._mk(eng, fn, reads, writes, deps, True)

    def emit(self):
        nc = self.nc
        for e in self.ENGS:
            for op in self.q[e]:
                for d in op.deps:
                    if d.is_dma or d.eng != op.eng:
                        d.needed = True
                    elif d.eng != "pe" and (op.pos - d.pos) <= HAZ:
                        d.needed = True
        for e in self.ENGS:
            c = 0
            for op in self.q[e]:
                if (not op.is_dma) and op.needed:
                    c += 1
                    op.sigidx = c
        with contextlib.ExitStack() as st:
            esem = {e: st.enter_context(nc.semaphore("s_" + e)) for e in self.ENGS}
            dsems = {}
            for e in self.ENGS:
                if any(o.is_dma for o in self.q[e]):
                    dsems[e] = [st.enter_context(nc.semaphore("d_%s_%d" % (e, i)))
                                for i in range(self.n_dma_sems)]
            for e, sems in dsems.items():
                cnt = [0] * len(sems)
                prev = [None] * len(sems)
                i = 0
                for op in self.q[e]:
                    if op.is_dma:
                        k = i % len(sems)
                        cnt[k] += 1
                        op.sem = sems[k]
                        op.semval = 16 * cnt[k]
                        if prev[k] is not None:
                            op.deps.append(prev[k])
                        prev[k] = op
                        i += 1
            block = st.enter_context(nc.Block())
            handles = {"pe": block.tensor, "act": block.scalar, "dve": block.vector,
                       "pool": block.gpsimd, "sp": block.sync}

            def mk(e):
                ops = self.q[e]

                def body(eng):
                    waited = {}
                    for op in ops:
                        for d in op.deps:
                            if d.is_dma:
                                key = ("d", d.sem.name)
                                val = d.semval
                                sem = d.sem
                            else:
                                if d.eng == op.eng:
                                    if d.eng == "pe" or (op.pos - d.pos) > HAZ:
                                        continue
                                key = ("e", d.eng)
                                val = d.sigidx
                                sem = esem[d.eng]
                            if waited.get(key, 0) >= val:
                                continue
                            waited[key] = val
                            eng.wait_ge(sem, val)
                        ins = op.fn(eng)
                        if op.is_dma:
                            ins.then_inc(op.sem, 16)
                        elif op.needed:
                            ins.then_inc(esem[op.eng], 1)
                    if e in dsems:
                        last = {}
                        for op in ops:
                            if op.is_dma:
                                last[op.sem.name] = op
                        for op in last.values():
                            if waited.get(("d", op.sem.name), 0) < op.semval:
                                eng.wait_ge(op.sem, op.semval)
                return body

            for e in self.ENGS:
                if self.q[e]:
                    handles[e](mk(e))


def build_program():
    nc = bass.Bass("TRN2", target_bir_lowering=False)

    def din(name, shape):
        return nc.dram_tensor(name, list(shape), F32, kind="ExternalInput").ap()

    def dout(name, shape):
        return nc.dram_tensor(name, list(shape), F32, kind="ExternalOutput").ap()

    xall = din("xall", [NTOT, D])
    s0re = din("s0re", [16, 64, 64])
    s0im = din("s0im", [16, 64, 64])
    spool = din("spool", [16, 15, D])
    w_in = din("w_in", [D, 6 * D])
    w_glu = din("w_glu", [D, D])
    pool_mix = din("pool_mix", [4, 256, 256])
    w_bs = din("w_bs", [D, D])
    w_bp = din("w_bp", [D, D])
    w_out = din("w_out", [D, D])
    norm_gain = din("norm_gain", [D])
    b_gate = din("b_gate", [2 * D])
    ssm_d = din("ssm_d", [D])
    b_glu = din("b_glu", [D])
    pool_scale = din("pool_scale", [D])
    fgain = din("fgain", [D])
    a_re = din("a_re", [64, 64])
    a_im = din("a_im", [64, 64])
    log_dt = din("log_dt", [64])
    b_re = din("b_re", [64, 64, 16])
    b_im = din("b_im", [64, 64, 16])
    c_re = din("c_re", [64, 16, 64])
    c_im = din("c_im", [64, 16, 64])
    c_ident = din("c_ident", [128, 128])
    c_mask = din("c_mask", [128, 128])
    c_nvals = din("c_nvals", [128, 32, 8])
    c_invc = din("c_invc", [128, 4, 16])

    y_p = dout("y_p", [2048, D])
    y_s = dout("y_s", [128, D])
    nre_p = dout("nre_p", [64, 64])
    nim_p = dout("nim_p", [64, 64])
    npool_p = dout("npool_p", [15, D])
    nre_s = dout("nre_s", [16, 64, 64])
    nim_s = dout("nim_s", [16, 64, 64])
    npool_s = dout("npool_s", [16, 15, D])

    with contextlib.ExitStack() as st:
        def sb(name, shape, dt):
            return st.enter_context(nc.sbuf_tensor(name, list(shape), dt))

        P = Prog(nc)
        NC = True

        arena = sb("arena", [128, 5 * SLOT], BF16)
        SL = {k: i * SLOT for i, k in enumerate("ABCED")}
        RS = {k: [Res(k + "0"), Res(k + "1"), Res(k + "2")] for k in "ABCDE"}

        def slot_fm(k):
            o = SL[k]
            return arena[:, o:o + 8 * NTM].rearrange("p (a n) -> p a n", a=8)

        def slot_raw(k, n=SLOT):
            o = SL[k]
            return arena[:, o:o + n]

        wbuf = sb("wbuf", [128, 2, 8, 512], BF16)
        RW = [Res("w0"), Res("w1")]
        W1 = sb("W1", [128, 64, 128], BF16)
        TM = sb("TM", [128, 64, 128], BF16)
        W3 = sb("W3", [128, 2, 32, 128], BF16)
        R_W1, R_TM, R_W3 = Res("W1"), Res("TM"), Res("W3")
        pmw = sb("pmw", [128, 4, 2, 256], BF16)
        R_pmw = Res("pmw")
        gB = sb("gB", [128, D], F32)
        fB = sb("fB", [128, D], F32)
        R_gB, R_fB = Res("gB"), Res("fB")
        NXT = 3
        xt = [sb("xt%d" % i, [128, D], F32) for i in range(NXT)]
        RXT = [Res("xt%d" % i) for i in range(NXT)]
        xn = [sb("xn%d" % i, [128, D], BF16) for i in range(2)]
        RXN = [Res("xn0"), Res("xn1")]
        NTMP = 2
        tmpb = [sb("tmpb%d" % i, [128, 512], BF16) for i in range(NTMP)]
        RTMP = [Res("tmp%d" % i) for i in range(NTMP)]
        NYML = 4
        yml = [sb("yml%d" % i, [128, 4, 128], BF16) for i in range(NYML)]
        RYML = [Res("yml%d" % i) for i in range(NYML)]
        ycm = sb("ycm", [128, 2, 1024], BF16)
        RYCM = [Res("ycm0"), Res("ycm1")]
        identf = sb("identf", [128, 128], F32)
        identb = sb("identb", [128, 128], BF16)
        maskf = sb("maskf", [128, 128], F32)
        invc = sb("invc", [128, 4, 16], F32)
        R_const = Res("const")
        vecs = sb("vecs", [128, 32], F32)
        bgate = vecs[:, 0:16]
        bglu = vecs[:, 16:24]
        pscale = vecs[:, 24:32]
        NSTG = 2
        stg = sb("stg", [128, NSTG, 128], F32)
        RSTG = [Res("stg%d" % i) for i in range(NSTG)]
        R_vec = Res("vec")
        Dt = sb("Dt", [128, 64], F32)
        ArAr = sb("ArAr", [128, 2, 32], F32)
        AiPM = sb("AiPM", [128, 2, 32], F32)
        ArAr2 = sb("ArAr2", [128, 2, 32], F32)
        AiPM2 = sb("AiPM2", [128, 2, 32], F32)
        R_A8 = Res("A8")
        Sf = [sb("Sf%d" % i, [128, 2, 32], F32) for i in range(2)]
        RSF = [Res("Sf0"), Res("Sf1")]
        st1 = sb("st1", [128, 2, 32], F32)
        st2 = sb("st2", [128, 2, 32], F32)
        R_st1, R_st2 = Res("st1"), Res("st2")
        R_XSx = Res("XSx")
        S0f = sb("S0f", [128, 2, 32, 16], F32)
        R_S0f = Res("S0f")
        UPF = sb("UPF", [128, 8, 144], F32)
        R_UPF = Res("UPF")
        bufT = arena[:, SL["B"] + 7168:SL["B"] + 7168 + 1920].rearrange("p (a b r) -> p a b r", a=8, b=16)
        carry = sb("carry", [128, 8, 15], BF16)
        R_carry = Res("carry")
        stat = sb("stat", [128, 8], F32)
        RSTAT = [Res("stat%d" % i) for i in range(4)]

        NB = 8
        psf = [st.enter_context(nc.psum_tensor("ps%d" % i, [128, 512], F32)) for i in range(NB)]
        RB = [Res("bank%d" % i, excl=True) for i in range(NB)]
        bank_ctr = [0]

        def getbank():
            i = bank_ctr[0] % NB
            bank_ctr[0] += 1
            return psf[i], RB[i]

        rr = {"xt": 0, "xn": 0, "tmp": 0, "yml": 0, "w": 0, "ev": 0, "stat": 0, "stg": 0}

        def nxt(key, n):
            i = rr[key] % n
            rr[key] += 1
            return i

        def evac_eng():
            rr["ev"] += 1
            return "act" if rr["ev"] % 2 == 0 else "dve"

        def copy_op(eng, out, in_):
            if eng == "act":
                return lambda e: e.activation(out=out, in_=in_, func=AF.Copy)
            return lambda e: e.tensor_copy(out=out, in_=in_)

        P.dma("sp", lambda e: e.dma_start(out=identf[:], in_=c_ident), writes=[R_const])
        P.dma("sp", lambda e: e.dma_start(out=maskf[:], in_=c_mask), writes=[R_const])
        P.dma("sp", lambda e: e.dma_start(out=invc[:], in_=c_invc), writes=[R_const])
        P.dma("sp", lambda e: e.dma_start(out=gB[:], in_=norm_gain.partition_broadcast(128)), writes=[R_gB])
        P.dma("sp", lambda e: e.dma_start(out=fB[:], in_=fgain.partition_broadcast(128)), writes=[R_fB])
        P.op("dve", lambda e: e.memset(stg[:], 0.0), writes=RSTG)

        def stage_T(loads, K, ncols, evacs, tag=""):
            k = nxt("stg", NSTG)
            for (sl, src) in loads:
                P.dma("sp", lambda e, sl=sl, src=src: e.dma_start(out=sl(k), in_=src), writes=[RSTG[k]])
            bank, rb = getbank()
            P.op("pe", lambda e: e.transpose(bank[:, 0:K], in_=stg[0:K, k, :], identity=identf[0:K, 0:K]),
                 reads=[RSTG[k], R_const], writes=[rb])
            for (dst, srcf, wr) in evacs:
                eng = evac_eng()
                P.op(eng, copy_op(eng, dst, srcf(bank)), reads=[rb], writes=wr)

        stage_T([(lambda k: stg[0:16, k, :], b_gate.rearrange("(a p) -> a p", p=128)),
                 (lambda k: stg[16:24, k, :], b_glu.rearrange("(a p) -> a p", p=128)),
                 (lambda k: stg[24:32, k, :], pool_scale.rearrange("(a p) -> a p", p=128))],
                32, 128, [(vecs[:, :], lambda bank: bank[:, 0:32], [R_vec])], tag="v")
        P.op("dve", lambda e: e.tensor_copy(out=identb[:], in_=identf[:]), reads=[R_const], writes=[R_const])

        def f32view(off_bf16, shape):
            n = int(np.prod(shape))
            v = arena[:, off_bf16:off_bf16 + 2 * n].bitcast(F32)
            return v

        W3f_off = 0
        H_off = 16384
        sm_off = 32768
        W3f = W3[:].rearrange("p r g (j c) -> p r g j c", j=8)
        Hf = arena[:, H_off:H_off + 8192].rearrange("p (r g j c) -> p r g j c", r=2, g=32, j=8)
        R_W3f, R_Hf = R_W3, Res("Hf")
        smp = [sm_off]

        def small(shape):
            n = int(np.prod(shape))
            v = f32view(smp[0], [n])
            smp[0] += 2 * n
            return v

        def small3(a, b):
            return small([a * b]).rearrange("p (a b) -> p a b", a=a)

        are = small([32]); aim = small([32]); ldt = small([32]); dtt = small([32])
        dtar = small([32]); th = small([32])
        upf_flat = UPF[:].rearrange("p a b -> p (a b)")

        def as_s3(ap2d):
            v = ap2d if ap2d.dtype == F32 else ap2d.bitcast(F32)
            return v.rearrange("p (a b) -> p a b", a=32)
        ang = as_s3(yml[0][:].rearrange("p a b -> p (a b)"))
        ang2 = as_s3(yml[1][:].rearrange("p a b -> p (a b)"))
        marg = as_s3(yml[2][:].rearrange("p a b -> p (a b)"))
        magp = as_s3(yml[3][:].rearrange("p a b -> p (a b)"))
        magn = as_s3(tmpb[0][:])
        yk = as_s3(tmpb[1][:])
        kf = as_s3(upf_flat[:, 0:256])
        rs = as_s3(upf_flat[:, 256:512])
        rc = small3(32, 8)
        sn = small3(32, 8); cs = small3(32, 8)
        Pre = small3(32, 8); Pim = small3(32, 8); Nre = small3(32, 8); Nim = small3(32, 8)
        nr = small([32]); den = small([32]); rden = small([32])
        qre = small([32]); qim = small([32]); tq1 = small([32]); tq2 = small([32])
        nvals = upf_flat[:, 512:768].rearrange("p (a b) -> p a b", a=32)
        ki = upf_flat[:, 768:1024].bitcast(I32).rearrange("p (a b) -> p a b", a=32)

        def xth(i, part):
            return xt[i][:, part * 512:(part + 1) * 512].rearrange("p (a b) -> p a b", a=32)
        Bre = xth(0, 0); Bim = xth(0, 1); Cre = xth(1, 0); Cim = xth(1, 1); Bbre = xth(2, 0); Bbim = xth(2, 1)
        wflat = wbuf[:].rearrange("p a k n -> p (a k n)").bitcast(F32)
        t4a = wflat[:, 0:2048].rearrange("p (g j c) -> p g j c", g=32, j=4)
        t4b = wflat[:, 2048:4096].rearrange("p (g j c) -> p g j c", g=32, j=4)
        assert smp[0] <= 4 * SLOT, smp[0]
        R_sx = [RXT[0], RXT[1], RXT[2], R_UPF, RW[0], RW[1]] + RYML + RTMP
        R_s = Res("setup_small")

        for (src, dstv) in ((a_re, are), (a_im, aim)):
            stage_T([(lambda k: stg[0:64, k, 0:64], src), (lambda k: stg[0:64, k, 64:128], src)], 64, 128,
                    [(dstv[0:64, :], lambda bank: bank[0:64, 0:32], [R_s]),
                     (dstv[64:128, :], lambda bank: bank[64:128, 32:64], [R_s])], tag="a")
        for gh in range(2):
            ps_ = slice(gh * 64, (gh + 1) * 64)
            gs_ = slice(gh * 32, (gh + 1) * 32)
            P.dma("sp", lambda e, ps_=ps_, gs_=gs_: e.dma_start(
                out=ldt[ps_, :], in_=log_dt[gs_].partition_broadcast(64)), writes=[R_s])
            P.dma("sp", lambda e, ps_=ps_, gs_=gs_: e.dma_start(
                out=Bre[ps_, :, :], in_=b_re[gs_].rearrange("g p c -> p g c")), writes=[R_s, RXT[0]])
            P.dma("sp", lambda e, ps_=ps_, gs_=gs_: e.dma_start(
                out=Bim[ps_, :, :], in_=b_im[gs_].rearrange("g p c -> p g c")), writes=[R_s, RXT[0]])
        for r in range(8):
            gh = r // 4
            ps_ = slice(gh * 64, (gh + 1) * 64)
            for (src, dstC) in ((c_re, Cre), (c_im, Cim)):
                stage_T([(lambda k, ps_=ps_: stg[:, k, ps_],
                          src.rearrange("g c p -> (g c) p")[128 * r:128 * (r + 1), :])], 128, 128,
                        [(dstC[ps_, (r % 4) * 8:(r % 4) * 8 + 8, :].rearrange("p g c -> p (g c)"),
                          lambda bank, ps_=ps_: bank[ps_, 0:128], [R_s, RXT[1]])], tag="c")
        P.dma("sp", lambda e: e.dma_start(out=nvals, in_=c_nvals), writes=[R_s, R_UPF])
        stage_T([(lambda k: stg[0:64, k, :].rearrange("g (s c) -> g s c", s=8),
                  ssm_d.rearrange("(g c) -> g c", c=16).unsqueeze(1).broadcast_to([64, 8, 16]))], 64, 128,
                [(Dt[:, :], lambda bank: bank[:, 0:64], [R_vec])], tag="d")
        def load_S0(S0f, R_S0f):
            for ri, src in enumerate((s0re, s0im)):
                for r in range(8):
                    rows = src.rearrange("b g p -> (b g) p")[128 * r:128 * (r + 1), :]
                    stage_T([(lambda k: stg[:, k, 0:64], rows), (lambda k: stg[:, k, 64:128], rows)], 128, 128,
                            [(S0f[0:64, ri, :, 2 * r:2 * r + 2].rearrange("p g b -> p b g"),
                              lambda bank: bank[0:64, 0:128].rearrange("p (b g) -> p b g", b=2)[:, :, 0:32], [R_S0f]),
                             (S0f[64:128, ri, :, 2 * r:2 * r + 2].rearrange("p g b -> p b g"),
                              lambda bank: bank[64:128, 0:128].rearrange("p (b g) -> p b g", b=2)[:, :, 32:64],
                              [R_S0f])])
        P.dma("pool", lambda e: e.dma_start(out=pmw[:], in_=pool_mix.rearrange("g (k p) n -> p g k n", p=128)),
              writes=[R_pmw])

        def phaseN_gen(ti, T0, NT, first_deps, xt_ids=None):
            xnT = slot_fm("D")
            for r0 in range(0, NT, 128):
                rows = min(128, NT - r0)
                ix = nxt("xt", NXT) if xt_ids is None else xt_ids[(r0 // 128) % len(xt_ids)]
                P.dma("sp", lambda e, ix=ix, r0=r0, rows=rows: e.dma_start(
                    out=xt[ix][0:rows, :], in_=xall[T0 + r0:T0 + r0 + rows, :]), writes=[RXT[ix]])
                si = nxt("stat", 4)
                jn = nxt("xn", 2)
                P.op("act", lambda e, ix=ix, rows=rows, si=si, jn=jn: e.activation(
                    out=xn[jn][0:rows, :], in_=xt[ix][0:rows, :], func=AF.Square,
                    accum_out=stat[0:rows, 2 * si:2 * si + 1]), reads=[RXT[ix]], writes=[RXN[jn], RSTAT[si]])
                P.op("act", lambda e, rows=rows, si=si: e.activation(
                    out=stat[0:rows, 2 * si + 1:2 * si + 2], in_=stat[0:rows, 2 * si:2 * si + 1],
                    func=AF.Sqrt, scale=1.0 / D, bias=EPS), reads=[RSTAT[si]], writes=[RSTAT[si]])
                P.op("dve", lambda e, rows=rows, si=si: e.reciprocal(
                    out=stat[0:rows, 2 * si:2 * si + 1], in_=stat[0:rows, 2 * si + 1:2 * si + 2]),
                    reads=[RSTAT[si]], writes=[RSTAT[si]])
                P.op("dve", lambda e, ix=ix, jn=jn, rows=rows, si=si: e.scalar_tensor_tensor(
                    out=xn[jn][0:rows, :], in0=xt[ix][0:rows, :], scalar=stat[0:rows, 2 * si:2 * si + 1],
                    in1=gB[0:rows, :], op0=ALU.mult, op1=ALU.mult),
                    reads=[RXT[ix], RSTAT[si], R_gB], writes=[RXN[jn]])
                bank, rb = getbank()
                bb = bank[:].bitcast(BF16)

                def trn(e, jn=jn, rows=rows, bb=bb):
                    ins = None
                    for kt in range(8):
                        ins = e.transpose(bb[:, kt * 128:kt * 128 + rows], in_=xn[jn][0:rows, kt * 128:(kt + 1) * 128],
                                          identity=identb[0:rows, 0:rows])
                    return ins
                P.op("pe", trn, reads=[RXN[jn], R_const], writes=[rb])
                eng = evac_eng()
                P.op(eng, copy_op(eng, xnT[:, :, r0:r0 + rows],
                                  bb.rearrange("p (k t) -> p k t", k=8)[:, :, 0:rows]),
                     reads=[rb], writes=[RS["D"][r0 // 512]], deps=first_deps)
                yield


        def S(eng, fn):
            return P.op(eng, fn, reads=[R_s, R_vec], writes=[R_s] + R_sx)

        def bc8(v):
            return v.unsqueeze(2).broadcast_to([128, 32, 8])

        TWO_PI = 2.0 * math.pi
        S("act", lambda e: e.activation(out=dtt, in_=ldt, func=AF.Exp))
        S("dve", lambda e: e.tensor_tensor(out=dtar, in0=dtt, in1=are, op=ALU.mult))
        S("dve", lambda e: e.tensor_tensor(out=th, in0=dtt, in1=aim, op=ALU.mult))
        S("dve", lambda e: e.tensor_tensor(out=ang, in0=nvals, in1=bc8(th), op=ALU.mult))
        S("dve", lambda e: e.tensor_tensor(out=marg, in0=nvals, in1=bc8(dtar), op=ALU.mult))
        S("act", lambda e: e.activation(out=magp, in_=marg, func=AF.Exp))
        S("act", lambda e: e.activation(out=magn, in_=marg, func=AF.Exp, scale=-1.0))
        S("dve", lambda e: e.tensor_scalar(out=ang2, in0=ang, scalar1=math.pi / 2, scalar2=None, op0=ALU.add))

        def reduce_angle(src, dst):
            S("dve", lambda e: e.tensor_scalar(out=yk, in0=src, scalar1=1.0 / TWO_PI, scalar2=None, op0=ALU.mult))
            S("dve", lambda e: e.tensor_copy(out=ki, in_=yk))
            S("dve", lambda e: e.tensor_copy(out=kf, in_=ki))
            S("dve", lambda e: e.scalar_tensor_tensor(out=dst, in0=kf, scalar=-TWO_PI, in1=src,
                                                      op0=ALU.mult, op1=ALU.add))
            S("dve", lambda e: e.tensor_scalar(out=dst, in0=dst, scalar1=3.1415925, scalar2=-3.1415925,
                                               op0=ALU.min, op1=ALU.max))

        reduce_angle(ang, rs)
        S("act", lambda e: e.activation(out=sn, in_=rs, func=AF.Sin))
        reduce_angle(ang2, rc)
        S("act", lambda e: e.activation(out=cs, in_=rc, func=AF.Sin))
        S("dve", lambda e: e.tensor_tensor(out=Pre, in0=magp, in1=cs, op=ALU.mult))
        S("dve", lambda e: e.tensor_tensor(out=Pim, in0=magp, in1=sn, op=ALU.mult))
        S("dve", lambda e: e.tensor_tensor(out=Nre, in0=magn, in1=cs, op=ALU.mult))
        S("dve", lambda e: e.scalar_tensor_tensor(out=Nim, in0=magn, scalar=-1.0, in1=sn, op0=ALU.mult, op1=ALU.mult))
        S("dve", lambda e: e.tensor_scalar(out=nr, in0=Pre[:, :, 0], scalar1=-1.0, scalar2=None, op0=ALU.add))
        S("dve", lambda e: e.tensor_tensor(out=den, in0=are, in1=are, op=ALU.mult))
        S("dve", lambda e: e.tensor_tensor(out=tq1, in0=aim, in1=aim, op=ALU.mult))
        S("dve", lambda e: e.tensor_tensor(out=den, in0=den, in1=tq1, op=ALU.add))
        S("dve", lambda e: e.reciprocal(out=rden, in_=den))
        S("dve", lambda e: e.tensor_tensor(out=tq1, in0=nr, in1=are, op=ALU.mult))
        S("dve", lambda e: e.tensor_tensor(out=tq2, in0=Pim[:, :, 0], in1=aim, op=ALU.mult))
        S("dve", lambda e: e.tensor_tensor(out=tq1, in0=tq1, in1=tq2, op=ALU.add))
        S("dve", lambda e: e.tensor_tensor(out=qre, in0=tq1, in1=rden, op=ALU.mult))
        S("dve", lambda e: e.tensor_tensor(out=tq1, in0=Pim[:, :, 0], in1=are, op=ALU.mult))
        S("dve", lambda e: e.tensor_tensor(out=tq2, in0=nr, in1=aim, op=ALU.mult))
        S("dve", lambda e: e.tensor_tensor(out=tq1, in0=tq1, in1=tq2, op=ALU.subtract))
        S("dve", lambda e: e.tensor_tensor(out=qim, in0=tq1, in1=rden, op=ALU.mult))

        def bc16(v):
            return v.unsqueeze(2).broadcast_to([128, 32, 16])

        t3a = t4a[:, :, 0, :]
        t3b = t4b[:, :, 0, :]
        S("dve", lambda e: e.tensor_tensor(out=t3a, in0=Bre, in1=bc16(qre), op=ALU.mult))
        S("dve", lambda e: e.tensor_tensor(out=t3b, in0=Bim, in1=bc16(qim), op=ALU.mult))
        S("dve", lambda e: e.tensor_tensor(out=Bbre, in0=t3a, in1=t3b, op=ALU.subtract))
        S("dve", lambda e: e.tensor_tensor(out=t3a, in0=Bim, in1=bc16(qre), op=ALU.mult))
        S("dve", lambda e: e.tensor_tensor(out=t3b, in0=Bre, in1=bc16(qim), op=ALU.mult))
        S("dve", lambda e: e.tensor_tensor(out=Bbim, in0=t3a, in1=t3b, op=ALU.add))
        P.op("dve", lambda e: e.tensor_copy(out=ArAr[:, 0, :], in_=Pre[:, :, 7]), reads=[R_s], writes=[R_A8])
        P.op("dve", lambda e: e.tensor_copy(out=ArAr[:, 1, :], in_=Pre[:, :, 7]), reads=[R_s], writes=[R_A8])
        P.op("dve", lambda e: e.tensor_scalar(out=AiPM[:, 0, :], in0=Pim[:, :, 7], scalar1=-1.0, scalar2=None,
                                              op0=ALU.mult), reads=[R_s], writes=[R_A8])
        P.op("dve", lambda e: e.tensor_copy(out=AiPM[:, 1, :], in_=Pim[:, :, 7]), reads=[R_s], writes=[R_A8])
        S("dve", lambda e: e.tensor_tensor(out=tq1, in0=Pre[:, :, 7], in1=Pre[:, :, 7], op=ALU.mult))
        S("dve", lambda e: e.tensor_tensor(out=tq2, in0=Pim[:, :, 7], in1=Pim[:, :, 7], op=ALU.mult))
        P.op("dve", lambda e: e.tensor_tensor(out=ArAr2[:, 0, :], in0=tq1, in1=tq2, op=ALU.subtract),
             reads=[R_s], writes=[R_A8])
        P.op("dve", lambda e: e.tensor_tensor(out=ArAr2[:, 1, :], in0=tq1, in1=tq2, op=ALU.subtract),
             reads=[R_s], writes=[R_A8])
        S("dve", lambda e: e.tensor_tensor(out=tq1, in0=Pre[:, :, 7], in1=Pim[:, :, 7], op=ALU.mult))
        P.op("dve", lambda e: e.tensor_scalar(out=AiPM2[:, 0, :], in0=tq1, scalar1=-2.0, scalar2=None, op0=ALU.mult),
             reads=[R_s], writes=[R_A8])
        P.op("dve", lambda e: e.tensor_scalar(out=AiPM2[:, 1, :], in0=tq1, scalar1=2.0, scalar2=None, op0=ALU.mult),
             reads=[R_s], writes=[R_A8])
        P.op("dve", lambda e: e.memset(Sf[0][:], 0.0), writes=[RSF[0]])
        P.op("dve", lambda e: e.memset(carry[:], 0.0), writes=[R_carry])

        def bcj(v, n=4):
            return v.unsqueeze(2).broadcast_to([128, 32, n, 16])

        def bcc(v, n=4):
            return v.unsqueeze(3).broadcast_to([128, 32, n, 16])

        def cplx(dst_re, dst_im, Xre, Xim, Yre, Yim, neg_im, writes, n=4):
            ta, tb = t4a[:, :, 0:n, :], t4b[:, :, 0:n, :]

            def op(fn):
                P.op("dve", fn, reads=[R_s], writes=writes + [R_s] + R_sx)
            op(lambda e: e.tensor_tensor(out=ta, in0=Xre, in1=Yre, op=ALU.mult))
            op(lambda e: e.tensor_tensor(out=tb, in0=Xim, in1=Yim, op=ALU.mult))
            op(lambda e: e.tensor_tensor(out=dst_re, in0=ta, in1=tb, op=ALU.subtract))
            op(lambda e: e.tensor_tensor(out=ta, in0=Xre, in1=Yim, op=ALU.mult))
            op(lambda e: e.tensor_tensor(out=tb, in0=Xim, in1=Yre, op=ALU.mult))
            if neg_im:
                op(lambda e: e.scalar_tensor_tensor(out=dst_im, in0=ta, scalar=-1.0, in1=tb,
                                                    op0=ALU.mult, op1=ALU.subtract))
            else:
                op(lambda e: e.tensor_tensor(out=dst_im, in0=ta, in1=tb, op=ALU.add))

        for jh in range(2):
            js = slice(jh * 4, jh * 4 + 4)
            cplx(W3f[:, 0, :, js, :], W3f[:, 1, :, js, :], bcj(Cre), bcj(Cim), bcc(Pre[:, :, js]), bcc(Pim[:, :, js]),
                 True, [R_W3f])
        for jh in range(2):
            js = slice(jh * 4, jh * 4 + 4)
            cplx(Hf[:, 0, :, js, :], Hf[:, 1, :, js, :], bcj(Bbre), bcj(Bbim), bcc(Nre[:, :, js]), bcc(Nim[:, :, js]),
                 False, [R_Hf])
        tmfs = [ycm[:, 0, :].bitcast(F32).rearrange("p (a b) -> p a b", a=4),
                ycm[:, 1, :].bitcast(F32).rearrange("p (a b) -> p a b", a=4)]
        R_tmf = RYCM
        n0_gen = phaseN_gen(0, TILES[0][0], TILES[0][1], [], xt_ids=[0, 1])
        for g4 in range(16):
            if g4 % 2 == 1:
                next(n0_gen, None)
            bank, rb = getbank()

            def mm(e, g4=g4, bank=bank):
                ins = None
                for gi in range(4):
                    g = g4 * 4 + gi
                    gh, g32 = divmod(g, 32)
                    ps_ = slice(gh * 64, (gh + 1) * 64)
                    o = bank[:, gi * 128:(gi + 1) * 128]
                    e.matmul(o, lhsT=Hf[ps_, 0, g32].rearrange("p j c -> p (j c)"),
                             rhs=W3[ps_, 0, g32, :], start=True, stop=False)
                    ins = e.matmul(o, lhsT=Hf[ps_, 1, g32].rearrange("p j c -> p (j c)"),
                                   rhs=W3[ps_, 1, g32, :], start=False, stop=True)
                return ins
            P.op("pe", mm, reads=[R_Hf, R_W3f], writes=[rb])
            tmf_ = tmfs[g4 % 2]
            rt_ = R_tmf[g4 % 2]
            P.op("dve", lambda e, bank=bank, tmf_=tmf_: e.tensor_tensor(
                out=tmf_, in0=bank[:, :].rearrange("p (a b) -> p a b", a=4),
                in1=maskf[:].unsqueeze(1).broadcast_to([128, 4, 128]), op=ALU.mult),
                reads=[rb, R_const], writes=[rt_])
            for gi in range(4):
                g = g4 * 4 + gi
                P.op("dve", lambda e, g=g, gi=gi, tmf_=tmf_: e.scalar_tensor_tensor(
                    out=TM[:, g, :], in0=identf[:], scalar=Dt[:, g:g + 1], in1=tmf_[:, gi, :],
                    op0=ALU.mult, op1=ALU.add), reads=[rt_, R_const, R_vec], writes=[R_TM])
        W1T = Hf

        def W(fn):
            P.op("dve", fn, reads=[R_s], writes=[R_Hf, R_s] + R_sx)
        PreR = Pre[:, :, 6::-1]
        PimR = Pim[:, :, 6::-1]
        for (j0, n) in ((0, 4), (4, 3)):
            js = slice(j0, j0 + n)
            cplx(W1T[:, 0, :, js, :], W1T[:, 1, :, js, :], bcj(Bbre, n), bcj(Bbim, n),
                 bcc(PreR[:, :, js], n), bcc(PimR[:, :, js], n), False, [R_Hf], n=n)
        W(lambda e: e.tensor_copy(out=W1T[:, 0, :, 7, :], in_=Bbre))
        W(lambda e: e.tensor_copy(out=W1T[:, 1, :, 7, :], in_=Bbim))
        for g4 in range(16):
            bank, rb = getbank()
            bbw = bank[:].bitcast(BF16)

            def tr(e, g4=g4, bbw=bbw):
                ins = None
                for gi in range(4):
                    g = g4 * 4 + gi
                    gh, g32 = divmod(g, 32)
                    ps_ = slice(gh * 64, (gh + 1) * 64)
                    for ri in range(2):
                        c0 = (gi * 2 + ri) * 64
                        ins = e.transpose(bbw[:, c0:c0 + 64], in_=W1T[ps_, ri, g32, :, :].rearrange("p j c -> p (j c)"),
                                          identity=identb[ps_, ps_])
                return ins
            P.op("pe", tr, reads=[R_Hf, R_const], writes=[rb])
            eng = evac_eng()
            P.op(eng, copy_op(eng, W1[:, g4 * 4:(g4 + 1) * 4, :].rearrange("p a b -> p (a b)"), bbw[:, 0:512]),
                 reads=[rb], writes=[R_W1])

        setup_done = [P.q[e_][-1] for e_ in ("pe", "act", "dve") if P.q[e_]]


        NBLK = 20
        wcache = nc.dram_tensor("wcache", [NBLK, 128, 4096], BF16).ap()
        wc_idx = {}
        RC = [Res("wc%d" % i) for i in range(NBLK)]

        def load_w(src_ap, col0, name):
            i = nxt("w", 2)
            key = (name, col0)
            flat = wbuf[:, i].rearrange("p k n -> p (k n)")
            if key not in wc_idx:
                idx = len(wc_idx)
                wc_idx[key] = idx
                P.dma("pool", lambda e: e.dma_start(
                    out=wbuf[:, i, :, :],
                    in_=src_ap[:, col0:col0 + 512].rearrange("(k p) n -> p k n", p=128)), writes=[RW[i]])
                P.dma("pool", lambda e: e.dma_start(out=wcache[idx], in_=flat), reads=[RW[i]], writes=[RC[idx]])
            else:
                idx = wc_idx[key]
                P.dma("pool", lambda e: e.dma_start(out=flat, in_=wcache[idx]), reads=[RC[idx]], writes=[RW[i]])
            return wbuf[:, i], RW[i]

        state = {"sf": 0}

        def do_tile(ti, T0, NT):
            NCH = NT // 8
            ntiles = [(n0, min(512, NT - n0)) for n0 in range(0, NT, 512)]
            csubs = [(c0, min(128, NCH - c0)) for c0 in range(0, NCH, 128)]
            nP = 128 if ti == 0 else 130
            NTp = 8 * nP

            def rh(k, n0):
                return RS[k][n0 // 512]


            first_deps = setup_done if ti == 0 else []
            xnT = slot_fm("D")

            if ti == 0:
                for _ in n0_gen:
                    pass

            xnT = slot_fm("D")
            UCM = slot_raw("A", 8192).rearrange("p (g s c) -> p g s c", g=64, s=8)
            Uml = slot_raw("B", 64 * KU).rearrange("p (g k) -> p g k", g=64)
            XS = slot_raw("C", 64 * KX).rearrange("p (r g k) -> p r g k", r=2, g=32)
            for h in range(2):
                wv, rw = load_w(w_in, h * 512, "w_in")
                for (c0, nc) in csubs:
                    for s in range(8):
                        bank, rb = getbank()

                        def mm(e, s=s, bank=bank, wv=wv, c0=c0, nc=nc):
                            ins = None
                            for kt in range(8):
                                ins = e.matmul(bank[0:nc, 0:512], lhsT=xnT[:, kt, 8 * c0 + s:8 * (c0 + nc):8],
                                               rhs=wv[:, kt, :], start=(kt == 0), stop=(kt == 7))
                            return ins
                        P.op("pe", mm, reads=RS["D"] + [rw], writes=[rb])
                        eng = evac_eng()
                        P.op(eng, copy_op(eng, UCM[0:nc, h * 32:(h + 1) * 32, s, :],
                                          bank[0:nc, 0:512].rearrange("p (g c) -> p g c", g=32)),
                             reads=[rb], writes=RS["A"], deps=first_deps)
                    for gb in range(4 * h, 4 * h + 4):
                        bank, rb = getbank()
                        bb = bank[:].bitcast(BF16)

                        def tr(e, gb=gb, bb=bb, nc=nc):
                            ins = None
                            for gi in range(8):
                                g = gb * 8 + gi
                                ins = e.transpose(bb[:, gi * 128:gi * 128 + nc],
                                                  in_=UCM[0:nc, g, :, :].rearrange("p s c -> p (s c)"),
                                                  identity=identb[0:nc, 0:nc])
                            return ins
                        P.op("pe", tr, reads=RS["A"] + [R_const], writes=[rb])
                        eng = evac_eng()
                        P.op(eng, copy_op(eng, Uml[:, gb * 8:(gb + 1) * 8, c0:c0 + nc],
                                          bb.rearrange("p (g k) -> p g k", g=8)[:, :, 0:nc]),
                             reads=[rb], writes=RS["B"], deps=first_deps)
            for q in range(32):
                bank, rb = getbank()

                def mm(e, q=q, bank=bank):
                    ins = None
                    for gh in range(2):
                        g = gh * 32 + q
                        for ri in range(2):
                            ins = e.matmul(bank[gh * 64:(gh + 1) * 64, ri * 256:ri * 256 + NCH],
                                           lhsT=W1[:, g, ri * 64:(ri + 1) * 64], rhs=Uml[:, g, 0:NCH],
                                           start=True, stop=True)
                    return ins
                P.op("pe", mm, reads=RS["B"] + [R_W1], writes=[rb])
                eng = evac_eng()
                P.op(eng, copy_op(eng, XS[:, :, q, 1:1 + NCH],
                                  bank[:, :].rearrange("p (r k) -> p r k", r=2)[:, :, 0:NCH]),
                     reads=[rb], writes=RS["C"] + [R_XSx], deps=first_deps)

            def proj_gen(src_w, col0, rhs_slot, evac, name):
                rhs = slot_fm(rhs_slot)
                for h in range(2):
                    wv, rw = load_w(src_w, col0 + h * 512, name)
                    for (n0, nn) in ntiles:
                        for f4 in range(4):
                            fo = h * 4 + f4
                            bank, rb = getbank()

                            def mm(e, bank=bank, wv=wv, f4=f4, n0=n0, nn=nn):
                                ins = None
                                for kt in range(8):
                                    ins = e.matmul(bank[:, 0:nn], lhsT=wv[:, kt, f4 * 128:(f4 + 1) * 128],
                                                   rhs=rhs[:, kt, n0:n0 + nn], start=(kt == 0), stop=(kt == 7))
                                return ins
                            P.op("pe", mm, reads=[rh(rhs_slot, n0), rw], writes=[rb])
                            evac(bank, rb, fo, n0, nn)
                            yield

            def proj(src_w, col0, rhs_slot, evac, name):
                for _ in proj_gen(src_w, col0, rhs_slot, evac, name):
                    pass

            gsB = slot_fm("E")

            def ev_gs(bank, rb, fo, n0, nn):
                P.op("act", lambda e: e.activation(out=gsB[:, fo, n0:n0 + nn], in_=bank[:, 0:nn], func=AF.Sigmoid,
                                                   bias=bgate[:, fo:fo + 1]),
                     reads=[rb, R_vec], writes=[rh("E", n0)])
            gs_gen = proj_gen(w_in, 4 * D, "D", ev_gs, "w_in")
            ysA = slot_fm("A")

            def ev_zs(bank, rb, fo, n0, nn):
                P.op("act", lambda e: e.activation(out=ysA[:, fo, n0:n0 + nn], in_=bank[:, 0:nn], func=AF.Silu),
                     reads=[rb], writes=[rh("A", n0)])
            zs_gen = proj_gen(w_in, 1 * D, "D", ev_zs, "w_in")

            if ti == 0:
                load_S0(S0f, R_S0f)
            hstep = nP // 2
            cur = state["sf"]
            P.op("act", lambda e, cur=cur: e.activation(out=XS[:, :, :, 0], in_=Sf[cur][:], func=AF.Copy),
                 reads=[RSF[cur]], writes=RS["C"])
            Xe = XS[:, :, :, 1:nP + 1:2]
            Xe_sw = XS[:, ::-1, :, 1:nP + 1:2]
            Xo = XS[:, :, :, 2:nP + 2:2]
            tA = slot_raw("A", 2 * 64 * hstep).bitcast(F32).rearrange("p (r g j) -> p r g j", r=2, g=32)
            tE = slot_raw("E", 2 * 64 * hstep).bitcast(F32).rearrange("p (r g j) -> p r g j", r=2, g=32)
            bch = lambda v, n_: v.unsqueeze(3).broadcast_to([128, 2, 32, n_])
            P.op("dve", lambda e: e.tensor_tensor(out=tA, in0=Xe, in1=bch(ArAr[:], hstep), op=ALU.mult),
                 reads=[R_XSx, R_A8], writes=RS["A"])
            P.op("dve", lambda e: e.tensor_tensor(out=tE, in0=Xe_sw, in1=bch(AiPM[:], hstep), op=ALU.mult),
                 reads=[R_XSx, R_A8], writes=RS["E"])
            P.op("dve", lambda e: e.tensor_tensor(out=tA, in0=tA, in1=tE, op=ALU.add),
                 reads=RS["E"], writes=RS["A"])
            P.op("dve", lambda e: e.tensor_tensor(out=Xo, in0=tA, in1=Xo, op=ALU.add),
                 reads=RS["A"], writes=RS["C"] + [R_XSx])
            for j in range(hstep):
                if next(gs_gen, "done") == "done":
                    next(zs_gen, None)
                nx = 1 - cur
                col = 2 * j + 2
                P.op("dve", lambda e, cur=cur: e.tensor_tensor(out=st1[:], in0=Sf[cur][:], in1=ArAr2[:], op=ALU.mult),
                     reads=[RSF[cur], R_A8], writes=[R_st1])
                P.op("dve", lambda e, cur=cur: e.tensor_tensor(out=st2[:], in0=Sf[cur][:, ::-1, :], in1=AiPM2[:],
                                                               op=ALU.mult),
                     reads=[RSF[cur], R_A8], writes=[R_st2])
                P.op("dve", lambda e, col=col: e.tensor_tensor(out=st1[:], in0=st1[:], in1=XS[:, :, :, col], op=ALU.add),
                     reads=[R_st1, R_XSx], writes=[R_st1])
                P.op("dve", lambda e, nx=nx: e.tensor_tensor(out=Sf[nx][:], in0=st1[:], in1=st2[:], op=ALU.add),
                     reads=[R_st1, R_st2], writes=[RSF[nx]])
                P.op("act", lambda e, nx=nx, col=col: e.activation(out=XS[:, :, :, col], in_=Sf[nx][:], func=AF.Copy),
                     reads=[RSF[nx]], writes=RS["C"])
                cur = nx
            p1, p2 = nxt("xt", NXT), nxt("xt", NXT)
            for jb0 in range(0, hstep, 16):
                n_ = min(16, hstep - jb0)
                w1 = xt[p1][:, 0:64 * n_].rearrange("p (r g j) -> p r g j", r=2, g=32)
                w2 = xt[p2][:, 0:64 * n_].rearrange("p (r g j) -> p r g j", r=2, g=32)
                so = XS[:, :, :, 2 * jb0:2 * (jb0 + n_):2]
                so_sw = XS[:, ::-1, :, 2 * jb0:2 * (jb0 + n_):2]
                xe = XS[:, :, :, 2 * jb0 + 1:2 * (jb0 + n_) + 1:2]
                P.op("dve", lambda e, w1=w1, so=so, n_=n_: e.tensor_tensor(out=w1, in0=so, in1=bch(ArAr[:], n_),
                                                                          op=ALU.mult),
                     reads=RS["C"] + [R_A8], writes=[RXT[p1]])
                P.op("dve", lambda e, w2=w2, so_sw=so_sw, n_=n_: e.tensor_tensor(out=w2, in0=so_sw,
                                                                                in1=bch(AiPM[:], n_), op=ALU.mult),
                     reads=RS["C"] + [R_A8], writes=[RXT[p2]])
                P.op("dve", lambda e, w1=w1, w2=w2: e.tensor_tensor(out=w1, in0=w1, in1=w2, op=ALU.add),
                     reads=[RXT[p2]], writes=[RXT[p1]])
                P.op("dve", lambda e, w1=w1, xe=xe: e.tensor_tensor(out=xe, in0=w1, in1=xe, op=ALU.add),
                     reads=[RXT[p1]], writes=RS["C"])
            state["sf"] = cur
            for _ in gs_gen:
                pass
            for _ in zs_gen:
                pass
            if ti == LAST:
                io0 = nxt("xt", NXT)
                for gh in range(2):
                    bank, rb = getbank()
                    ps_ = slice(gh * 64, (gh + 1) * 64)

                    def trp(e, bank=bank, cur=cur, ps_=ps_):
                        ins = None
                        for ri in range(2):
                            ins = e.transpose(bank[0:32, ri * 64:(ri + 1) * 64], in_=Sf[cur][ps_, ri, :],
                                              identity=identf[ps_, ps_])
                        return ins
                    P.op("pe", trp, reads=[RSF[cur], R_const], writes=[rb])
                    P.op("dve", lambda e, bank=bank, io0=io0, gh=gh: e.tensor_copy(
                        out=xt[io0][0:32, gh * 128:(gh + 1) * 128], in_=bank[0:32, 0:128]),
                        reads=[rb], writes=[RXT[io0]])
                for ri, dst in enumerate((nre_p, nim_p)):
                    for gh in range(2):
                        idx = gh * 2 + ri
                        P.dma("sp", lambda e, dst=dst, gh=gh, idx=idx, io0=io0: e.dma_start(
                            out=dst[gh * 32:(gh + 1) * 32, :], in_=xt[io0][0:32, idx * 64:(idx + 1) * 64]),
                            reads=[RXT[io0]])
                i1, i2, i3 = nxt("xt", NXT), nxt("xt", NXT), nxt("xt", NXT)
                v1 = xt[i1][:].rearrange("p (r g b) -> p r g b", r=2, g=32)
                v2 = xt[i2][:].rearrange("p (r g b) -> p r g b", r=2, g=32)
                v3p = xt[i3][:].rearrange("p (r b g) -> p r b g", r=2, b=16)
                v3 = v3p.rearrange("p r b g -> p r g b")
                bcb = lambda v: v.unsqueeze(3).broadcast_to([128, 2, 32, 16])
                P.op("dve", lambda e: e.tensor_tensor(out=v1, in0=S0f[:], in1=bcb(ArAr[:]), op=ALU.mult),
                     reads=[R_S0f, R_A8], writes=[RXT[i1]])
                P.op("dve", lambda e: e.tensor_tensor(out=v2, in0=S0f[:, ::-1, :, :], in1=bcb(AiPM[:]), op=ALU.mult),
                     reads=[R_S0f, R_A8], writes=[RXT[i2]])
                P.op("dve", lambda e: e.tensor_tensor(out=v1, in0=v1, in1=v2, op=ALU.add),
                     reads=[RXT[i1], RXT[i2]], writes=[RXT[i1]])
                P.op("dve", lambda e: e.tensor_tensor(out=v3, in0=v1, in1=XS[:, :, :, nP + 1:nP + 17], op=ALU.add),
                     reads=[RXT[i1]] + RS["C"], writes=[RXT[i3]])
                io1 = nxt("xt", NXT)
                for gh in range(2):
                    bank, rb = getbank()
                    ps_ = slice(gh * 64, (gh + 1) * 64)

                    def trs(e, bank=bank, ps_=ps_):
                        ins = None
                        for j in range(8):
                            ri, b4 = divmod(j, 4)
                            ins = e.transpose(bank[:, j * 64:(j + 1) * 64],
                                              in_=v3p[ps_, ri, b4 * 4:(b4 + 1) * 4, :].rearrange("p b g -> p (b g)"),
                                              identity=identf[ps_, ps_])
                        return ins
                    P.op("pe", trs, reads=[RXT[i3], R_const], writes=[rb])
                    eng = evac_eng()
                    P.op(eng, copy_op(eng, xt[io1][:, gh * 512:(gh + 1) * 512], bank[:, :]),
                         reads=[rb], writes=[RXT[io1]])
                for gh in range(2):
                    for j in range(8):
                        ri, b4 = divmod(j, 4)
                        idx = gh * 8 + j
                        dst = (nre_s, nim_s)[ri]
                        for bb in range(4):
                            P.dma("sp", lambda e, dst=dst, gh=gh, b4=b4, bb=bb, idx=idx, io1=io1: e.dma_start(
                                out=dst[b4 * 4 + bb, gh * 32:(gh + 1) * 32, :],
                                in_=xt[io1][bb * 32:(bb + 1) * 32, idx * 64:(idx + 1) * 64]), reads=[RXT[io1]])
                P.op("act", lambda e: e.activation(out=XS[:, :, :, nP:nP + 16], in_=S0f[:], func=AF.Copy),
                     reads=[R_S0f], writes=RS["C"])

            ygT = slot_fm("B")

            def stA(gb, c0, nc):
                ymls = []
                for half in range(2):
                    bank, rb = getbank()

                    def mm(e, half=half, bank=bank):
                        ins = None
                        for gi in range(4):
                            g = gb * 8 + half * 4 + gi
                            gh, g32 = divmod(g, 32)
                            ps_ = slice(gh * 64, (gh + 1) * 64)
                            o = bank[:, gi * 128:gi * 128 + nc]
                            e.matmul(o, lhsT=TM[:, g, :], rhs=Uml[:, g, c0:c0 + nc], start=True, stop=False)
                            e.matmul(o, lhsT=W3[ps_, 0, g32, :], rhs=XS[ps_, 0, g32, c0:c0 + nc],
                                     start=False, stop=False)
                            ins = e.matmul(o, lhsT=W3[ps_, 1, g32, :], rhs=XS[ps_, 1, g32, c0:c0 + nc],
                                           start=False, stop=True)
                        return ins
                    P.op("pe", mm, reads=RS["B"] + RS["C"] + [R_TM, R_W3], writes=[rb])
                    iy = nxt("yml", NYML)
                    eng = evac_eng()
                    P.op(eng, copy_op(eng, yml[iy][:, :, 0:nc],
                                      bank[:, :].rearrange("p (g k) -> p g k", g=4)[:, :, 0:nc]),
                         reads=[rb], writes=[RYML[iy]])
                    ymls.append(iy)
                return ymls

            def stB(ic, ymls, nc):
                bank, rb = getbank()
                bb = bank[:].bitcast(BF16)

                def tr(e):
                    ins = None
                    for half in range(2):
                        for gi in range(4):
                            q0 = (half * 4 + gi) * 128
                            ins = e.transpose(bb[0:nc, q0:q0 + 128], in_=yml[ymls[half]][:, gi, 0:nc],
                                              identity=identb[:, :])
                    return ins
                P.op("pe", tr, reads=[RYML[ymls[0]], RYML[ymls[1]], R_const], writes=[rb])
                P.op("act", lambda e: e.activation(
                    out=ycm[0:nc, ic, :].rearrange("p (j g c) -> p g j c", j=8, g=8),
                    in_=bb[0:nc, :].rearrange("p (g j c) -> p g j c", g=8, j=8), func=AF.Gelu_apprx_tanh),
                    reads=[rb], writes=[RYCM[ic]])

            def stC(ic, gb, c0, nc):
                bank2, rb2 = getbank()
                bb2 = bank2[:].bitcast(BF16)

                def tr2(e):
                    ins = None
                    for j in range(8):
                        ins = e.transpose(bb2[:, j * 128:j * 128 + nc], in_=ycm[0:nc, ic, j * 128:(j + 1) * 128],
                                          identity=identb[0:nc, 0:nc])
                    return ins
                P.op("pe", tr2, reads=[RYCM[ic], R_const], writes=[rb2])
                eng = evac_eng()
                P.op(eng, copy_op(eng, ygT[:, gb, 8 * c0:8 * (c0 + nc)].rearrange("p (k j) -> p j k", j=8),
                                  bb2.rearrange("p (j k) -> p j k", j=8)[:, :, 0:nc]),
                     reads=[rb2], writes=RS["B"])

            items = [(gb, c0, nc) for gb in range(8) for (c0, nc) in csubs]
            ymls_of = {}
            for it_ in range(len(items) + 2):
                if it_ < len(items):
                    ymls_of[it_] = stA(*items[it_])
                if 0 <= it_ - 1 < len(items):
                    stB((it_ - 1) % 2, ymls_of[it_ - 1], items[it_ - 1][2])
                if 0 <= it_ - 2 < len(items):
                    stC((it_ - 2) % 2, *items[it_ - 2])

            def ev_glu(bank, rb, fo, n0, nn):
                it = nxt("tmp", NTMP)
                P.op("act", lambda e: e.activation(out=tmpb[it][:, 0:nn], in_=bank[:, 0:nn], func=AF.Sigmoid,
                                                   bias=bglu[:, fo:fo + 1]),
                     reads=[rb, R_vec], writes=[RTMP[it]])
                P.op("dve", lambda e: e.tensor_tensor(out=ysA[:, fo, n0:n0 + nn], in0=ysA[:, fo, n0:n0 + nn],
                                                      in1=tmpb[it][:, 0:nn], op=ALU.mult),
                     reads=[RTMP[it]], writes=[rh("A", n0)])
                P.op("dve", lambda e: e.tensor_tensor(out=ysA[:, fo, n0:n0 + nn], in0=ysA[:, fo, n0:n0 + nn],
                                                      in1=ygT[:, fo, n0:n0 + nn], op=ALU.mult),
                     reads=[rh("B", n0)], writes=[rh("A", n0)])
            proj(w_glu, 0, "B", ev_glu, "w_glu")

            def ev_bs(bank, rb, fo, n0, nn):
                P.op("dve", lambda e: e.tensor_tensor(out=gsB[:, fo, n0:n0 + nn], in0=bank[:, 0:nn],
                                                      in1=gsB[:, fo, n0:n0 + nn], op=ALU.mult),
                     reads=[rb], writes=[rh("E", n0)])
            proj(w_bs, 0, "A", ev_bs, "w_bs")

            RCX = [Res("cx%d" % i) for i in range(4)]
            Lp = 15 + NTp
            extp = slot_raw("C", 8 * Lp).rearrange("p (a l) -> p a l", a=8)
            if ti == LAST:
                exts = arena[:, SL["B"] + 4224:SL["B"] + 4224 + 8 * 16 * 23].rearrange(
                    "p (a b l) -> p a b l", a=8, b=16)
                for half in range(2):
                    ix = nxt("xt", NXT)
                    P.dma("sp", lambda e, ix=ix, half=half: e.dma_start(
                        out=xt[ix][0:120, :],
                        in_=spool[half * 8:(half + 1) * 8].rearrange("b r f -> (b r) f")), writes=[RXT[ix]])
                    for q in range(2):
                        bank, rb = getbank()

                        def tr(e, ix=ix, q=q, bank=bank):
                            ins = None
                            for f4 in range(4):
                                ft = q * 4 + f4
                                ins = e.transpose(bank[:, f4 * 128:f4 * 128 + 120],
                                                  in_=xt[ix][0:120, ft * 128:(ft + 1) * 128],
                                                  identity=identf[0:120, 0:120])
                            return ins
                        P.op("pe", tr, reads=[RXT[ix], R_const], writes=[rb])
                        eng = evac_eng()
                        P.op(eng, copy_op(
                            eng, bufT[:, q * 4:(q + 1) * 4, half * 8:(half + 1) * 8, :].rearrange("p a b r -> p a (b r)"),
                            bank[:, :].rearrange("p (a t) -> p a t", a=4)[:, :, 0:120]),
                            reads=[rb], writes=RS["B"])
                P.op("dve", lambda e: e.tensor_copy(out=exts[:, :, :, 0:15], in_=bufT[:]),
                     reads=[], writes=RS["B"] + RCX)
                P.dma("sp", lambda e: e.dma_start(out=npool_s[:, 0:7, :], in_=spool[:, 8:15, :]))
            P.op("dve", lambda e: e.tensor_copy(out=extp[:, :, 0:15], in_=carry[:]),
                 reads=[R_carry], writes=RS["C"] + RCX)

            def ev_up(bank, rb, fo, n0, nn):
                wr = [RCX[fo // 2]]
                if n0 + nn <= NTp:
                    eng = evac_eng()
                    P.op(eng, copy_op(eng, extp[:, fo, 15 + n0:15 + n0 + nn], bank[:, 0:nn]),
                         reads=[rb], writes=wr)
                else:
                    assert ti == LAST and n0 == 1024 and nn == 144 and NTp == 1040
                    P.op("act", lambda e: e.activation(out=extp[:, fo, 15 + n0:15 + n0 + 16], in_=bank[:, 0:16],
                                                       func=AF.Copy), reads=[rb], writes=wr)
                    P.op("dve", lambda e: e.tensor_copy(
                        out=exts[:, fo, :, 15:23], in_=bank[:, 16:144].rearrange("p (b t) -> p b t", b=16)),
                        reads=[rb], writes=wr + RS["B"])
                    P.op("act", lambda e: e.activation(out=UPF[:, fo, :], in_=bank[:, 0:144], func=AF.Copy),
                         reads=[rb], writes=[R_UPF])
            up_gen = proj_gen(w_in, 2 * D, "D", ev_up, "w_in")

            pooledA = slot_fm("A")
            T1o, T2o = (SL["B"] + 5184, SL["B"] + 7296) if ti != LAST else (SL["B"], SL["B"] + 2112)

            def pool_group(ext3, L, rows, gi, out3, cnt_fix, fin_views=None):
                w = 2 ** (gi + 1)
                t1 = arena[:, T1o:T1o + rows * L].rearrange("p (a l) -> p a l", a=rows)
                t2 = arena[:, T2o:T2o + rows * L].rearrange("p (a l) -> p a l", a=rows)

                def lvl(dst, src, lo, d):
                    P.op("dve", lambda e: e.tensor_tensor(out=dst[:, :, lo:L], in0=src[:, :, lo:L],
                                                          in1=src[:, :, lo - d:L - d], op=ALU.add),
                         reads=[RCX[gi]] + RS["B"], writes=RS["B"])
                lvl(t1, ext3, 1, 1)
                fin = t1
                if w >= 4:
                    lvl(t2, t1, 3, 2)
                    fin = t2
                if w >= 8:
                    lvl(t1, t2, 7, 4)
                    fin = t1
                if w >= 16:
                    lvl(t2, t1, 15, 8)
                    fin = t2
                if cnt_fix:
                    P.op("dve", lambda e: e.tensor_tensor(
                        out=fin[:, :, 15:31], in0=fin[:, :, 15:31],
                        in1=invc[:, gi, :].unsqueeze(1).broadcast_to([128, rows, 16]), op=ALU.mult),
                        reads=RS["B"] + [R_const], writes=RS["B"])
                if fin_views is None:
                    o_, a_, b_ = out3, fin[:, :, 15:L], ext3[:, :, 15:L]
                else:
                    o_, a_, b_ = fin_views(fin)
                P.op("dve", lambda e: e.scalar_tensor_tensor(
                    out=o_, in0=a_, scalar=1.0 / w, in1=b_,
                    op0=ALU.mult, op1=ALU.subtract), reads=[RCX[gi]] + RS["C"] + RS["B"], writes=RS["A"])

            def pool_gi(gi):
                fs = slice(2 * gi, 2 * gi + 2)
                pool_group(extp[:, fs, :], Lp, 2, gi, pooledA[:, fs, 0:NTp], ti == 0)
                if ti == LAST:
                    pool_group(exts[:, fs, :, :].rearrange("p a b l -> p (a b) l"), 23, 32, gi, None, False,
                               fin_views=lambda fin, fs=fs: (
                                   pooledA[:, fs, NTp:NTp + 128].rearrange("p a (b t) -> p a b t", b=16),
                                   fin.rearrange("p (a b) l -> p a b l", a=2)[:, :, :, 15:23],
                                   exts[:, fs, :, 15:23]))

            ypE = slot_fm("B")

            def pm(gis):
                for (n0, nn) in ntiles:
                    for gi in gis:
                        for fo2 in range(2):
                            fo = 2 * gi + fo2
                            bank, rb = getbank()

                            def mm(e, bank=bank, gi=gi, fo2=fo2, n0=n0, nn=nn):
                                ins = None
                                for k2 in range(2):
                                    ins = e.matmul(bank[:, 0:nn], lhsT=pmw[:, gi, k2, fo2 * 128:(fo2 + 1) * 128],
                                                   rhs=pooledA[:, 2 * gi + k2, n0:n0 + nn],
                                                   start=(k2 == 0), stop=(k2 == 1))
                                return ins
                            P.op("pe", mm, reads=[rh("A", n0), R_pmw], writes=[rb])
                            P.op("act", lambda e, bank=bank, fo=fo, n0=n0, nn=nn: e.activation(
                                out=ypE[:, fo, n0:n0 + nn], in_=bank[:, 0:nn], func=AF.Copy,
                                scale=pscale[:, fo:fo + 1]), reads=[rb, R_vec], writes=[rh("B", n0)])

            nb_blk = 4 * len(ntiles)
            for _ in range(nb_blk):
                next(up_gen)
            pool_gi(0)
            pool_gi(1)
            for _ in up_gen:
                pass
            if ti != LAST:
                P.op("dve", lambda e: e.tensor_copy(out=carry[:], in_=extp[:, :, NTp:NTp + 15]),
                     reads=RS["C"] + RCX, writes=[R_carry])
                pm([0, 1])
                pool_gi(2)
                pool_gi(3)
                pm([2, 3])
            else:
                pool_gi(2)
                pool_gi(3)
                pm([0, 1, 2, 3])

            def ev_zp(bank, rb, fo, n0, nn):
                it = nxt("tmp", NTMP)
                P.op("act", lambda e: e.activation(out=tmpb[it][:, 0:nn], in_=bank[:, 0:nn], func=AF.Silu),
                     reads=[rb], writes=[RTMP[it]])
                P.op("dve", lambda e: e.tensor_tensor(out=ypE[:, fo, n0:n0 + nn], in0=ypE[:, fo, n0:n0 + nn],
                                                      in1=tmpb[it][:, 0:nn], op=ALU.mult),
                     reads=[RTMP[it]], writes=[rh("B", n0)])
            proj(w_in, 3 * D, "D", ev_zp, "w_in")

            gpA = slot_fm("A")

            def ev_gp(bank, rb, fo, n0, nn):
                P.op("act", lambda e: e.activation(out=gpA[:, fo, n0:n0 + nn], in_=bank[:, 0:nn], func=AF.Sigmoid,
                                                   bias=bgate[:, 8 + fo:9 + fo]),
                     reads=[rb, R_vec], writes=[rh("A", n0)])
            proj(w_in, 5 * D, "D", ev_gp, "w_in")

            def ev_bp(bank, rb, fo, n0, nn):
                it = nxt("tmp", NTMP)
                P.op("dve", lambda e: e.tensor_tensor(out=tmpb[it][:, 0:nn], in0=bank[:, 0:nn],
                                                      in1=gpA[:, fo, n0:n0 + nn], op=ALU.mult),
                     reads=[rb, rh("A", n0)], writes=[RTMP[it]])
                P.op("dve", lambda e: e.tensor_tensor(out=gsB[:, fo, n0:n0 + nn], in0=gsB[:, fo, n0:n0 + nn],
                                                      in1=tmpb[it][:, 0:nn], op=ALU.add),
                     reads=[RTMP[it]], writes=[rh("E", n0)])
            bp_gen = proj_gen(w_bp, 0, "B", ev_bp, "w_bp")
            if ti + 1 < len(TILES):
                n_gen = phaseN_gen(ti + 1, TILES[ti + 1][0], TILES[ti + 1][1], [])
            else:
                n_gen = iter(())
            for ib, _ in enumerate(bp_gen):
                if ib % 2 == 1:
                    next(n_gen, None)
            for _ in n_gen:
                pass

            if ti == LAST:
                for q in range(2):
                    bank, rb = getbank()

                    def tr(e, q=q, bank=bank):
                        ins = None
                        for f4 in range(4):
                            ft = q * 4 + f4
                            ins = e.transpose(bank[:, f4 * 128:(f4 + 1) * 128], in_=UPF[:, ft, 16:144],
                                              identity=identf[:, :])
                        return ins
                    P.op("pe", tr, reads=[R_UPF, R_const], writes=[rb])
                    ix = nxt("xt", NXT) if q == 0 else ix
                    eng = evac_eng()
                    P.op(eng, copy_op(eng, xt[ix][:, q * 512:(q + 1) * 512], bank[:, :]),
                         reads=[rb], writes=[RXT[ix]])
                for b in range(16):
                    P.dma("sp", lambda e, b=b, ix=ix: e.dma_start(out=npool_s[b, 7:15, :], in_=xt[ix][b * 8:(b + 1) * 8, :]),
                          reads=[RXT[ix]])
                for q in range(2):
                    bank, rb = getbank()

                    def tr(e, q=q, bank=bank):
                        ins = None
                        for f4 in range(4):
                            ft = q * 4 + f4
                            ins = e.transpose(bank[0:16, f4 * 128:(f4 + 1) * 128], in_=UPF[:, ft, 0:16],
                                              identity=identf[:, :])
                        return ins
                    P.op("pe", tr, reads=[R_UPF, R_const], writes=[rb])
                    ix2 = nxt("xt", NXT) if q == 0 else ix2
                    eng = evac_eng()
                    P.op(eng, copy_op(eng, xt[ix2][0:16, q * 512:(q + 1) * 512], bank[0:16, :]),
                         reads=[rb], writes=[RXT[ix2]])
                P.dma("sp", lambda e, ix2=ix2: e.dma_start(out=npool_p[:, :], in_=xt[ix2][1:16, :]), reads=[RXT[ix2]])

            wv0, rw0 = load_w(w_out, 0, "w_out")
            wv1, rw1 = load_w(w_out, 512, "w_out")
            wvs = [(wv0, rw0), (wv1, rw1)]
            mB = slot_fm("E")
            if ti == 0:
                rowt = [(16 + 128 * i, min(128, 1024 - 16 - 128 * i)) for i in range(8)]
            else:
                rowt = [(1024 + 128 * i, 128) for i in range(8)] + [(2048, 16), (2064, 128)]
            res_ix = {}

            def issue_res_load(i_):
                if i_ < len(rowt) and i_ not in res_ix:
                    tok_, rows_ = rowt[i_]
                    jx = nxt("xt", NXT)
                    res_ix[i_] = jx
                    P.dma("sp", lambda e, jx=jx, tok_=tok_, rows_=rows_: e.dma_start(
                        out=xt[jx][0:rows_, :], in_=xall[tok_:tok_ + rows_, :]), writes=[RXT[jx]])
            for i_rt, (tok0, rows) in enumerate(rowt):
                c0 = tok0 - T0
                issue_res_load(i_rt)
                issue_res_load(i_rt + 1)
                ix = res_ix[i_rt]
                for h in range(2):
                    bank, rb = getbank()
                    wv, rw = wvs[h]

                    def mm(e, bank=bank, wv=wv, c0=c0, rows=rows):
                        ins = None
                        for kt in range(8):
                            ins = e.matmul(bank[0:rows, 0:512], lhsT=mB[:, kt, c0:c0 + rows], rhs=wv[:, kt, :],
                                           start=(kt == 0), stop=(kt == 7))
                        return ins
                    P.op("pe", mm, reads=[rh("E", c0), rh("E", c0 + rows - 1), rw], writes=[rb])
                    P.op("dve", lambda e, bank=bank, ix=ix, h=h, rows=rows: e.tensor_tensor(
                        out=xt[ix][0:rows, h * 512:(h + 1) * 512], in0=bank[0:rows, 0:512],
                        in1=xt[ix][0:rows, h * 512:(h + 1) * 512], op=ALU.add), reads=[rb], writes=[RXT[ix]])
                si = nxt("stat", 4)
                jq = nxt("xn", 2)
                P.op("act", lambda e, ix=ix, rows=rows, si=si, jq=jq: e.activation(
                    out=xn[jq][0:rows, :], in_=xt[ix][0:rows, :], func=AF.Square,
                    accum_out=stat[0:rows, 2 * si:2 * si + 1]), reads=[RXT[ix]], writes=[RXN[jq], RSTAT[si]])
                P.op("act", lambda e, rows=rows, si=si: e.activation(
                    out=stat[0:rows, 2 * si + 1:2 * si + 2], in_=stat[0:rows, 2 * si:2 * si + 1],
                    func=AF.Sqrt, scale=1.0 / D, bias=EPS), reads=[RSTAT[si]], writes=[RSTAT[si]])
                P.op("dve", lambda e, rows=rows, si=si: e.reciprocal(
                    out=stat[0:rows, 2 * si:2 * si + 1], in_=stat[0:rows, 2 * si + 1:2 * si + 2]),
                    reads=[RSTAT[si]], writes=[RSTAT[si]])
                P.op("dve", lambda e, ix=ix, rows=rows, si=si: e.scalar_tensor_tensor(
                    out=xt[ix][0:rows, :], in0=xt[ix][0:rows, :], scalar=stat[0:rows, 2 * si:2 * si + 1],
                    in1=fB[0:rows, :], op0=ALU.mult, op1=ALU.mult),
                    reads=[RSTAT[si], R_fB], writes=[RXT[ix]])
                if tok0 < NPROMPT:
                    dst = y_p[tok0 - 16:tok0 - 16 + rows, :]
                else:
                    dst = y_s[tok0 - NPROMPT:tok0 - NPROMPT + rows, :]
                P.dma("sp", lambda e, ix=ix, rows=rows, dst=dst: e.dma_start(out=dst, in_=xt[ix][0:rows, :]),
                      reads=[RXT[ix]])

        for ti_, (T0_, NT_) in enumerate(TILES):
            do_tile(ti_, T0_, NT_)
        P.emit()
    return nc


_CACHE = {}


def _consts():
    ident = np.eye(128, dtype=np.float32)
    s_idx = np.arange(128) // 16
    mask = (s_idx[None, :] >= s_idx[:, None]).astype(np.float32)
    nvals = np.broadcast_to(np.arange(1, 9, dtype=np.float32)[None, None, :], (128, 32, 8)).copy()
    invc = np.zeros((128, 4, 16), np.float32)
    for gi in range(4):
        w = 2 ** (gi + 1)
        for pos in range(16):
            invc[:, gi, pos] = w / min(pos + 1, w)
    return ident, mask, nvals, invc


def kernel(x_prompt, x_sample, state_ssm_re, state_ssm_im, state_pool, meta_tokens,
           norm_gain, w_in, b_gate, ssm_a_re, ssm_a_im, ssm_log_dt, ssm_b_re, ssm_b_im,
           ssm_c_re, ssm_c_im, ssm_d, w_glu, b_glu, pool_mix, pool_scale,
           w_branch_ssm, w_branch_pool, w_out, final_norm_gain):
    f = lambda a: np.ascontiguousarray(np.asarray(a, dtype=np.float32))
    x_prompt, x_sample = f(x_prompt), f(x_sample)
    meta = f(meta_tokens)
    if "nc" not in _CACHE:
        _CACHE["nc"] = build_program()
    nc = _CACHE["nc"]
    ident, mask, nvals, invc = _consts()
    shared = {
        "w_in": f(w_in[0]), "w_glu": f(w_glu[0]), "pool_mix": f(pool_mix[0]), "w_bs": f(w_branch_ssm[0]),
        "w_bp": f(w_branch_pool[0]), "w_out": f(w_out[0]), "norm_gain": f(norm_gain[0]), "b_gate": f(b_gate[0]),
        "ssm_d": f(ssm_d[0]), "b_glu": f(b_glu[0]), "pool_scale": f(pool_scale[0]), "fgain": f(final_norm_gain),
        "a_re": f(ssm_a_re[0]), "a_im": f(ssm_a_im[0]), "log_dt": f(ssm_log_dt[0]),
        "b_re": f(ssm_b_re[0]), "b_im": f(ssm_b_im[0]), "c_re": f(ssm_c_re[0]), "c_im": f(ssm_c_im[0]),
        "c_ident": ident, "c_mask": mask, "c_nvals": nvals, "c_invc": invc,
    }
    in_maps = []
    for c in range(NCORES):
        m = dict(shared)
        m["xall"] = np.ascontiguousarray(np.concatenate(
            [meta, x_prompt[c], x_sample[16 * c:16 * (c + 1)].reshape(128, D)], axis=0))
        m["s0re"] = f(state_ssm_re[0, 16 * c:16 * (c + 1)])
        m["s0im"] = f(state_ssm_im[0, 16 * c:16 * (c + 1)])
        m["spool"] = f(state_pool[0, 16 * c:16 * (c + 1)])
        in_maps.append(m)
    res = run_bass_kernel_spmd(nc, in_maps, core_ids=list(range(NCORES)))
    R = res.results
    y_prompt = np.stack([R[c]["y_p"] for c in range(NCORES)], axis=0)
    y_sample = np.concatenate([R[c]["y_s"].reshape(16, 8, D) for c in range(NCORES)], axis=0)
    nre_p = np.stack([R[c]["nre_p"] for c in range(NCORES)], axis=0)[None]
    nim_p = np.stack([R[c]["nim_p"] for c in range(NCORES)], axis=0)[None]
    npool_p = np.stack([R[c]["npool_p"] for c in range(NCORES)], axis=0)[None]
    nre_s = np.concatenate([R[c]["nre_s"] for c in range(NCORES)], axis=0)[None]
    nim_s = np.concatenate([R[c]["nim_s"] for c in range(NCORES)], axis=0)[None]
    npool_s = np.concatenate([R[c]["npool_s"] for c in range(NCORES)], axis=0)[None]
    return (y_prompt.astype(np.float32), y_sample.astype(np.float32), nre_p.astype(np.float32),
            nim_p.astype(np.float32), npool_p.astype(np.float32), nre_s.astype(np.float32),
            nim_s.astype(np.float32), npool_s.astype(np.float32))
```

```python
import contextlib
import math
import numpy as np
import concourse.bass as bass
import concourse.mybir as mybir
from concourse.bass_utils import run_bass_kernel_spmd

F32 = mybir.dt.float32
BF16 = mybir.dt.bfloat16
I32 = mybir.dt.int32
ALU = mybir.AluOpType
AF = mybir.ActivationFunctionType

NCORES = 8
D = 1024
NPROMPT = 2064
NTOT = 2192
SLOT = 9408
NTM = 1168
KU = 146
KX = 147
TILES = [(0, 1024), (1024, 1168)]
LAST = len(TILES) - 1
EPS = 1e-6


class Res:
    __slots__ = ("w", "r", "name", "excl")

    def __init__(self, name="", excl=False):
        self.w = None
        self.r = []
        self.name = name
        self.excl = excl


class Op:
    __slots__ = ("eng", "fn", "deps", "pos", "sigidx", "is_dma", "sem", "semval", "needed")

    def __init__(self, eng, fn, is_dma):
        self.eng = eng
        self.fn = fn
        self.deps = []
        self.pos = -1
        self.sigidx = None
        self.is_dma = is_dma
        self.sem = None
        self.semval = None
        self.needed = False


HAZ = 2


class Prog:
    ENGS = ["pe", "act", "dve", "pool", "sp"]

    def __init__(self, nc, n_dma_sems=14):
        self.nc = nc
        self.q = {e: [] for e in self.ENGS}
        self.n_dma_sems = n_dma_sems

    def _mk(self, eng, fn, reads, writes, deps, is_dma):
        op = Op(eng, fn, is_dma)
        ds = list(deps)
        if any(r.excl for r in reads):
            writes = list(writes) + [r for r in reads if r.excl]
            reads = [r for r in reads if not r.excl]
        for r in reads:
            if r.w is not None:
                ds.append(r.w)
        for w in writes:
            if w.w is not None:
                ds.append(w.w)
            ds.extend(w.r)
        best = {}
        dmas = []
        seen = set()
        for d in ds:
            if d is None or id(d) in seen:
                continue
            seen.add(id(d))
            if d.is_dma:
                dmas.append(d)
            else:
                b = best.get(d.eng)
                if b is None or d.pos > b.pos:
                    best[d.eng] = d
        op.deps = dmas + list(best.values())
        for r in reads:
            r.r.append(op)
            if len(r.r) > 64:
                keep = {}
                kd = []
                for x in r.r:
                    if x.is_dma:
                        kd.append(x)
                    elif x.eng not in keep or x.pos > keep[x.eng].pos:
                        keep[x.eng] = x
                r.r = kd + list(keep.values())
        for w in writes:
            w.w = op
            w.r = []
        op.pos = len(self.q[eng])
        self.q[eng].append(op)
        return op

    def op(self, eng, fn, reads=(), writes=(), deps=()):
        return self._mk(eng, fn, reads, writes, deps, False)

    def dma(self, eng, fn, reads=(), writes=(), deps=()):
        return self._mk(eng, fn, reads, writes, deps, True)

    def emit(self):
        nc = self.nc
        for e in self.ENGS:
            for op in self.q[e]:
                for d in op.deps:
                    if d.is_dma or d.eng != op.eng:
                        d.needed = True
                    elif d.eng != "pe" and (op.pos - d.pos) <= HAZ:
                        d.needed = True
        for e in self.ENGS:
            c = 0
            for op in self.q[e]:
                if (not op.is_dma) and op.needed:
                    c += 1
                    op.sigidx = c
        with contextlib.ExitStack() as st:
            esem = {e: st.enter_context(nc.semaphore("s_" + e)) for e in self.ENGS}
            dsems = {}
            for e in self.ENGS:
                if any(o.is_dma for o in self.q[e]):
                    dsems[e] = [st.enter_context(nc.semaphore("d_%s_%d" % (e, i)))
                                for i in range(self.n_dma_sems)]
            for e, sems in dsems.items():
                cnt = [0] * len(sems)
                prev = [None] * len(sems)
                i = 0
                for op in self.q[e]:
                    if op.is_dma:
                        k = i % len(sems)
                        cnt[k] += 1
                        op.sem = sems[k]
                        op.semval = 16 * cnt[k]
                        if prev[k] is not None:
                            op.deps.append(prev[k])
                        prev[k] = op
                        i += 1
            block = st.enter_context(nc.Block())
            handles = {"pe": block.tensor, "act": block.scalar, "dve": block.vector,
                       "pool": block.gpsimd, "sp": block.sync}

            def mk(e):
                ops = self.q[e]

                def body(eng):
                    waited = {}
                    for op in ops:
                        for d in op.deps:
                            if d.is_dma:
                                key = ("d", d.sem.name)
                                val = d.semval
                                sem = d.sem
                            else:
                                if d.eng == op.eng:
                                    if d.eng == "pe" or (op.pos - d.pos) > HAZ:
                                        continue
                                key = ("e", d.eng)
                                val = d.sigidx
                                sem = esem[d.eng]
                            if waited.get(key, 0) >= val:
                                continue
                            waited[key] = val
                            eng.wait_ge(sem, val)
                        ins = op.fn(eng)
                        if op.is_dma:
                            ins.then_inc(op.sem, 16)
                        elif op.needed:
                            ins.then_inc(esem[op.eng], 1)
                    if e in dsems:
                        last = {}
                        for op in ops:
                            if op.is_dma:
                                last[op.sem.name] = op
                        for op in last.values():
                            if waited.get(("d", op.sem.name), 0) < op.semval:
                                eng.wait_ge(op.sem, op.semval)
                return body

            for e in self.ENGS:
                if self.q[e]:
                    handles[e](mk(e))


def build_program():
    nc = bass.Bass("TRN2", target_bir_lowering=False)

    def din(name, shape):
        return nc.dram_tensor(name, list(shape), F32, kind="ExternalInput").ap()

    def dout(name, shape):
        return nc.dram_tensor(name, list(shape), F32, kind="ExternalOutput").ap()

    xall = din("xall", [NTOT, D])
    s0re = din("s0re", [16, 64, 64])
    s0im = din("s0im", [16, 64, 64])
    spool = din("spool", [16, 15, D])
    w_in = din("w_in", [D, 6 * D])
    w_glu = din("w_glu", [D, D])
    pool_mix = din("pool_mix", [4, 256, 256])
    w_bs = din("w_bs", [D, D])
    w_bp = din("w_bp", [D, D])
    w_out = din("w_out", [D, D])
    norm_gain = din("norm_gain", [D])
    b_gate = din("b_gate", [2 * D])
    ssm_d = din("ssm_d", [D])
    b_glu = din("b_glu", [D])
    pool_scale = din("pool_scale", [D])
    fgain = din("fgain", [D])
    a_re = din("a_re", [64, 64])
    a_im = din("a_im", [64, 64])
    log_dt = din("log_dt", [64])
    b_re = din("b_re", [64, 64, 16])
    b_im = din("b_im", [64, 64, 16])
    c_re = din("c_re", [64, 16, 64])
    c_im = din("c_im", [64, 16, 64])
    c_ident = din("c_ident", [128, 128])
    c_mask = din("c_mask", [128, 128])
    c_nvals = din("c_nvals", [128, 32, 8])
    c_invc = din("c_invc", [128, 4, 16])

    y_p = dout("y_p", [2048, D])
    y_s = dout("y_s", [128, D])
    nre_p = dout("nre_p", [64, 64])
    nim_p = dout("nim_p", [64, 64])
    npool_p = dout("npool_p", [15, D])
    nre_s = dout("nre_s", [16, 64, 64])
    nim_s = dout("nim_s", [16, 64, 64])
    npool_s = dout("npool_s", [16, 15, D])

    with contextlib.ExitStack() as st:
        def sb(name, shape, dt):
            return st.enter_context(nc.sbuf_tensor(name, list(shape), dt))

        P = Prog(nc)
        NC = True

        arena = sb("arena", [128, 5 * SLOT], BF16)
        SL = {k: i * SLOT for i, k in enumerate("ABCED")}
        RS = {k: [Res(k + "0"), Res(k + "1"), Res(k + "2")] for k in "ABCDE"}

        def slot_fm(k):
            o = SL[k]
            return arena[:, o:o + 8 * NTM].rearrange("p (a n) -> p a n", a=8)

        def slot_raw(k, n=SLOT):
            o = SL[k]
            return arena[:, o:o + n]

        wbuf = sb("wbuf", [128, 2, 8, 512], BF16)
        RW = [Res("w0"), Res("w1")]
        W1 = sb("W1", [128, 64, 128], BF16)
        TM = sb("TM", [128, 64, 128], BF16)
        W3 = sb("W3", [128, 2, 32, 128], BF16)
        R_W1, R_TM, R_W3 = Res("W1"), Res("TM"), Res("W3")
        pmw = sb("pmw", [128, 4, 2, 256], BF16)
        R_pmw = Res("pmw")
        gB = sb("gB", [128, D], F32)
        fB = sb("fB", [128, D], F32)
        R_gB, R_fB = Res("gB"), Res("fB")
        NXT = 3
        xt = [sb("xt%d" % i, [128, D], F32) for i in range(NXT)]
        RXT = [Res("xt%d" % i) for i in range(NXT)]
        xn = [sb("xn%d" % i, [128, D], BF16) for i in range(2)]
        RXN = [Res("xn0"), Res("xn1")]
        NTMP = 2
        tmpb = [sb("tmpb%d" % i, [128, 512], BF16) for i in range(NTMP)]
        RTMP = [Res("tmp%d" % i) for i in range(NTMP)]
        NYML = 4
        yml = [sb("yml%d" % i, [128, 4, 128], BF16) for i in range(NYML)]
        RYML = [Res("yml%d" % i) for i in range(NYML)]
        ycm = sb("ycm", [128, 2, 1024], BF16)
        RYCM = [Res("ycm0"), Res("ycm1")]
        identf = sb("identf", [128, 128], F32)
        identb = sb("identb", [128, 128], BF16)
        maskf = sb("maskf", [128, 128], F32)
        invc = sb("invc", [128, 4, 16], F32)
        R_const = Res("const")
        vecs = sb("vecs", [128, 32], F32)
        bgate = vecs[:, 0:16]
        bglu = vecs[:, 16:24]
        pscale = vecs[:, 24:32]
        NSTG = 2
        stg = sb("stg", [128, NSTG, 128], F32)
        RSTG = [Res("stg%d" % i) for i in range(NSTG)]
        R_vec = Res("vec")
        Dt = sb("Dt", [128, 64], F32)
        ArAr = sb("ArAr", [128, 2, 32], F32)
        AiPM = sb("AiPM", [128, 2, 32], F32)
        ArAr2 = sb("ArAr2", [128, 2, 32], F32)
        AiPM2 = sb("AiPM2", [128, 2, 32], F32)
        R_A8 = Res("A8")
        Sf = [sb("Sf%d" % i, [128, 2, 32], F32) for i in range(2)]
        RSF = [Res("Sf0"), Res("Sf1")]
        st1 = sb("st1", [128, 2, 32], F32)
        st2 = sb("st2", [128, 2, 32], F32)
        R_st1, R_st2 = Res("st1"), Res("st2")
        R_XSx = Res("XSx")
        S0f = sb("S0f", [128, 2, 32, 16], F32)
        R_S0f = Res("S0f")
        UPF = sb("UPF", [128, 8, 144], F32)
        R_UPF = Res("UPF")
        bufT = arena[:, SL["B"] + 7168:SL["B"] + 7168 + 1920].rearrange("p (a b r) -> p a b r", a=8, b=16)
        carry = sb("carry", [128, 8, 15], BF16)
        R_carry = Res("carry")
        stat = sb("stat", [128, 8], F32)
        RSTAT = [Res("stat%d" % i) for i in range(4)]

        NB = 8
        psf = [st.enter_context(nc.psum_tensor("ps%d" % i, [128, 512], F32)) for i in range(NB)]
        RB = [Res("bank%d" % i, excl=True) for i in range(NB)]
        bank_ctr = [0]

        def getbank():
            i = bank_ctr[0] % NB
            bank_ctr[0] += 1
            return psf[i], RB[i]

        rr = {"xt": 0, "xn": 0, "tmp": 0, "yml": 0, "w": 0, "ev": 0, "stat": 0, "stg": 0}

        def nxt(key, n):
            i = rr[key] % n
            rr[key] += 1
            return i

        def evac_eng():
            rr["ev"] += 1
            return "act" if rr["ev"] % 2 == 0 else "dve"

        def copy_op(eng, out, in_):
            if eng == "act":
                return lambda e: e.activation(out=out, in_=in_, func=AF.Copy)
            return lambda e: e.tensor_copy(out=out, in_=in_)

        P.dma("sp", lambda e: e.dma_start(out=identf[:], in_=c_ident), writes=[R_const])
        P.dma("sp", lambda e: e.dma_start(out=maskf[:], in_=c_mask), writes=[R_const])
        P.dma("sp", lambda e: e.dma_start(out=invc[:], in_=c_invc), writes=[R_const])
        P.dma("sp", lambda e: e.dma_start(out=gB[:], in_=norm_gain.partition_broadcast(128)), writes=[R_gB])
        P.dma("sp", lambda e: e.dma_start(out=fB[:], in_=fgain.partition_broadcast(128)), writes=[R_fB])
        P.op("dve", lambda e: e.memset(stg[:], 0.0), writes=RSTG)

        def stage_T(loads, K, ncols, evacs, tag=""):
            k = nxt("stg", NSTG)
            for (sl, src) in loads:
                P.dma("sp", lambda e, sl=sl, src=src: e.dma_start(out=sl(k), in_=src), writes=[RSTG[k]])
            bank, rb = getbank()
            P.op("pe", lambda e: e.transpose(bank[:, 0:K], in_=stg[0:K, k, :], identity=identf[0:K, 0:K]),
                 reads=[RSTG[k], R_const], writes=[rb])
            for (dst, srcf, wr) in evacs:
                eng = evac_eng()
                P.op(eng, copy_op(eng, dst, srcf(bank)), reads=[rb], writes=wr)

        stage_T([(lambda k: stg[0:16, k, :], b_gate.rearrange("(a p) -> a p", p=128)),
                 (lambda k: stg[16:24, k, :], b_glu.rearrange("(a p) -> a p", p=128)),
                 (lambda k: stg[24:32, k, :], pool_scale.rearrange("(a p) -> a p", p=128))],
                32, 128, [(vecs[:, :], lambda bank: bank[:, 0:32], [R_vec])], tag="v")
        P.op("dve", lambda e: e.tensor_copy(out=identb[:], in_=identf[:]), reads=[R_const], writes=[R_const])

        def f32view(off_bf16, shape):
            n = int(np.prod(shape))
            v = arena[:, off_bf16:off_bf16 + 2 * n].bitcast(F32)
            return v

        W3f_off = 0
        H_off = 16384
        sm_off = 32768
        W3f = W3[:].rearrange("p r g (j c) -> p r g j c", j=8)
        Hf = arena[:, H_off:H_off + 8192].rearrange("p (r g j c) -> p r g j c", r=2, g=32, j=8)
        R_W3f, R_Hf = R_W3, Res("Hf")
        smp = [sm_off]

        def small(shape):
            n = int(np.prod(shape))
            v = f32view(smp[0], [n])
            smp[0] += 2 * n
            return v

        def small3(a, b):
            return small([a * b]).rearrange("p (a b) -> p a b", a=a)

        are = small([32]); aim = small([32]); ldt = small([32]); dtt = small([32])
        dtar = small([32]); th = small([32])
        upf_flat = UPF[:].rearrange("p a b -> p (a b)")

        def as_s3(ap2d):
            v = ap2d if ap2d.dtype == F32 else ap2d.bitcast(F32)
            return v.rearrange("p (a b) -> p a b", a=32)
        ang = as_s3(yml[0][:].rearrange("p a b -> p (a b)"))
        ang2 = as_s3(yml[1][:].rearrange("p a b -> p (a b)"))
        marg = as_s3(yml[2][:].rearrange("p a b -> p (a b)"))
        magp = as_s3(yml[3][:].rearrange("p a b -> p (a b)"))
        magn = as_s3(tmpb[0][:])
        yk = as_s3(tmpb[1][:])
        kf = as_s3(upf_flat[:, 0:256])
        rs = as_s3(upf_flat[:, 256:512])
        rc = small3(32, 8)
        sn = small3(32, 8); cs = small3(32, 8)
        Pre = small3(32, 8); Pim = small3(32, 8); Nre = small3(32, 8); Nim = small3(32, 8)
        nr = small([32]); den = small([32]); rden = small([32])
        qre = small([32]); qim = small([32]); tq1 = small([32]); tq2 = small([32])
        nvals = upf_flat[:, 512:768].rearrange("p (a b) -> p a b", a=32)
        ki = upf_flat[:, 768:1024].bitcast(I32).rearrange("p (a b) -> p a b", a=32)

        def xth(i, part):
            return xt[i][:, part * 512:(part + 1) * 512].rearrange("p (a b) -> p a b", a=32)
        Bre = xth(0, 0); Bim = xth(0, 1); Cre = xth(1, 0); Cim = xth(1, 1); Bbre = xth(2, 0); Bbim = xth(2, 1)
        wflat = wbuf[:].rearrange("p a k n -> p (a k n)").bitcast(F32)
        t4a = wflat[:, 0:2048].rearrange("p (g j c) -> p g j c", g=32, j=4)
        t4b = wflat[:, 2048:4096].rearrange("p (g j c) -> p g j c", g=32, j=4)
        assert smp[0] <= 4 * SLOT, smp[0]
        R_sx = [RXT[0], RXT[1], RXT[2], R_UPF, RW[0], RW[1]] + RYML + RTMP
        R_s = Res("setup_small")

        for (src, dstv) in ((a_re, are), (a_im, aim)):
            stage_T([(lambda k: stg[0:64, k, 0:64], src), (lambda k: stg[0:64, k, 64:128], src)], 64, 128,
                    [(dstv[0:64, :], lambda bank: bank[0:64, 0:32], [R_s]),
                     (dstv[64:128, :], lambda bank: bank[64:128, 32:64], [R_s])], tag="a")
        for gh in range(2):
            ps_ = slice(gh * 64, (gh + 1) * 64)
            gs_ = slice(gh * 32, (gh + 1) * 32)
            P.dma("sp", lambda e, ps_=ps_, gs_=gs_: e.dma_start(
                out=ldt[ps_, :], in_=log_dt[gs_].partition_broadcast(64)), writes=[R_s])
            P.dma("sp", lambda e, ps_=ps_, gs_=gs_: e.dma_start(
                out=Bre[ps_, :, :], in_=b_re[gs_].rearrange("g p c -> p g c")), writes=[R_s, RXT[0]])
            P.dma("sp", lambda e, ps_=ps_, gs_=gs_: e.dma_start(
                out=Bim[ps_, :, :], in_=b_im[gs_].rearrange("g p c -> p g c")), writes=[R_s, RXT[0]])
        for r in range(8):
            gh = r // 4
            ps_ = slice(gh * 64, (gh + 1) * 64)
            for (src, dstC) in ((c_re, Cre), (c_im, Cim)):
                stage_T([(lambda k, ps_=ps_: stg[:, k, ps_],
                          src.rearrange("g c p -> (g c) p")[128 * r:128 * (r + 1), :])], 128, 128,
                        [(dstC[ps_, (r % 4) * 8:(r % 4) * 8 + 8, :].rearrange("p g c -> p (g c)"),
                          lambda bank, ps_=ps_: bank[ps_, 0:128], [R_s, RXT[1]])], tag="c")
        P.dma("sp", lambda e: e.dma_start(out=nvals, in_=c_nvals), writes=[R_s, R_UPF])
        stage_T([(lambda k: stg[0:64, k, :].rearrange("g (s c) -> g s c", s=8),
                  ssm_d.rearrange("(g c) -> g c", c=16).unsqueeze(1).broadcast_to([64, 8, 16]))], 64, 128,
                [(Dt[:, :], lambda bank: bank[:, 0:64], [R_vec])], tag="d")
        def load_S0(S0f, R_S0f):
            for ri, src in enumerate((s0re, s0im)):
                for r in range(8):
                    rows = src.rearrange("b g p -> (b g) p")[128 * r:128 * (r + 1), :]
                    stage_T([(lambda k: stg[:, k, 0:64], rows), (lambda k: stg[:, k, 64:128], rows)], 128, 128,
                            [(S0f[0:64, ri, :, 2 * r:2 * r + 2].rearrange("p g b -> p b g"),
                              lambda bank: bank[0:64, 0:128].rearrange("p (b g) -> p b g", b=2)[:, :, 0:32], [R_S0f]),
                             (S0f[64:128, ri, :, 2 * r:2 * r + 2].rearrange("p g b -> p b g"),
                              lambda bank: bank[64:128, 0:128].rearrange("p (b g) -> p b g", b=2)[:, :, 32:64],
                              [R_S0f])])
        P.dma("pool", lambda e: e.dma_start(out=pmw[:], in_=pool_mix.rearrange("g (k p) n -> p g k n", p=128)),
              writes=[R_pmw])

        def phaseN_gen(ti, T0, NT, first_deps, xt_ids=None):
            xnT = slot_fm("D")
            for r0 in range(0, NT, 128):
                rows = min(128, NT - r0)
                ix = nxt("xt", NXT) if xt_ids is None else xt_ids[(r0 // 128) % len(xt_ids)]
                P.dma("sp", lambda e, ix=ix, r0=r0, rows=rows: e.dma_start(
                    out=xt[ix][0:rows, :], in_=xall[T0 + r0:T0 + r0 + rows, :]), writes=[RXT[ix]])
                si = nxt("stat", 4)
                jn = nxt("xn", 2)
                P.op("act", lambda e, ix=ix, rows=rows, si=si, jn=jn: e.activation(
                    out=xn[jn][0:rows, :], in_=xt[ix][0:rows, :], func=AF.Square,
                    accum_out=stat[0:rows, 2 * si:2 * si + 1]), reads=[RXT[ix]], writes=[RXN[jn], RSTAT[si]])
                P.op("act", lambda e, rows=rows, si=si: e.activation(
                    out=stat[0:rows, 2 * si + 1:2 * si + 2], in_=stat[0:rows, 2 * si:2 * si + 1],
                    func=AF.Sqrt, scale=1.0 / D, bias=EPS), reads=[RSTAT[si]], writes=[RSTAT[si]])
                P.op("dve", lambda e, rows=rows, si=si: e.reciprocal(
                    out=stat[0:rows, 2 * si:2 * si + 1], in_=stat[0:rows, 2 * si + 1:2 * si + 2]),
                    reads=[RSTAT[si]], writes=[RSTAT[si]])
                P.op("dve", lambda e, ix=ix, jn=jn, rows=rows, si=si: e.scalar_tensor_tensor(
                    out=xn[jn][0:rows, :], in0=xt[ix][0:rows, :], scalar=stat[0:rows, 2 * si:2 * si + 1],
                    in1=gB[0:rows, :], op0=ALU.mult, op1=ALU.mult),
                    reads=[RXT[ix], RSTAT[si], R_gB], writes=[RXN[jn]])
                bank, rb = getbank()
                bb = bank[:].bitcast(BF16)

                def trn(e, jn=jn, rows=rows, bb=bb):
                    ins = None
                    for kt in range(8):
                        ins = e.transpose(bb[:, kt * 128:kt * 128 + rows], in_=xn[jn][0:rows, kt * 128:(kt + 1) * 128],
                                          identity=identb[0:rows, 0:rows])
                    return ins
                P.op("pe", trn, reads=[RXN[jn], R_const], writes=[rb])
                eng = evac_eng()
                P.op(eng, copy_op(eng, xnT[:, :, r0:r0 + rows],
                                  bb.rearrange("p (k t) -> p k t", k=8)[:, :, 0:rows]),
                     reads=[rb], writes=[RS["D"][r0 // 512]], deps=first_deps)
                yield


        def S(eng, fn):
            return P.op(eng, fn, reads=[R_s, R_vec], writes=[R_s] + R_sx)

        def bc8(v):
            return v.unsqueeze(2).broadcast_to([128, 32, 8])

        TWO_PI = 2.0 * math.pi
        S("act", lambda e: e.activation(out=dtt, in_=ldt, func=AF.Exp))
        S("dve", lambda e: e.tensor_tensor(out=dtar, in0=dtt, in1=are, op=ALU.mult))
        S("dve", lambda e: e.tensor_tensor(out=th, in0=dtt, in1=aim, op=ALU.mult))
        S("dve", lambda e: e.tensor_tensor(out=ang, in0=nvals, in1=bc8(th), op=ALU.mult))
        S("dve", lambda e: e.tensor_tensor(out=marg, in0=nvals, in1=bc8(dtar), op=ALU.mult))
        S("act", lambda e: e.activation(out=magp, in_=marg, func=AF.Exp))
        S("act", lambda e: e.activation(out=magn, in_=marg, func=AF.Exp, scale=-1.0))
        S("dve", lambda e: e.tensor_scalar(out=ang2, in0=ang, scalar1=math.pi / 2, scalar2=None, op0=ALU.add))

        def reduce_angle(src, dst):
            S("dve", lambda e: e.tensor_scalar(out=yk, in0=src, scalar1=1.0 / TWO_PI, scalar2=None, op0=ALU.mult))
            S("dve", lambda e: e.tensor_copy(out=ki, in_=yk))
            S("dve", lambda e: e.tensor_copy(out=kf, in_=ki))
            S("dve", lambda e: e.scalar_tensor_tensor(out=dst, in0=kf, scalar=-TWO_PI, in1=src,
                                                      op0=ALU.mult, op1=ALU.add))
            S("dve", lambda e: e.tensor_scalar(out=dst, in0=dst, scalar1=3.1415925, scalar2=-3.1415925,
                                               op0=ALU.min, op1=ALU.max))

        reduce_angle(ang, rs)
        S("act", lambda e: e.activation(out=sn, in_=rs, func=AF.Sin))
        reduce_angle(ang2, rc)
        S("act", lambda e: e.activation(out=cs, in_=rc, func=AF.Sin))
        S("dve", lambda e: e.tensor_tensor(out=Pre, in0=magp, in1=cs, op=ALU.mult))
        S("dve", lambda e: e.tensor_tensor(out=Pim, in0=magp, in1=sn, op=ALU.mult))
        S("dve", lambda e: e.tensor_tensor(out=Nre, in0=magn, in1=cs, op=ALU.mult))
        S("dve", lambda e: e.scalar_tensor_tensor(out=Nim, in0=magn, scalar=-1.0, in1=sn, op0=ALU.mult, op1=ALU.mult))
        S("dve", lambda e: e.tensor_scalar(out=nr, in0=Pre[:, :, 0], scalar1=-1.0, scalar2=None, op0=ALU.add))
        S("dve", lambda e: e.tensor_tensor(out=den, in0=are, in1=are, op=ALU.mult))
        S("dve", lambda e: e.tensor_tensor(out=tq1, in0=aim, in1=aim, op=ALU.mult))
        S("dve", lambda e: e.tensor_tensor(out=den, in0=den, in1=tq1, op=ALU.add))
        S("dve", lambda e: e.reciprocal(out=rden, in_=den))
        S("dve", lambda e: e.tensor_tensor(out=tq1, in0=nr, in1=are, op=ALU.mult))
        S("dve", lambda e: e.tensor_tensor(out=tq2, in0=Pim[:, :, 0], in1=aim, op=ALU.mult))
        S("dve", lambda e: e.tensor_tensor(out=tq1, in0=tq1, in1=tq2, op=ALU.add))
        S("dve", lambda e: e.tensor_tensor(out=qre, in0=tq1, in1=rden, op=ALU.mult))
        S("dve", lambda e: e.tensor_tensor(out=tq1, in0=Pim[:, :, 0], in1=are, op=ALU.mult))
        S("dve", lambda e: e.tensor_tensor(out=tq2, in0=nr, in1=aim, op=ALU.mult))
        S("dve", lambda e: e.tensor_tensor(out=tq1, in0=tq1, in1=tq2, op=ALU.subtract))
        S("dve", lambda e: e.tensor_tensor(out=qim, in0=tq1, in1=rden, op=ALU.mult))

        def bc16(v):
            return v.unsqueeze(2).broadcast_to([128, 32, 16])

        t3a = t4a[:, :, 0, :]
        t3b = t4b[:, :, 0, :]
        S("dve", lambda e: e.tensor_tensor(out=t3a, in0=Bre, in1=bc16(qre), op=ALU.mult))
        S("dve", lambda e: e.tensor_tensor(out=t3b, in0=Bim, in1=bc16(qim), op=ALU.mult))
        S("dve", lambda e: e.tensor_tensor(out=Bbre, in0=t3a, in1=t3b, op=ALU.subtract))
        S("dve", lambda e: e.tensor_tensor(out=t3a, in0=Bim, in1=bc16(qre), op=ALU.mult))
        S("dve", lambda e: e.tensor_tensor(out=t3b, in0=Bre, in1=bc16(qim), op=ALU.mult))
        S("dve", lambda e: e.tensor_tensor(out=Bbim, in0=t3a, in1=t3b, op=ALU.add))
        P.op("dve", lambda e: e.tensor_copy(out=ArAr[:, 0, :], in_=Pre[:, :, 7]), reads=[R_s], writes=[R_A8])
        P.op("dve", lambda e: e.tensor_copy(out=ArAr[:, 1, :], in_=Pre[:, :, 7]), reads=[R_s], writes=[R_A8])
        P.op("dve", lambda e: e.tensor_scalar(out=AiPM[:, 0, :], in0=Pim[:, :, 7], scalar1=-1.0, scalar2=None,
                                              op0=ALU.mult), reads=[R_s], writes=[R_A8])
        P.op("dve", lambda e: e.tensor_copy(out=AiPM[:, 1, :], in_=Pim[:, :, 7]), reads=[R_s], writes=[R_A8])
        S("dve", lambda e: e.tensor_tensor(out=tq1, in0=Pre[:, :, 7], in1=Pre[:, :, 7], op=ALU.mult))
        S("dve", lambda e: e.tensor_tensor(out=tq2, in0=Pim[:, :, 7], in1=Pim[:, :, 7], op=ALU.mult))
        P.op("dve", lambda e: e.tensor_tensor(out=ArAr2[:, 0, :], in0=tq1, in1=tq2, op=ALU.subtract),
             reads=[R_s], writes=[R_A8])
        P.op("dve", lambda e: e.tensor_tensor(out=ArAr2[:, 1, :], in0=tq1, in1=tq2, op=ALU.subtract),
             reads=[R_s], writes=[R_A8])
        S("dve", lambda e: e.tensor_tensor(out=tq1, in0=Pre[:, :, 7], in1=Pim[:, :, 7], op=ALU.mult))
        P.op("dve", lambda e: e.tensor_scalar(out=AiPM2[:, 0, :], in0=tq1, scalar1=-2.0, scalar2=None, op0=ALU.mult),
             reads=[R_s], writes=[R_A8])
        P.op("dve", lambda e: e.tensor_scalar(out=AiPM2[:, 1, :], in0=tq1, scalar1=2.0, scalar2=None, op0=ALU.mult),
             reads=[R_s], writes=[R_A8])
        P.op("dve", lambda e: e.memset(Sf[0][:], 0.0), writes=[RSF[0]])
        P.op("dve", lambda e: e.memset(carry[:], 0.0), writes=[R_carry])

        def bcj(v, n=4):
            return v.unsqueeze(2).broadcast_to([128, 32, n, 16])

        def bcc(v, n=4):
            return v.unsqueeze(3).broadcast_to([128, 32, n, 16])

        def cplx(dst_re, dst_im, Xre, Xim, Yre, Yim, neg_im, writes, n=4):
            ta, tb = t4a[:, :, 0:n, :], t4b[:, :, 0:n, :]

            def op(fn):
                P.op("dve", fn, reads=[R_s], writes=writes + [R_s] + R_sx)
            op(lambda e: e.tensor_tensor(out=ta, in0=Xre, in1=Yre, op=ALU.mult))
            op(lambda e: e.tensor_tensor(out=tb, in0=Xim, in1=Yim, op=ALU.mult))
            op(lambda e: e.tensor_tensor(out=dst_re, in0=ta, in1=tb, op=ALU.subtract))
            op(lambda e: e.tensor_tensor(out=ta, in0=Xre, in1=Yim, op=ALU.mult))
            op(lambda e: e.tensor_tensor(out=tb, in0=Xim, in1=Yre, op=ALU.mult))
            if neg_im:
                op(lambda e: e.scalar_tensor_tensor(out=dst_im, in0=ta, scalar=-1.0, in1=tb,
                                                    op0=ALU.mult, op1=ALU.subtract))
            else:
                op(lambda e: e.tensor_tensor(out=dst_im, in0=ta, in1=tb, op=ALU.add))

        for jh in range(2):
            js = slice(jh * 4, jh * 4 + 4)
            cplx(W3f[:, 0, :, js, :], W3f[:, 1, :, js, :], bcj(Cre), bcj(Cim), bcc(Pre[:, :, js]), bcc(Pim[:, :, js]),
                 True, [R_W3f])
        for jh in range(2):
            js = slice(jh * 4, jh * 4 + 4)
            cplx(Hf[:, 0, :, js, :], Hf[:, 1, :, js, :], bcj(Bbre), bcj(Bbim), bcc(Nre[:, :, js]), bcc(Nim[:, :, js]),
                 False, [R_Hf])
        tmfs = [ycm[:, 0, :].bitcast(F32).rearrange("p (a b) -> p a b", a=4),
                ycm[:, 1, :].bitcast(F32).rearrange("p (a b) -> p a b", a=4)]
        R_tmf = RYCM
        n0_gen = phaseN_gen(0, TILES[0][0], TILES[0][1], [], xt_ids=[0, 1])
        for g4 in range(16):
            if g4 % 2 == 1:
                next(n0_gen, None)
            bank, rb = getbank()

            def mm(e, g4=g4, bank=bank):
                ins = None
                for gi in range(4):
                    g = g4 * 4 + gi
                    gh, g32 = divmod(g, 32)
                    ps_ = slice(gh * 64, (gh + 1) * 64)
                    o = bank[:, gi * 128:(gi + 1) * 128]
                    e.matmul(o, lhsT=Hf[ps_, 0, g32].rearrange("p j c -> p (j c)"),
                             rhs=W3[ps_, 0, g32, :], start=True, stop=False)
                    ins = e.matmul(o, lhsT=Hf[ps_, 1, g32].rearrange("p j c -> p (j c)"),
                                   rhs=W3[ps_, 1, g32, :], start=False, stop=True)
                return ins
            P.op("pe", mm, reads=[R_Hf, R_W3f], writes=[rb])
            tmf_ = tmfs[g4 % 2]
            rt_ = R_tmf[g4 % 2]
            P.op("dve", lambda e, bank=bank, tmf_=tmf_: e.tensor_tensor(
                out=tmf_, in0=bank[:, :].rearrange("p (a b) -> p a b", a=4),
                in1=maskf[:].unsqueeze(1).broadcast_to([128, 4, 128]), op=ALU.mult),
                reads=[rb, R_const], writes=[rt_])
            for gi in range(4):
                g = g4 * 4 + gi
                P.op("dve", lambda e, g=g, gi=gi, tmf_=tmf_: e.scalar_tensor_tensor(
                    out=TM[:, g, :], in0=identf[:], scalar=Dt[:, g:g + 1], in1=tmf_[:, gi, :],
                    op0=ALU.mult, op1=ALU.add), reads=[rt_, R_const, R_vec], writes=[R_TM])
        W1T = Hf

        def W(fn):
            P.op("dve", fn, reads=[R_s], writes=[R_Hf, R_s] + R_sx)
        PreR = Pre[:, :, 6::-1]
        PimR = Pim[:, :, 6::-1]
        for (j0, n) in ((0, 4), (4, 3)):
            js = slice(j0, j0 + n)
            cplx(W1T[:, 0, :, js, :], W1T[:, 1, :, js, :], bcj(Bbre, n), bcj(Bbim, n),
                 bcc(PreR[:, :, js], n), bcc(PimR[:, :, js], n), False, [R_Hf], n=n)
        W(lambda e: e.tensor_copy(out=W1T[:, 0, :, 7, :], in_=Bbre))
        W(lambda e: e.tensor_copy(out=W1T[:, 1, :, 7, :], in_=Bbim))
        for g4 in range(16):
            bank, rb = getbank()
            bbw = bank[:].bitcast(BF16)

            def tr(e, g4=g4, bbw=bbw):
                ins = None
                for gi in range(4):
                    g = g4 * 4 + gi
                    gh, g32 = divmod(g, 32)
                    ps_ = slice(gh * 64, (gh + 1) * 64)
                    for ri in range(2):
                        c0 = (gi * 2 + ri) * 64
                        ins = e.transpose(bbw[:, c0:c0 + 64], in_=W1T[ps_, ri, g32, :, :].rearrange("p j c -> p (j c)"),
                                          identity=identb[ps_, ps_])
                return ins
            P.op("pe", tr, reads=[R_Hf, R_const], writes=[rb])
            eng = evac_eng()
            P.op(eng, copy_op(eng, W1[:, g4 * 4:(g4 + 1) * 4, :].rearrange("p a b -> p (a b)"), bbw[:, 0:512]),
                 reads=[rb], writes=[R_W1])

        setup_done = [P.q[e_][-1] for e_ in ("pe", "act", "dve") if P.q[e_]]


        NBLK = 20
        wcache = nc.dram_tensor("wcache", [NBLK, 128, 4096], BF16).ap()
        wc_idx = {}
        RC = [Res("wc%d" % i) for i in range(NBLK)]

        def load_w(src_ap, col0, name):
            i = nxt("w", 2)
            key = (name, col0)
            flat = wbuf[:, i].rearrange("p k n -> p (k n)")
            if key not in wc_idx:
                idx = len(wc_idx)
                wc_idx[key] = idx
                P.dma("pool", lambda e: e.dma_start(
                    out=wbuf[:, i, :, :],
                    in_=src_ap[:, col0:col0 + 512].rearrange("(k p) n -> p k n", p=128)), writes=[RW[i]])
                P.dma("pool", lambda e: e.dma_start(out=wcache[idx], in_=flat), reads=[RW[i]], writes=[RC[idx]])
            else:
                idx = wc_idx[key]
                P.dma("pool", lambda e: e.dma_start(out=flat, in_=wcache[idx]), reads=[RC[idx]], writes=[RW[i]])
            return wbuf[:, i], RW[i]

        state = {"sf": 0}

        def do_tile(ti, T0, NT):
            NCH = NT // 8
            ntiles = [(n0, min(512, NT - n0)) for n0 in range(0, NT, 512)]
            csubs = [(c0, min(128, NCH - c0)) for c0 in range(0, NCH, 128)]
            nP = 128 if ti == 0 else 130
            NTp = 8 * nP

            def rh(k, n0):
                return RS[k][n0 // 512]


            first_deps = setup_done if ti == 0 else []
            xnT = slot_fm("D")

            if ti == 0:
                for _ in n0_gen:
                    pass

            xnT = slot_fm("D")
            UCM = slot_raw("A", 8192).rearrange("p (g s c) -> p g s c", g=64, s=8)
            Uml = slot_raw("B", 64 * KU).rearrange("p (g k) -> p g k", g=64)
            XS = slot_raw("C", 64 * KX).rearrange("p (r g k) -> p r g k", r=2, g=32)
            for h in range(2):
                wv, rw = load_w(w_in, h * 512, "w_in")
                for (c0, nc) in csubs:
                    for s in range(8):
                        bank, rb = getbank()

                        def mm(e, s=s, bank=bank, wv=wv, c0=c0, nc=nc):
                            ins = None
                            for kt in range(8):
                                ins = e.matmul(bank[0:nc, 0:512], lhsT=xnT[:, kt, 8 * c0 + s:8 * (c0 + nc):8],
                                               rhs=wv[:, kt, :], start=(kt == 0), stop=(kt == 7))
                            return ins
                        P.op("pe", mm, reads=RS["D"] + [rw], writes=[rb])
                        eng = evac_eng()
                        P.op(eng, copy_op(eng, UCM[0:nc, h * 32:(h + 1) * 32, s, :],
                                          bank[0:nc, 0:512].rearrange("p (g c) -> p g c", g=32)),
                             reads=[rb], writes=RS["A"], deps=first_deps)
                    for gb in range(4 * h, 4 * h + 4):
                        bank, rb = getbank()
                        bb = bank[:].bitcast(BF16)

                        def tr(e, gb=gb, bb=bb, nc=nc):
                            ins = None
                            for gi in range(8):
                                g = gb * 8 + gi
                                ins = e.transpose(bb[:, gi * 128:gi * 128 + nc],
                                                  in_=UCM[0:nc, g, :, :].rearrange("p s c -> p (s c)"),
                                                  identity=identb[0:nc, 0:nc])
                            return ins
                        P.op("pe", tr, reads=RS["A"] + [R_const], writes=[rb])
                        eng = evac_eng()
                        P.op(eng, copy_op(eng, Uml[:, gb * 8:(gb + 1) * 8, c0:c0 + nc],
                                          bb.rearrange("p (g k) -> p g k", g=8)[:, :, 0:nc]),
                             reads=[rb], writes=RS["B"], deps=first_deps)
            for q in range(32):
                bank, rb = getbank()

                def mm(e, q=q, bank=bank):
                    ins = None
                    for gh in range(2):
                        g = gh * 32 + q
                        for ri in range(2):
                            ins = e.matmul(bank[gh * 64:(gh + 1) * 64, ri * 256:ri * 256 + NCH],
                                           lhsT=W1[:, g, ri * 64:(ri + 1) * 64], rhs=Uml[:, g, 0:NCH],
                                           start=True, stop=True)
                    return ins
                P.op("pe", mm, reads=RS["B"] + [R_W1], writes=[rb])
                eng = evac_eng()
                P.op(eng, copy_op(eng, XS[:, :, q, 1:1 + NCH],
                                  bank[:, :].rearrange("p (r k) -> p r k", r=2)[:, :, 0:NCH]),
                     reads=[rb], writes=RS["C"] + [R_XSx], deps=first_deps)

            def proj_gen(src_w, col0, rhs_slot, evac, name):
                rhs = slot_fm(rhs_slot)
                for h in range(2):
                    wv, rw = load_w(src_w, col0 + h * 512, name)
                    for (n0, nn) in ntiles:
                        for f4 in range(4):
                            fo = h * 4 + f4
                            bank, rb = getbank()

                            def mm(e, bank=bank, wv=wv, f4=f4, n0=n0, nn=nn):
                                ins = None
                                for kt in range(8):
                                    ins = e.matmul(bank[:, 0:nn], lhsT=wv[:, kt, f4 * 128:(f4 + 1) * 128],
                                                   rhs=rhs[:, kt, n0:n0 + nn], start=(kt == 0), stop=(kt == 7))
                                return ins
                            P.op("pe", mm, reads=[rh(rhs_slot, n0), rw], writes=[rb])
                            evac(bank, rb, fo, n0, nn)
                            yield

            def proj(src_w, col0, rhs_slot, evac, name):
                for _ in proj_gen(src_w, col0, rhs_slot, evac, name):
                    pass

            gsB = slot_fm("E")

            def ev_gs(bank, rb, fo, n0, nn):
                P.op("act", lambda e: e.activation(out=gsB[:, fo, n0:n0 + nn], in_=bank[:, 0:nn], func=AF.Sigmoid,
                                                   bias=bgate[:, fo:fo + 1]),
                     reads=[rb, R_vec], writes=[rh("E", n0)])
            gs_gen = proj_gen(w_in, 4 * D, "D", ev_gs, "w_in")
            ysA = slot_fm("A")

            def ev_zs(bank, rb, fo, n0, nn):
                P.op("act", lambda e: e.activation(out=ysA[:, fo, n0:n0 + nn], in_=bank[:, 0:nn], func=AF.Silu),
                     reads=[rb], writes=[rh("A", n0)])
            zs_gen = proj_gen(w_in, 1 * D, "D", ev_zs, "w_in")

            if ti == 0:
                load_S0(S0f, R_S0f)
            hstep = nP // 2
            cur = state["sf"]
            P.op("act", lambda e, cur=cur: e.activation(out=XS[:, :, :, 0], in_=Sf[cur][:], func=AF.Copy),
                 reads=[RSF[cur]], writes=RS["C"])
            Xe = XS[:, :, :, 1:nP + 1:2]
            Xe_sw = XS[:, ::-1, :, 1:nP + 1:2]
            Xo = XS[:, :, :, 2:nP + 2:2]
            tA = slot_raw("A", 2 * 64 * hstep).bitcast(F32).rearrange("p (r g j) -> p r g j", r=2, g=32)
            tE = slot_raw("E", 2 * 64 * hstep).bitcast(F32).rearrange("p (r g j) -> p r g j", r=2, g=32)
            bch = lambda v, n_: v.unsqueeze(3).broadcast_to([128, 2, 32, n_])
            P.op("dve", lambda e: e.tensor_tensor(out=tA, in0=Xe, in1=bch(ArAr[:], hstep), op=ALU.mult),
                 reads=[R_XSx, R_A8], writes=RS["A"])
            P.op("dve", lambda e: e.tensor_tensor(out=tE, in0=Xe_sw, in1=bch(AiPM[:], hstep), op=ALU.mult),
                 reads=[R_XSx, R_A8], writes=RS["E"])
            P.op("dve", lambda e: e.tensor_tensor(out=tA, in0=tA, in1=tE, op=ALU.add),
                 reads=RS["E"], writes=RS["A"])
            P.op("dve", lambda e: e.tensor_tensor(out=Xo, in0=tA, in1=Xo, op=ALU.add),
                 reads=RS["A"], writes=RS["C"] + [R_XSx])
            for j in range(hstep):
                if next(gs_gen, "done") == "done":
                    next(zs_gen, None)
                nx = 1 - cur
                col = 2 * j + 2
                P.op("dve", lambda e, cur=cur: e.tensor_tensor(out=st1[:], in0=Sf[cur][:], in1=ArAr2[:], op=ALU.mult),
                     reads=[RSF[cur], R_A8], writes=[R_st1])
                P.op("dve", lambda e, cur=cur: e.tensor_tensor(out=st2[:], in0=Sf[cur][:, ::-1, :], in1=AiPM2[:],
                                                               op=ALU.mult),
                     reads=[RSF[cur], R_A8], writes=[R_st2])
                P.op("dve", lambda e, col=col: e.tensor_tensor(out=st1[:], in0=st1[:], in1=XS[:, :, :, col], op=ALU.add),
                     reads=[R_st1, R_XSx], writes=[R_st1])
                P.op("dve", lambda e, nx=nx: e.tensor_tensor(out=Sf[nx][:], in0=st1[:], in1=st2[:], op=ALU.add),
                     reads=[R_st1, R_st2], writes=[RSF[nx]])
                P.op("act", lambda e, nx=nx, col=col: e.activation(out=XS[:, :, :, col], in_=Sf[nx][:], func=AF.Copy),
                     reads=[RSF[nx]], writes=RS["C"])
                cur = nx
            p1, p2 = nxt("xt", NXT), nxt("xt", NXT)
            for jb0 in range(0, hstep, 16):
                n_ = min(16, hstep - jb0)
                w1 = xt[p1][:, 0:64 * n_].rearrange("p (r g j) -> p r g j", r=2, g=32)
                w2 = xt[p2][:, 0:64 * n_].rearrange("p (r g j) -> p r g j", r=2, g=32)
                so = XS[:, :, :, 2 * jb0:2 * (jb0 + n_):2]
                so_sw = XS[:, ::-1, :, 2 * jb0:2 * (jb0 + n_):2]
                xe = XS[:, :, :, 2 * jb0 + 1:2 * (jb0 + n_) + 1:2]
                P.op("dve", lambda e, w1=w1, so=so, n_=n_: e.tensor_tensor(out=w1, in0=so, in1=bch(ArAr[:], n_),
                                                                          op=ALU.mult),
                     reads=RS["C"] + [R_A8], writes=[RXT[p1]])
                P.op("dve", lambda e, w2=w2, so_sw=so_sw, n_=n_: e.tensor_tensor(out=w2, in0=so_sw,
                                                                                in1=bch(AiPM[:], n_), op=ALU.mult),
                     reads=RS["C"] + [R_A8], writes=[RXT[p2]])
                P.op("dve", lambda e, w1=w1, w2=w2: e.tensor_tensor(out=w1, in0=w1, in1=w2, op=ALU.add),
                     reads=[RXT[p2]], writes=[RXT[p1]])
                P.op("dve", lambda e, w1=w1, xe=xe: e.tensor_tensor(out=xe, in0=w1, in1=xe, op=ALU.add),
                     reads=[RXT[p1]], writes=RS["C"])
            state["sf"] = cur
            for _ in gs_gen:
                pass
            for _ in zs_gen:
                pass
            if ti == LAST:
                io0 = nxt("xt", NXT)
                for gh in range(2):
                    bank, rb = getbank()
                    ps_ = slice(gh * 64, (gh + 1) * 64)

                    def trp(e, bank=bank, cur=cur, ps_=ps_):
                        ins = None
                        for ri in range(2):
                            ins = e.transpose(bank[0:32, ri * 64:(ri + 1) * 64], in_=Sf[cur][ps_, ri, :],
                                              identity=identf[ps_, ps_])
                        return ins
                    P.op("pe", trp, reads=[RSF[cur], R_const], writes=[rb])
                    P.op("dve", lambda e, bank=bank, io0=io0, gh=gh: e.tensor_copy(
                        out=xt[io0][0:32, gh * 128:(gh + 1) * 128], in_=bank[0:32, 0:128]),
                        reads=[rb], writes=[RXT[io0]])
                for ri, dst in enumerate((nre_p, nim_p)):
                    for gh in range(2):
                        idx = gh * 2 + ri
                        P.dma("sp", lambda e, dst=dst, gh=gh, idx=idx, io0=io0: e.dma_start(
                            out=dst[gh * 32:(gh + 1) * 32, :], in_=xt[io0][0:32, idx * 64:(idx + 1) * 64]),
                            reads=[RXT[io0]])
                i1, i2, i3 = nxt("xt", NXT), nxt("xt", NXT), nxt("xt", NXT)
                v1 = xt[i1][:].rearrange("p (r g b) -> p r g b", r=2, g=32)
                v2 = xt[i2][:].rearrange("p (r g b) -> p r g b", r=2, g=32)
                v3p = xt[i3][:].rearrange("p (r b g) -> p r b g", r=2, b=16)
                v3 = v3p.rearrange("p r b g -> p r g b")
                bcb = lambda v: v.unsqueeze(3).broadcast_to([128, 2, 32, 16])
                P.op("dve", lambda e: e.tensor_tensor(out=v1, in0=S0f[:], in1=bcb(ArAr[:]), op=ALU.mult),
                     reads=[R_S0f, R_A8], writes=[RXT[i1]])
                P.op("dve", lambda e: e.tensor_tensor(out=v2, in0=S0f[:, ::-1, :, :], in1=bcb(AiPM[:]), op=ALU.mult),
                     reads=[R_S0f, R_A8], writes=[RXT[i2]])
                P.op("dve", lambda e: e.tensor_tensor(out=v1, in0=v1, in1=v2, op=ALU.add),
                     reads=[RXT[i1], RXT[i2]], writes=[RXT[i1]])
                P.op("dve", lambda e: e.tensor_tensor(out=v3, in0=v1, in1=XS[:, :, :, nP + 1:nP + 17], op=ALU.add),
                     reads=[RXT[i1]] + RS["C"], writes=[RXT[i3]])
                io1 = nxt("xt", NXT)
                for gh in range(2):
                    bank, rb = getbank()
                    ps_ = slice(gh * 64, (gh + 1) * 64)

                    def trs(e, bank=bank, ps_=ps_):
                        ins = None
                        for j in range(8):
                            ri, b4 = divmod(j, 4)
                            ins = e.transpose(bank[:, j * 64:(j + 1) * 64],
                                              in_=v3p[ps_, ri, b4 * 4:(b4 + 1) * 4, :].rearrange("p b g -> p (b g)"),
                                              identity=identf[ps_, ps_])
                        return ins
                    P.op("pe", trs, reads=[RXT[i3], R_const], writes=[rb])
                    eng = evac_eng()
                    P.op(eng, copy_op(eng, xt[io1][:, gh * 512:(gh + 1) * 512], bank[:, :]),
                         reads=[rb], writes=[RXT[io1]])
                for gh in range(2):
                    for j in range(8):
                        ri, b4 = divmod(j, 4)
                        idx = gh * 8 + j
                        dst = (nre_s, nim_s)[ri]
                        for bb in range(4):
                            P.dma("sp", lambda e, dst=dst, gh=gh, b4=b4, bb=bb, idx=idx, io1=io1: e.dma_start(
                                out=dst[b4 * 4 + bb, gh * 32:(gh + 1) * 32, :],
                                in_=xt[io1][bb * 32:(bb + 1) * 32, idx * 64:(idx + 1) * 64]), reads=[RXT[io1]])
                P.op("act", lambda e: e.activation(out=XS[:, :, :, nP:nP + 16], in_=S0f[:], func=AF.Copy),
                     reads=[R_S0f], writes=RS["C"])

            ygT = slot_fm("B")

            def stA(gb, c0, nc):
                ymls = []
                for half in range(2):
                    bank, rb = getbank()

                    def mm(e, half=half, bank=bank):
                        ins = None
                        for gi in range(4):
                            g = gb * 8 + half * 4 + gi
                            gh, g32 = divmod(g, 32)
                            ps_ = slice(gh * 64, (gh + 1) * 64)
                            o = bank[:, gi * 128:gi * 128 + nc]
                            e.matmul(o, lhsT=TM[:, g, :], rhs=Uml[:, g, c0:c0 + nc], start=True, stop=False)
                            e.matmul(o, lhsT=W3[ps_, 0, g32, :], rhs=XS[ps_, 0, g32, c0:c0 + nc],
                                     start=False, stop=False)
                            ins = e.matmul(o, lhsT=W3[ps_, 1, g32, :], rhs=XS[ps_, 1, g32, c0:c0 + nc],
                                           start=False, stop=True)
                        return ins
                    P.op("pe", mm, reads=RS["B"] + RS["C"] + [R_TM, R_W3], writes=[rb])
                    iy = nxt("yml", NYML)
                    eng = evac_eng()
                    P.op(eng, copy_op(eng, yml[iy][:, :, 0:nc],
                                      bank[:, :].rearrange("p (g k) -> p g k", g=4)[:, :, 0:nc]),
                         reads=[rb], writes=[RYML[iy]])
                    ymls.append(iy)
                return ymls

            def stB(ic, ymls, nc):
                bank, rb = getbank()
                bb = bank[:].bitcast(BF16)

                def tr(e):
                    ins = None
                    for half in range(2):
                        for gi in range(4):
                            q0 = (half * 4 + gi) * 128
                            ins = e.transpose(bb[0:nc, q0:q0 + 128], in_=yml[ymls[half]][:, gi, 0:nc],
                                              identity=identb[:, :])
                    return ins
                P.op("pe", tr, reads=[RYML[ymls[0]], RYML[ymls[1]], R_const], writes=[rb])
                P.op("act", lambda e: e.activation(
                    out=ycm[0:nc, ic, :].rearrange("p (j g c) -> p g j c", j=8, g=8),
                    in_=bb[0:nc, :].rearrange("p (g j c) -> p g j c", g=8, j=8), func=AF.Gelu_apprx_tanh),
                    reads=[rb], writes=[RYCM[ic]])

            def stC(ic, gb, c0, nc):
                bank2, rb2 = getbank()
                bb2 = bank2[:].bitcast(BF16)

                def tr2(e):
                    ins = None
                    for j in range(8):
                        ins = e.transpose(bb2[:, j * 128:j * 128 + nc], in_=ycm[0:nc, ic, j * 128:(j + 1) * 128],
                                          identity=identb[0:nc, 0:nc])
                    return ins
                P.op("pe", tr2, reads=[RYCM[ic], R_const], writes=[rb2])
                eng = evac_eng()
                P.op(eng, copy_op(eng, ygT[:, gb, 8 * c0:8 * (c0 + nc)].rearrange("p (k j) -> p j k", j=8),
                                  bb2.rearrange("p (j k) -> p j k", j=8)[:, :, 0:nc]),
                     reads=[rb2], writes=RS["B"])

            items = [(gb, c0, nc) for gb in range(8) for (c0, nc) in csubs]
            ymls_of = {}
            for it_ in range(len(items) + 2):
                if it_ < len(items):
                    ymls_of[it_] = stA(*items[it_])
                if 0 <= it_ - 1 < len(items):
                    stB((it_ - 1) % 2, ymls_of[it_ - 1], items[it_ - 1][2])
                if 0 <= it_ - 2 < len(items):
                    stC((it_ - 2) % 2, *items[it_ - 2])

            def ev_glu(bank, rb, fo, n0, nn):
                it = nxt("tmp", NTMP)
                P.op("act", lambda e: e.activation(out=tmpb[it][:, 0:nn], in_=bank[:, 0:nn], func=AF.Sigmoid,
                                                   bias=bglu[:, fo:fo + 1]),
                     reads=[rb, R_vec], writes=[RTMP[it]])
                P.op("dve", lambda e: e.tensor_tensor(out=ysA[:, fo, n0:n0 + nn], in0=ysA[:, fo, n0:n0 + nn],
                                                      in1=tmpb[it][:, 0:nn], op=ALU.mult),
                     reads=[RTMP[it]], writes=[rh("A", n0)])
                P.op("dve", lambda e: e.tensor_tensor(out=ysA[:, fo, n0:n0 + nn], in0=ysA[:, fo, n0:n0 + nn],
                                                      in1=ygT[:, fo, n0:n0 + nn], op=ALU.mult),
                     reads=[rh("B", n0)], writes=[rh("A", n0)])
            proj(w_glu, 0, "B", ev_glu, "w_glu")

            def ev_bs(bank, rb, fo, n0, nn):
                P.op("dve", lambda e: e.tensor_tensor(out=gsB[:, fo, n0:n0 + nn], in0=bank[:, 0:nn],
                                                      in1=gsB[:, fo, n0:n0 + nn], op=ALU.mult),
                     reads=[rb], writes=[rh("E", n0)])
            proj(w_bs, 0, "A", ev_bs, "w_bs")

            RCX = [Res("cx%d" % i) for i in range(4)]
            Lp = 15 + NTp
            extp = slot_raw("C", 8 * Lp).rearrange("p (a l) -> p a l", a=8)
            if ti == LAST:
                exts = arena[:, SL["B"] + 4224:SL["B"] + 4224 + 8 * 16 * 23].rearrange(
                    "p (a b l) -> p a b l", a=8, b=16)
                for half in range(2):
                    ix = nxt("xt", NXT)
                    P.dma("sp", lambda e, ix=ix, half=half: e.dma_start(
                        out=xt[ix][0:120, :],
                        in_=spool[half * 8:(half + 1) * 8].rearrange("b r f -> (b r) f")), writes=[RXT[ix]])
                    for q in range(2):
                        bank, rb = getbank()

                        def tr(e, ix=ix, q=q, bank=bank):
                            ins = None
                            for f4 in range(4):
                                ft = q * 4 + f4
                                ins = e.transpose(bank[:, f4 * 128:f4 * 128 + 120],
                                                  in_=xt[ix][0:120, ft * 128:(ft + 1) * 128],
                                                  identity=identf[0:120, 0:120])
                            return ins
                        P.op("pe", tr, reads=[RXT[ix], R_const], writes=[rb])
                        eng = evac_eng()
                        P.op(eng, copy_op(
                            eng, bufT[:, q * 4:(q + 1) * 4, half * 8:(half + 1) * 8, :].rearrange("p a b r -> p a (b r)"),
                            bank[:, :].rearrange("p (a t) -> p a t", a=4)[:, :, 0:120]),
                            reads=[rb], writes=RS["B"])
                P.op("dve", lambda e: e.tensor_copy(out=exts[:, :, :, 0:15], in_=bufT[:]),
                     reads=[], writes=RS["B"] + RCX)
                P.dma("sp", lambda e: e.dma_start(out=npool_s[:, 0:7, :], in_=spool[:, 8:15, :]))
            P.op("dve", lambda e: e.tensor_copy(out=extp[:, :, 0:15], in_=carry[:]),
                 reads=[R_carry], writes=RS["C"] + RCX)

            def ev_up(bank, rb, fo, n0, nn):
                wr = [RCX[fo // 2]]
                if n0 + nn <= NTp:
                    eng = evac_eng()
                    P.op(eng, copy_op(eng, extp[:, fo, 15 + n0:15 + n0 + nn], bank[:, 0:nn]),
                         reads=[rb], writes=wr)
                else:
                    assert ti == LAST and n0 == 1024 and nn == 144 and NTp == 1040
                    P.op("act", lambda e: e.activation(out=extp[:, fo, 15 + n0:15 + n0 + 16], in_=bank[:, 0:16],
                                                       func=AF.Copy), reads=[rb], writes=wr)
                    P.op("dve", lambda e: e.tensor_copy(
                        out=exts[:, fo, :, 15:23], in_=bank[:, 16:144].rearrange("p (b t) -> p b t", b=16)),
                        reads=[rb], writes=wr + RS["B"])
                    P.op("act", lambda e: e.activation(out=UPF[:, fo, :], in_=bank[:, 0:144], func=AF.Copy),
                         reads=[rb], writes=[R_UPF])
            up_gen = proj_gen(w_in, 2 * D, "D", ev_up, "w_in")

            pooledA = slot_fm("A")
            T1o, T2o = (SL["B"] + 5184, SL["B"] + 7296) if ti != LAST else (SL["B"], SL["B"] + 2112)

            def pool_group(ext3, L, rows, gi, out3, cnt_fix, fin_views=None):
                w = 2 ** (gi + 1)
                t1 = arena[:, T1o:T1o + rows * L].rearrange("p (a l) -> p a l", a=rows)
                t2 = arena[:, T2o:T2o + rows * L].rearrange("p (a l) -> p a l", a=rows)

                def lvl(dst, src, lo, d):
                    P.op("dve", lambda e: e.tensor_tensor(out=dst[:, :, lo:L], in0=src[:, :, lo:L],
                                                          in1=src[:, :, lo - d:L - d], op=ALU.add),
                         reads=[RCX[gi]] + RS["B"], writes=RS["B"])
                lvl(t1, ext3, 1, 1)
                fin = t1
                if w >= 4:
                    lvl(t2, t1, 3, 2)
                    fin = t2
                if w >= 8:
                    lvl(t1, t2, 7, 4)
                    fin = t1
                if w >= 16:
                    lvl(t2, t1, 15, 8)
                    fin = t2
                if cnt_fix:
                    P.op("dve", lambda e: e.tensor_tensor(
                        out=fin[:, :, 15:31], in0=fin[:, :, 15:31],
                        in1=invc[:, gi, :].unsqueeze(1).broadcast_to([128, rows, 16]), op=ALU.mult),
                        reads=RS["B"] + [R_const], writes=RS["B"])
                if fin_views is None:
                    o_, a_, b_ = out3, fin[:, :, 15:L], ext3[:, :, 15:L]
                else:
                    o_, a_, b_ = fin_views(fin)
                P.op("dve", lambda e: e.scalar_tensor_tensor(
                    out=o_, in0=a_, scalar=1.0 / w, in1=b_,
                    op0=ALU.mult, op1=ALU.subtract), reads=[RCX[gi]] + RS["C"] + RS["B"], writes=RS["A"])

            def pool_gi(gi):
                fs = slice(2 * gi, 2 * gi + 2)
                pool_group(extp[:, fs, :], Lp, 2, gi, pooledA[:, fs, 0:NTp], ti == 0)
                if ti == LAST:
                    pool_group(exts[:, fs, :, :].rearrange("p a b l -> p (a b) l"), 23, 32, gi, None, False,
                               fin_views=lambda fin, fs=fs: (
                                   pooledA[:, fs, NTp:NTp + 128].rearrange("p a (b t) -> p a b t", b=16),
                                   fin.rearrange("p (a b) l -> p a b l", a=2)[:, :, :, 15:23],
                                   exts[:, fs, :, 15:23]))

            ypE = slot_fm("B")

            def pm(gis):
                for (n0, nn) in ntiles:
                    for gi in gis:
                        for fo2 in range(2):
                            fo = 2 * gi + fo2
                            bank, rb = getbank()

                            def mm(e, bank=bank, gi=gi, fo2=fo2, n0=n0, nn=nn):
                                ins = None
                                for k2 in range(2):
                                    ins = e.matmul(bank[:, 0:nn], lhsT=pmw[:, gi, k2, fo2 * 128:(fo2 + 1) * 128],
                                                   rhs=pooledA[:, 2 * gi + k2, n0:n0 + nn],
                                                   start=(k2 == 0), stop=(k2 == 1))
                                return ins
                            P.op("pe", mm, reads=[rh("A", n0), R_pmw], writes=[rb])
                            P.op("act", lambda e, bank=bank, fo=fo, n0=n0, nn=nn: e.activation(
                                out=ypE[:, fo, n0:n0 + nn], in_=bank[:, 0:nn], func=AF.Copy,
                                scale=pscale[:, fo:fo + 1]), reads=[rb, R_vec], writes=[rh("B", n0)])

            nb_blk = 4 * len(ntiles)
            for _ in range(nb_blk):
                next(up_gen)
            pool_gi(0)
            pool_gi(1)
            for _ in up_gen:
                pass
            if ti != LAST:
                P.op("dve", lambda e: e.tensor_copy(out=carry[:], in_=extp[:, :, NTp:NTp + 15]),
                     reads=RS["C"] + RCX, writes=[R_carry])
                pm([0, 1])
                pool_gi(2)
                pool_gi(3)
                pm([2, 3])
            else:
                pool_gi(2)
                pool_gi(3)
                pm([0, 1, 2, 3])

            def ev_zp(bank, rb, fo, n0, nn):
                it = nxt("tmp", NTMP)
                P.op("act", lambda e: e.activation(out=tmpb[it][:, 0:nn], in_=bank[:, 0:nn], func=AF.Silu),
                     reads=[rb], writes=[RTMP[it]])
                P.op("dve", lambda e: e.tensor_tensor(out=ypE[:, fo, n0:n0 + nn], in0=ypE[:, fo, n0:n0 + nn],
                                                      in1=tmpb[it][:, 0:nn], op=ALU.mult),
                     reads=[RTMP[it]], writes=[rh("B", n0)])
            proj(w_in, 3 * D, "D", ev_zp, "w_in")

            gpA = slot_fm("A")

            def ev_gp(bank, rb, fo, n0, nn):
                P.op("act", lambda e: e.activation(out=gpA[:, fo, n0:n0 + nn], in_=bank[:, 0:nn], func=AF.Sigmoid,
                                                   bias=bgate[:, 8 + fo:9 + fo]),
                     reads=[rb, R_vec], writes=[rh("A", n0)])
            proj(w_in, 5 * D, "D", ev_gp, "w_in")

            def ev_bp(bank, rb, fo, n0, nn):
                it = nxt("tmp", NTMP)
                P.op("dve", lambda e: e.tensor_tensor(out=tmpb[it][:, 0:nn], in0=bank[:, 0:nn],
                                                      in1=gpA[:, fo, n0:n0 + nn], op=ALU.mult),
                     reads=[rb, rh("A", n0)], writes=[RTMP[it]])
                P.op("dve", lambda e: e.tensor_tensor(out=gsB[:, fo, n0:n0 + nn], in0=gsB[:, fo, n0:n0 + nn],
                                                      in1=tmpb[it][:, 0:nn], op=ALU.add),
                     reads=[RTMP[it]], writes=[rh("E", n0)])
            bp_gen = proj_gen(w_bp, 0, "B", ev_bp, "w_bp")
            if ti + 1 < len(TILES):
                n_gen = phaseN_gen(ti + 1, TILES[ti + 1][0], TILES[ti + 1][1], [])
            else:
                n_gen = iter(())
            for ib, _ in enumerate(bp_gen):
                if ib % 2 == 1:
                    next(n_gen, None)
            for _ in n_gen:
                pass

            if ti == LAST:
                for q in range(2):
                    bank, rb = getbank()

                    def tr(e, q=q, bank=bank):
                        ins = None
                        for f4 in range(4):
                            ft = q * 4 + f4
                            ins = e.transpose(bank[:, f4 * 128:(f4 + 1) * 128], in_=UPF[:, ft, 16:144],
                                              identity=identf[:, :])
                        return ins
                    P.op("pe", tr, reads=[R_UPF, R_const], writes=[rb])
                    ix = nxt("xt", NXT) if q == 0 else ix
                    eng = evac_eng()
                    P.op(eng, copy_op(eng, xt[ix][:, q * 512:(q + 1) * 512], bank[:, :]),
                         reads=[rb], writes=[RXT[ix]])
                for b in range(16):
                    P.dma("sp", lambda e, b=b, ix=ix: e.dma_start(out=npool_s[b, 7:15, :], in_=xt[ix][b * 8:(b + 1) * 8, :]),
                          reads=[RXT[ix]])
                for q in range(2):
                    bank, rb = getbank()

                    def tr(e, q=q, bank=bank):
                        ins = None
                        for f4 in range(4):
                            ft = q * 4 + f4
                            ins = e.transpose(bank[0:16, f4 * 128:(f4 + 1) * 128], in_=UPF[:, ft, 0:16],
                                              identity=identf[:, :])
                        return ins
                    P.op("pe", tr, reads=[R_UPF, R_const], writes=[rb])
                    ix2 = nxt("xt", NXT) if q == 0 else ix2
                    eng = evac_eng()
                    P.op(eng, copy_op(eng, xt[ix2][0:16, q * 512:(q + 1) * 512], bank[0:16, :]),
                         reads=[rb], writes=[RXT[ix2]])
                P.dma("sp", lambda e, ix2=ix2: e.dma_start(out=npool_p[:, :], in_=xt[ix2][1:16, :]), reads=[RXT[ix2]])

            wv0, rw0 = load_w(w_out, 0, "w_out")
            wv1, rw1 = load_w(w_out, 512, "w_out")
            wvs = [(wv0, rw0), (wv1, rw1)]
            mB = slot_fm("E")
            if ti == 0:
                rowt = [(16 + 128 * i, min(128, 1024 - 16 - 128 * i)) for i in range(8)]
            else:
                rowt = [(1024 + 128 * i, 128) for i in range(8)] + [(2048, 16), (2064, 128)]
            res_ix = {}

            def issue_res_load(i_):
                if i_ < len(rowt) and i_ not in res_ix:
                    tok_, rows_ = rowt[i_]
                    jx = nxt("xt", NXT)
                    res_ix[i_] = jx
                    P.dma("sp", lambda e, jx=jx, tok_=tok_, rows_=rows_: e.dma_start(
                        out=xt[jx][0:rows_, :], in_=xall[tok_:tok_ + rows_, :]), writes=[RXT[jx]])
            for i_rt, (tok0, rows) in enumerate(rowt):
                c0 = tok0 - T0
                issue_res_load(i_rt)
                issue_res_load(i_rt + 1)
                issue_res_load(i_rt + 2)
                ix = res_ix[i_rt]
                for h in range(2):
                    bank, rb = getbank()
                    wv, rw = wvs[h]

                    def mm(e, bank=bank, wv=wv, c0=c0, rows=rows):
                        ins = None
                        for kt in range(8):
                            ins = e.matmul(bank[0:rows, 0:512], lhsT=mB[:, kt, c0:c0 + rows], rhs=wv[:, kt, :],
                                           start=(kt == 0), stop=(kt == 7))
                        return ins
                    P.op("pe", mm, reads=[rh("E", c0), rh("E", c0 + rows - 1), rw], writes=[rb])
                    P.op("dve", lambda e, bank=bank, ix=ix, h=h, rows=rows: e.tensor_tensor(
                        out=xt[ix][0:rows, h * 512:(h + 1) * 512], in0=bank[0:rows, 0:512],
                        in1=xt[ix][0:rows, h * 512:(h + 1) * 512], op=ALU.add), reads=[rb], writes=[RXT[ix]])
                si = nxt("stat", 4)
                jq = nxt("xn", 2)
                P.op("act", lambda e, ix=ix, rows=rows, si=si, jq=jq: e.activation(
                    out=xn[jq][0:rows, :], in_=xt[ix][0:rows, :], func=AF.Square,
                    accum_out=stat[0:rows, 2 * si:2 * si + 1]), reads=[RXT[ix]], writes=[RXN[jq], RSTAT[si]])
                P.op("act", lambda e, rows=rows, si=si: e.activation(
                    out=stat[0:rows, 2 * si + 1:2 * si + 2], in_=stat[0:rows, 2 * si:2 * si + 1],
                    func=AF.Sqrt, scale=1.0 / D, bias=EPS), reads=[RSTAT[si]], writes=[RSTAT[si]])
                P.op("dve", lambda e, rows=rows, si=si: e.reciprocal(
                    out=stat[0:rows, 2 * si:2 * si + 1], in_=stat[0:rows, 2 * si + 1:2 * si + 2]),
                    reads=[RSTAT[si]], writes=[RSTAT[si]])
                P.op("dve", lambda e, ix=ix, rows=rows, si=si: e.scalar_tensor_tensor(
                    out=xt[ix][0:rows, :], in0=xt[ix][0:rows, :], scalar=stat[0:rows, 2 * si:2 * si + 1],
                    in1=fB[0:rows, :], op0=ALU.mult, op1=ALU.mult),
                    reads=[RSTAT[si], R_fB], writes=[RXT[ix]])
                if tok0 < NPROMPT:
                    dst = y_p[tok0 - 16:tok0 - 16 + rows, :]
                else:
                    dst = y_s[tok0 - NPROMPT:tok0 - NPROMPT + rows, :]
                P.dma("sp", lambda e, ix=ix, rows=rows, dst=dst: e.dma_start(out=dst, in_=xt[ix][0:rows, :]),
                      reads=[RXT[ix]])

        for ti_, (T0_, NT_) in enumerate(TILES):
            do_tile(ti_, T0_, NT_)
        P.emit()
    return nc


_CACHE = {}


def _consts():
    ident = np.eye(128, dtype=np.float32)
    s_idx = np.arange(128) // 16
    mask = (s_idx[None, :] >= s_idx[:, None]).astype(np.float32)
    nvals = np.broadcast_to(np.arange(1, 9, dtype=np.float32)[None, None, :], (128, 32, 8)).copy()
    invc = np.zeros((128, 4, 16), np.float32)
    for gi in range(4):
        w = 2 ** (gi + 1)
        for pos in range(16):
            invc[:, gi, pos] = w / min(pos + 1, w)
    return ident, mask, nvals, invc


def kernel(x_prompt, x_sample, state_ssm_re, state_ssm_im, state_pool, meta_tokens,
           norm_gain, w_in, b_gate, ssm_a_re, ssm_a_im, ssm_log_dt, ssm_b_re, ssm_b_im,
           ssm_c_re, ssm_c_im, ssm_d, w_glu, b_glu, pool_mix, pool_scale,
           w_branch_ssm, w_branch_pool, w_out, final_norm_gain):
    f = lambda a: np.ascontiguousarray(np.asarray(a, dtype=np.float32))
    x_prompt, x_sample = f(x_prompt), f(x_sample)
    meta = f(meta_tokens)
    if "nc" not in _CACHE:
        _CACHE["nc"] = build_program()
    nc = _CACHE["nc"]
    ident, mask, nvals, invc = _consts()
    shared = {
        "w_in": f(w_in[0]), "w_glu": f(w_glu[0]), "pool_mix": f(pool_mix[0]), "w_bs": f(w_branch_ssm[0]),
        "w_bp": f(w_branch_pool[0]), "w_out": f(w_out[0]), "norm_gain": f(norm_gain[0]), "b_gate": f(b_gate[0]),
        "ssm_d": f(ssm_d[0]), "b_glu": f(b_glu[0]), "pool_scale": f(pool_scale[0]), "fgain": f(final_norm_gain),
        "a_re": f(ssm_a_re[0]), "a_im": f(ssm_a_im[0]), "log_dt": f(ssm_log_dt[0]),
        "b_re": f(ssm_b_re[0]), "b_im": f(ssm_b_im[0]), "c_re": f(ssm_c_re[0]), "c_im": f(ssm_c_im[0]),
        "c_ident": ident, "c_mask": mask, "c_nvals": nvals, "c_invc": invc,
    }
    in_maps = []
    for c in range(NCORES):
        m = dict(shared)
        m["xall"] = np.ascontiguousarray(np.concatenate(
            [meta, x_prompt[c], x_sample[16 * c:16 * (c + 1)].reshape(128, D)], axis=0))
        m["s0re"] = f(state_ssm_re[0, 16 * c:16 * (c + 1)])
        m["s0im"] = f(state_ssm_im[0, 16 * c:16 * (c + 1)])
        m["spool"] = f(state_pool[0, 16 * c:16 * (c + 1)])
        in_maps.append(m)
    res = run_bass_kernel_spmd(nc, in_maps, core_ids=list(range(NCORES)))
    R = res.results
    y_prompt = np.stack([R[c]["y_p"] for c in range(NCORES)], axis=0)
    y_sample = np.concatenate([R[c]["y_s"].reshape(16, 8, D) for c in range(NCORES)], axis=0)
    nre_p = np.stack([R[c]["nre_p"] for c in range(NCORES)], axis=0)[None]
    nim_p = np.stack([R[c]["nim_p"] for c in range(NCORES)], axis=0)[None]
    npool_p = np.stack([R[c]["npool_p"] for c in range(NCORES)], axis=0)[None]
    nre_s = np.concatenate([R[c]["nre_s"] for c in range(NCORES)], axis=0)[None]
    nim_s = np.concatenate([R[c]["nim_s"] for c in range(NCORES)], axis=0)[None]
    npool_s = np.concatenate([R[c]["npool_s"] for c in range(NCORES)], axis=0)[None]
    return (y_prompt.astype(np.float32), y_sample.astype(np.float32), nre_p.astype(np.float32),
            nim_p.astype(np.float32), npool_p.astype(np.float32), nre_s.astype(np.float32),
            nim_s.astype(np.float32), npool_s.astype(np.float32))
```
